# Optimizing a Trainium2 kernel written in Bass

```python
import math
import jax, jax.numpy as jnp
from jax import lax
import numpy as np

D_MODEL = 1024
BATCH = 4
SEQ = 8192
DEPTH = 2

EPS = 1e-6
MIX_WIDTH = 512
N_BRANCH = 4
RET_HEADS = 4
RET_DK = MIX_WIDTH // RET_HEADS
RET_DV = MIX_WIDTH // RET_HEADS
RET_CHUNK = 128
ROPE_BASE = 10000.0
CONV_CH = MIX_WIDTH
CONV_WIDTH = 31
GMLP_CH = MIX_WIDTH
GMLP_GROUPS = 4
GMLP_GROUP_CH = GMLP_CH // GMLP_GROUPS
GMLP_CHUNK = 128
SWA_HEADS = 8
SWA_KV_HEADS = 2
SWA_GROUP = SWA_HEADS // SWA_KV_HEADS
SWA_HEAD_DIM = MIX_WIDTH // SWA_HEADS
SWA_WINDOW = 128
SWA_BLOCK = 128
REL_BUCKETS = 32
REL_MAX_DIST = 128
D_FF = ((8 * D_MODEL + 3 * 256 - 1) // (3 * 256)) * 256
RET_COLS = 4 * MIX_WIDTH
CONV_COLS = 2 * CONV_CH
GMLP_COLS = 2 * GMLP_CH
SWA_COLS = (SWA_HEADS + 2 * SWA_KV_HEADS) * SWA_HEAD_DIM
GATE_COLS = N_BRANCH * D_MODEL
IN_COLS = RET_COLS + CONV_COLS + GMLP_COLS + SWA_COLS + GATE_COLS
IN_SPLITS = [RET_COLS, RET_COLS + CONV_COLS, RET_COLS + CONV_COLS + GMLP_COLS,
             RET_COLS + CONV_COLS + GMLP_COLS + SWA_COLS]

kernel_name = "hybrid_gated_parallel_mixers"


def rms_norm(x, g):
    xf = x.astype(jnp.float32)
    y = xf * lax.rsqrt(jnp.mean(xf * xf, axis=-1, keepdims=True) + EPS)
    return (y * g.astype(jnp.float32)).astype(x.dtype)


def layer_norm(x, g, b):
    xf = x.astype(jnp.float32)
    mu = jnp.mean(xf, axis=-1, keepdims=True)
    var = jnp.mean(jnp.square(xf - mu), axis=-1, keepdims=True)
    return (xf - mu) * lax.rsqrt(var + EPS) * g.astype(jnp.float32) + b.astype(jnp.float32)


def rotary(x, pos):
    half = x.shape[-1] // 2
    inv = ROPE_BASE ** (-jnp.arange(half, dtype=jnp.float32) / half)
    ang = pos.astype(jnp.float32)[:, None] * inv[None, :]
    cos = jnp.cos(ang)[None, :, None, :]
    sin = jnp.sin(ang)[None, :, None, :]
    x1, x2 = x[..., :half], x[..., half:]
    return jnp.concatenate([x1 * cos - x2 * sin, x1 * sin + x2 * cos], axis=-1)


def retention(za, gn_g):
    B, S, _ = za.shape
    C = RET_CHUNK
    N = S // C
    H = RET_HEADS
    qa, ka, va, ga = jnp.split(za, 4, axis=-1)
    pos = jnp.arange(S)
    q = rotary(qa.astype(jnp.float32).reshape(B, S, H, RET_DK), pos)
    k = rotary(ka.astype(jnp.float32).reshape(B, S, H, RET_DK), pos) * (RET_DK ** -0.5)
    v = va.astype(jnp.float32).reshape(B, S, H, RET_DV)
    gamma = 1.0 - 2.0 ** (-5.0 - jnp.arange(H, dtype=jnp.float32))
    log_g = jnp.log(gamma)
    idx = jnp.arange(C, dtype=jnp.float32)
    diff = idx[:, None] - idx[None, :]
    decay = jnp.where(diff >= 0, jnp.exp(log_g[:, None, None] * jnp.maximum(diff, 0.0)), 0.0)
    xi = jnp.exp(log_g[None, :] * (idx[:, None] + 1.0))
    zeta = jnp.exp(log_g[None, :] * (C - 1.0 - idx[:, None]))
    chunk_decay = jnp.exp(log_g * C)
    qc = q.reshape(B, N, C, H, RET_DK)
    kc = k.reshape(B, N, C, H, RET_DK)
    vc = v.reshape(B, N, C, H, RET_DV)
    scores = jnp.einsum('bnthd,bnshd->bnhts', qc, kc) * decay[None, None]
    y_inner = jnp.einsum('bnhts,bnshv->bnthv', scores, vc)
    kv = jnp.einsum('bnshd,bnshv->bnhdv', kc * zeta[None, None, :, :, None], vc)

    def step(state, kv_n):
        return chunk_decay[None, :, None, None] * state + kv_n, state

    state0 = jnp.zeros((B, H, RET_DK, RET_DV), jnp.float32)
    _, state_prev = lax.scan(step, state0, jnp.moveaxis(kv, 1, 0))
    state_prev = jnp.moveaxis(state_prev, 0, 1)
    y_cross = jnp.einsum('bnthd,bnhdv->bnthv', qc * xi[None, None, :, :, None], state_prev)
    y = (y_inner + y_cross).reshape(B, S, H, RET_DV)
    mu = jnp.mean(y, axis=-1, keepdims=True)
    var = jnp.mean(jnp.square(y - mu), axis=-1, keepdims=True)
    y = ((y - mu) * lax.rsqrt(var + EPS)).reshape(B, S, H * RET_DV) * gn_g.astype(jnp.float32)
    return jax.nn.silu(ga.astype(jnp.float32)) * y


def conformer_conv(zb, conv_w, conv_b, ln_g, ln_b):
    a, gate = jnp.split(zb, 2, axis=-1)
    u = (a * jax.nn.sigmoid(gate)).astype(conv_w.dtype)
    u = lax.conv_general_dilated(u, conv_w, window_strides=(1,), padding=[(CONV_WIDTH - 1, 0)],
                                 dimension_numbers=('NWC', 'WIO', 'NWC'),
                                 feature_group_count=CONV_CH)
    u = u + conv_b
    return jax.nn.silu(layer_norm(u, ln_g, ln_b))


def spatial_gating(zc, ln_g, ln_b, w_s, b_s):
    B, S, _ = zc.shape
    N = S // GMLP_CHUNK
    hc = jax.nn.gelu(zc.astype(jnp.float32))
    u, v = jnp.split(hc, 2, axis=-1)
    v = layer_norm(v, ln_g, ln_b).reshape(B, N, GMLP_CHUNK, GMLP_GROUPS, GMLP_GROUP_CH)
    mask = jnp.tril(jnp.ones((GMLP_CHUNK, GMLP_CHUNK), jnp.float32))
    ws = w_s.astype(jnp.float32) * mask[None]
    s = jnp.einsum('gts,bnsgc->bntgc', ws, v) + b_s.astype(jnp.float32).T[None, None, :, :, None]
    return u * s.reshape(B, S, GMLP_CH)


def t5_bucket(dist):
    max_exact = REL_BUCKETS // 2
    d = jnp.maximum(dist, 1).astype(jnp.float32)
    large = max_exact + (jnp.log(d / max_exact) / math.log(REL_MAX_DIST / max_exact)
                         * (REL_BUCKETS - max_exact)).astype(jnp.int32)
    large = jnp.minimum(large, REL_BUCKETS - 1)
    return jnp.where(dist < max_exact, dist, large)


def swa_attention(zd, q_g, k_g, sinks, rel_bias):
    B, S, _ = zd.shape
    Bk = SWA_BLOCK
    N = S // Bk
    d = SWA_HEAD_DIM
    qd, kd, vd = jnp.split(zd, [SWA_HEADS * d, (SWA_HEADS + SWA_KV_HEADS) * d], axis=-1)
    q = rms_norm(qd.reshape(B, S, SWA_HEADS, d), q_g).astype(jnp.float32)
    k = rms_norm(kd.reshape(B, S, SWA_KV_HEADS, d), k_g).astype(jnp.float32)
    v = vd.reshape(B, S, SWA_KV_HEADS, d).astype(jnp.float32)
    qb = q.reshape(B, N, Bk, SWA_KV_HEADS, SWA_GROUP, d)

    def band(t):
        tb = t.reshape(B, N, Bk, SWA_KV_HEADS, d)
        prev = jnp.pad(tb[:, :-1], ((0, 0), (1, 0), (0, 0), (0, 0), (0, 0)))
        return jnp.concatenate([prev, tb], axis=2)

    kb, vb = band(k), band(v)
    s = jnp.einsum('bnqhgd,bnkhd->bnhgqk', qb, kb) * (d ** -0.5)
    qi = jnp.arange(Bk)[:, None] + Bk
    kj = jnp.arange(2 * Bk)[None, :]
    dist = qi - kj
    in_win = (dist >= 0) & (dist < SWA_WINDOW)
    bias = rel_bias.astype(jnp.float32)[t5_bucket(jnp.maximum(dist, 0))]
    bias = bias.transpose(2, 0, 1).reshape(SWA_KV_HEADS, SWA_GROUP, Bk, 2 * Bk)
    valid = in_win[None] & ((jnp.arange(N)[:, None, None] > 0) | (kj[None] >= Bk))
    s = jnp.where(valid[None, :, None, None], s + bias[None, None], -jnp.inf)
    sink = sinks.astype(jnp.float32).reshape(SWA_KV_HEADS, SWA_GROUP)[None, None, :, :, None, None]
    m = jnp.maximum(jnp.max(s, axis=-1, keepdims=True), sink)
    p = jnp.exp(s - m)
    denom = jnp.sum(p, axis=-1, keepdims=True) + jnp.exp(sink - m)
    o = jnp.einsum('bnhgqk,bnkhd->bnqhgd', p / denom, vb)
    return o.reshape(B, S, SWA_HEADS * d)


def hybrid_layer(x, rel_bias, norm1_g, w_in, ret_gn_g, conv_w, conv_b, conv_ln_g, conv_ln_b,
                 gmlp_ln_g, gmlp_ln_b, gmlp_ws, gmlp_bs, swa_q_g, swa_k_g, swa_sinks,
                 w_branch, w_out, norm2_g, w_ffn_in, w_ffn_out):
    B, S, _ = x.shape
    h = rms_norm(x, norm1_g)
    z = h @ w_in
    za, zb, zc, zd, zg = jnp.split(z, IN_SPLITS, axis=-1)
    ya = retention(za, ret_gn_g)
    yb = conformer_conv(zb, conv_w, conv_b, conv_ln_g, conv_ln_b)
    yc = spatial_gating(zc, gmlp_ln_g, gmlp_ln_b, gmlp_ws, gmlp_bs)
    yd = swa_attention(zd, swa_q_g, swa_k_g, swa_sinks, rel_bias)
    ys = jnp.stack([ya, yb, yc, yd], axis=2).astype(w_branch.dtype)
    branches = jnp.einsum('bsnc,ncd->bsnd', ys, w_branch)
    gates = jax.nn.sigmoid(zg.reshape(B, S, N_BRANCH, D_MODEL))
    mix = jnp.sum(gates * branches, axis=2)
    x = x + (mix @ w_out).astype(x.dtype)
    h = rms_norm(x, norm2_g)
    gt, up = jnp.split(h @ w_ffn_in, 2, axis=-1)
    return x + ((jax.nn.silu(gt) * up) @ w_ffn_out).astype(x.dtype)


def setup_inputs(seed: int = 0) -> dict:
    key = jax.random.key(seed)
    ks = jax.random.split(key, 24)
    f32 = jnp.float32
    nrm = lambda k, shape, s: jax.random.normal(k, shape, f32) * s
    L = DEPTH
    return {
        'x': nrm(ks[0], (BATCH, SEQ, D_MODEL), 1.0),
        'rel_bias': nrm(ks[1], (REL_BUCKETS, SWA_HEADS), 0.5),
        'norm1_g': 1.0 + nrm(ks[2], (L, D_MODEL), 0.02),
        'w_in': nrm(ks[3], (L, D_MODEL, IN_COLS), D_MODEL ** -0.5),
        'ret_gn_g': 1.0 + nrm(ks[4], (L, MIX_WIDTH), 0.02),
        'conv_w': nrm(ks[5], (L, CONV_WIDTH, 1, CONV_CH), CONV_WIDTH ** -0.5),
        'conv_b': nrm(ks[6], (L, CONV_CH), 0.02),
        'conv_ln_g': 1.0 + nrm(ks[7], (L, CONV_CH), 0.02),
        'conv_ln_b': nrm(ks[8], (L, CONV_CH), 0.02),
        'gmlp_ln_g': 1.0 + nrm(ks[9], (L, GMLP_CH), 0.02),
        'gmlp_ln_b': nrm(ks[10], (L, GMLP_CH), 0.02),
        'gmlp_ws': nrm(ks[11], (L, GMLP_GROUPS, GMLP_CHUNK, GMLP_CHUNK), 0.5 * GMLP_CHUNK ** -0.5),
        'gmlp_bs': 1.0 + nrm(ks[12], (L, GMLP_GROUPS, GMLP_CHUNK), 0.02),
        'swa_q_g': 1.0 + nrm(ks[13], (L, SWA_HEAD_DIM), 0.02),
        'swa_k_g': 1.0 + nrm(ks[14], (L, SWA_HEAD_DIM), 0.02),
        'swa_sinks': nrm(ks[15], (L, SWA_HEADS), 0.5),
        'w_branch': nrm(ks[16], (L, N_BRANCH, MIX_WIDTH, D_MODEL), MIX_WIDTH ** -0.5),
        'w_out': nrm(ks[17], (L, D_MODEL, D_MODEL), 0.5 * D_MODEL ** -0.5),
        'norm2_g': 1.0 + nrm(ks[18], (L, D_MODEL), 0.02),
        'w_ffn_in': nrm(ks[19], (L, D_MODEL, 2 * D_FF), D_MODEL ** -0.5),
        'w_ffn_out': nrm(ks[20], (L, D_FF, D_MODEL), 0.5 * D_FF ** -0.5),
    }


def reference(x, rel_bias, norm1_g, w_in, ret_gn_g, conv_w, conv_b, conv_ln_g, conv_ln_b,
              gmlp_ln_g, gmlp_ln_b, gmlp_ws, gmlp_bs, swa_q_g, swa_k_g, swa_sinks,
              w_branch, w_out, norm2_g, w_ffn_in, w_ffn_out):
    for l in range(DEPTH):
        x = hybrid_layer(x, rel_bias, norm1_g[l], w_in[l], ret_gn_g[l], conv_w[l], conv_b[l],
                         conv_ln_g[l], conv_ln_b[l], gmlp_ln_g[l], gmlp_ln_b[l], gmlp_ws[l],
                         gmlp_bs[l], swa_q_g[l], swa_k_g[l], swa_sinks[l], w_branch[l],
                         w_out[l], norm2_g[l], w_ffn_in[l], w_ffn_out[l])
    return x
```

```python
import numpy as np
from contextlib import ExitStack
import concourse.bass as bass
import concourse.mybir as mybir
from concourse.bass_utils import run_bass_kernel_spmd

F32 = mybir.dt.float32
BF16 = mybir.dt.bfloat16
AF = mybir.ActivationFunctionType
ALU = mybir.AluOpType
AX = mybir.AxisListType


class Tile:
    def __init__(self, name, ap, space):
        self.name = name
        self.ap = ap
        self.space = space
        self.lastw = {}
        self.lastr = {}
        self.aliases = []
        self.dma_sem = None
        self.dma_cnt = 0

    def __getitem__(self, k):
        return View(self, self.ap[k])

    @property
    def v(self):
        return View(self, self.ap)


class View:
    def __init__(self, tile, ap):
        self.tile = tile
        self.ap = ap

    def __getitem__(self, k):
        return View(self.tile, self.ap[k])

    def rearrange(self, s, **kw):
        return View(self.tile, self.ap.rearrange(s, **kw))

    def to_broadcast(self, shape):
        return View(self.tile, self.ap.to_broadcast(shape))

    def broadcast_to(self, shape):
        return View(self.tile, self.ap.broadcast_to(shape))

    def unsqueeze(self, ax):
        return View(self.tile, self.ap.unsqueeze(ax))

    def bitcast(self, dt):
        return View(self.tile, self.ap.bitcast(dt))

    @property
    def shape(self):
        return self.ap.shape


class Op:
    __slots__ = ("eng", "meth", "args", "kwargs", "deps", "signal", "cnt", "idx",
                 "is_dma", "dma_tile", "dma_cnt", "extra_waits")

    def __init__(self, eng, meth, args, kwargs):
        self.eng = eng
        self.meth = meth
        self.args = args
        self.kwargs = kwargs
        self.deps = []
        self.signal = False
        self.cnt = 0
        self.idx = 0
        self.is_dma = False
        self.dma_tile = None
        self.dma_cnt = 0


class Prog:
    ENGS = ("pe", "act", "dve", "pool", "sp")

    def __init__(self, nc):
        self.nc = nc
        self.ops = {e: [] for e in self.ENGS}
        self.sb_off = 16512
        self.sb_end = 229344
        self.tiles = []
        self.dram_tiles = {}
        self.near = 3
        self.dry = False
        self.dry_cost = {"pe": 0.0, "dve": 0.0, "act": 0.0, "pool": 0.0, "sp": 0.0}

    def sb(self, name, shape, dtype, at=None):
        esz = 4 if dtype == F32 else 2
        n = 1
        for s in shape[1:]:
            n *= s
        nbytes = n * esz
        if at is None:
            off = (self.sb_off + 63) // 64 * 64
            self.sb_off = off + nbytes
            assert self.sb_off <= self.sb_end, (name, self.sb_off)
        else:
            off = at
        h = self.nc.alloc_sbuf_tensor_at(name, list(shape), dtype, offset=off)
        t = Tile(name, h[:] if len(shape) == 2 else h[(slice(None),) * len(shape)], "sb")
        t.off = off
        t.nbytes = nbytes
        self.tiles.append(t)
        return t

    def alias(self, a, b):
        a.aliases.append(b)
        b.aliases.append(a)

    def ps(self, name, shape, dtype=F32):
        h = self.nc.alloc_psum_tensor(name, list(shape), dtype)
        t = Tile(name, h[(slice(None),) * len(shape)], "ps")
        self.tiles.append(t)
        return t

    def dram(self, name, shape, dtype, kind):
        h = self.nc.dram_tensor(name, list(shape), dtype, kind=kind)
        t = Tile(name, h.ap(), "dram")
        self.dram_tiles[name] = t
        return t

    def _collect(self, op, reads, writes):
        eng = op.eng
        deps = op.deps
        for t in reads:
            for d in t.lastw.values():
                deps.append(d)
        for t in writes:
            for tt in [t] + t.aliases:
                for d in tt.lastw.values():
                    deps.append(d)
                for d in tt.lastr.values():
                    deps.append(d)
        return deps

    def add(self, eng, meth, *args, reads=None, writes=None, **kwargs):
        if self.dry:
            n = 1
            for a in args:
                if isinstance(a, View):
                    for d in a.ap.shape[1:]:
                        n *= d
                    break
            if eng == "pe":
                c = 0.03 + n * 0.00047
            elif eng == "dve":
                c = 0.1 + n * 0.00112
            else:
                c = 0.12 + n * 0.00095
            self.dry_cost[eng] += c
            return None
        op = Op(eng, meth, args, kwargs)
        r, w = [], []
        first = True
        for a in args:
            if isinstance(a, View):
                if first:
                    w.append(a.tile)
                    first = False
                else:
                    r.append(a.tile)
        for k, a in kwargs.items():
            if isinstance(a, View):
                if k in ("accum_out", "out"):
                    w.append(a.tile)
                else:
                    r.append(a.tile)
        if reads:
            r += [x.tile if isinstance(x, View) else x for x in reads]
        if writes:
            w += [x.tile if isinstance(x, View) else x for x in writes]
        self._collect(op, r, w)
        op.idx = len(self.ops[eng])
        self.ops[eng].append(op)
        for t in r:
            if t.space != "dram" or True:
                t.lastr[eng] = ("op", op)
        for t in w:
            t.lastw[eng] = ("op", op)
        return op

    def dma(self, eng, out, in_, sync_tile=None, **kwargs):
        if self.dry:
            return None
        op = Op(eng, "dma_start", (), dict(out=out, in_=in_, **kwargs))
        op.is_dma = True
        if sync_tile is None:
            sync_tile = out.tile if out.tile.space != "dram" else in_.tile
        st = sync_tile
        self._collect(op, [in_.tile], [out.tile])
        st.dma_cnt += 16
        op.dma_tile = st
        op.dma_cnt = st.dma_cnt
        op.idx = len(self.ops[eng])
        self.ops[eng].append(op)
        key = ("dma", st.name)
        dep = ("dma", st, st.dma_cnt)
        in_.tile.lastr[key] = dep
        out.tile.lastw[key] = dep
        return op

    def collective(self, kind, groups, in_tile, out_tile):
        op = Op("pool", "collective_compute", (kind, ALU.bypass),
                dict(replica_groups=groups, ins=[View(in_tile, in_tile.ap.opt())],
                     outs=[View(out_tile, out_tile.ap.opt())]))
        op.is_dma = True
        self._collect(op, [in_tile], [out_tile])
        out_tile.dma_cnt += 1
        op.dma_tile = out_tile
        op.dma_cnt = out_tile.dma_cnt
        op.idx = len(self.ops["pool"])
        self.ops["pool"].append(op)
        dep = ("dma", out_tile, out_tile.dma_cnt)
        in_tile.lastr[("dma", out_tile.name)] = dep
        out_tile.lastw[("dma", out_tile.name)] = dep
        return op

    def finish(self, eng="sp"):
        op = Op(eng, "nop", (), {})
        for t in self.tiles + list(self.dram_tiles.values()):
            if t.dma_cnt:
                op.deps.append(("dma", t, t.dma_cnt))
        op.idx = len(self.ops[eng])
        self.ops[eng].append(op)

    def emit(self, stack):
        nc = self.nc
        for e in self.ENGS:
            for op in self.ops[e]:
                for d in op.deps:
                    if d[0] == "op":
                        y = d[1]
                        if y.eng != e:
                            y.signal = True
                        elif e != "pe" and op.idx - y.idx <= self.near:
                            y.signal = True
        esem = {}
        for e in self.ENGS:
            c = 0
            for op in self.ops[e]:
                if op.signal and not op.is_dma:
                    c += 1
                    op.cnt = c
            if c:
                esem[e] = stack.enter_context(nc.semaphore("s_" + e))
        self.sig_counts = {e: max([o.cnt for o in self.ops[e]] + [0]) for e in self.ENGS}
        for t in self.tiles + list(self.dram_tiles.values()):
            if t.dma_cnt:
                t.dma_sem = stack.enter_context(nc.semaphore("d_" + t.name))
        block = stack.enter_context(nc.Block())
        nwaits = {e: 0 for e in self.ENGS}

        def run(e, eng):
            seen = {}
            for op in self.ops[e]:
                need = {}
                for d in op.deps:
                    if d[0] == "op":
                        y = d[1]
                        if y.is_dma:
                            continue
                        if y.eng == e and not (e != "pe" and op.idx - y.idx <= self.near):
                            continue
                        if y.eng == e and y is op:
                            continue
                        k = ("e", y.eng)
                        v = y.cnt
                        sem = esem[y.eng]
                    else:
                        k = ("d", d[1].name)
                        v = d[2]
                        sem = d[1].dma_sem
                    if seen.get(k, 0) >= v:
                        continue
                    if need.get(k, (None, 0))[1] < v:
                        need[k] = (sem, v)
                for k, (sem, v) in need.items():
                    eng.wait_ge(sem, v)
                    seen[k] = v
                    nwaits[e] += 1
                if op.meth == "nop":
                    continue
                args = [a.ap if isinstance(a, View) else a for a in op.args]
                kwargs = {k: (a.ap if isinstance(a, View) else ([x.ap if isinstance(x, View) else x for x in a] if isinstance(a, list) and a and isinstance(a[0], View) else a)) for k, a in op.kwargs.items()}
                ins = getattr(eng, op.meth)(*args, **kwargs)
                if op.is_dma:
                    if op.meth == "collective_compute":
                        ins.then_inc(op.dma_tile.dma_sem)
                    else:
                        ins.then_inc(op.dma_tile.dma_sem, 16)
                elif op.signal:
                    ins.then_inc(esem[e], 1)

        if self.ops["pe"]:
            @block.tensor
            def _(eng):
                run("pe", eng)
        if self.ops["act"]:
            @block.scalar
            def _(eng):
                run("act", eng)
        if self.ops["dve"]:
            @block.vector
            def _(eng):
                run("dve", eng)
        if self.ops["pool"]:
            @block.gpsimd
            def _(eng):
                run("pool", eng)
        if self.ops["sp"]:
            @block.sync
            def _(eng):
                run("sp", eng)
        self.nwaits = nwaits

D = 1024
GT = 512
NS = 5
SQ044 = 0.044715 ** 0.5
GELU_C = 1.5957691216057308
NEG = -30000.0


def build(TOK, L, debug=False, pair=None):
    nc = bass.Bass("TRN2", target_bir_lowering=False)
    P = Prog(nc)
    NG = TOK // GT
    EI = "ExternalInput"
    x_d = P.dram("x", [TOK, D], F32, EI)
    out_d = P.dram("out", [TOK, D], F32, "ExternalOutput")
    w_in_d = P.dram("w_in", [L, D, 8960], F32, EI)
    w_br_d = P.dram("w_br", [L, 4, 512, D], F32, EI)
    w_out_d = P.dram("w_out", [L, D, D], F32, EI)
    w_fi_d = P.dram("w_fi", [L, D, 5632], F32, EI)
    w_fo_d = P.dram("w_fo", [L, 2816, D], F32, EI)
    n1g_d = P.dram("n1g", [L, 128, 8], F32, EI)
    n2g_d = P.dram("n2g", [L, 128, 8], F32, EI)
    gng_d = P.dram("gng", [L, 512], F32, EI)
    cw_d = P.dram("cw", [L, 128, 4 * 31], F32, EI)
    cb_d = P.dram("cb", [L, 128, 4], F32, EI)
    clg_d = P.dram("clg", [L, 128, 4], F32, EI)
    clb_d = P.dram("clb", [L, 128, 4], F32, EI)
    glg_d = P.dram("glg", [L, 512], F32, EI)
    glb_d = P.dram("glb", [L, 512], F32, EI)
    gws_d = P.dram("gws", [L, 128, 512], F32, EI)
    gbs_d = P.dram("gbs", [L, 512], F32, EI)
    sqg_d = P.dram("sqg", [L, 64], F32, EI)
    skg_d = P.dram("skg", [L, 64], F32, EI)
    sink_d = P.dram("sink", [L, 8], F32, EI)
    bias_d = P.dram("biasg", [128, 2048], F32, EI)
    ident_d = P.dram("ident", [128, 128], F32, EI)
    cs_d = P.dram("cs", [TOK, 256], F32, EI)
    dec_d = P.dram("dec", [128, 512], F32, EI)
    xz_d = P.dram("xz", [128, 8], F32, EI)
    tril_d = P.dram("tril", [128, 512], F32, EI)
    mneg_d = P.dram("mneg", [128, 256], F32, EI)
    MSG = 892
    if pair:
        flag_d = P.dram("flag", [128, 2], F32, EI)
        msg_in_d = P.dram("msg_in", [128, MSG], F32, "Internal")
        msg_out_d = P.dram("msg_out", [256, MSG], F32, "Internal")
    if L > 1:
        x1_d = P.dram("x1", [TOK, D], F32, "Internal")
    dg_d = P.dram("dgm", [L * 4, 128, 31 * 128], BF16, "Internal")
    if debug:
        dbg_d = P.dram("dbg", [4, 128, 2048], F32, "ExternalOutput")
        dbg2_d = P.dram("dbg2", [8, 128, 512], F32, "ExternalOutput")
        def dump(k, view, n=512):
            P.dma("sp", dbg2_d[k, :, 0:n], view)

    st = ExitStack()
    ident = P.sb("ident", [128, 128], BF16)
    dec = P.sb("dec", [128, 512], F32)
    xz = P.sb("xz", [128, 8], F32)
    onesf = P.sb("onesf", [128, 128], F32)
    eps = P.sb("eps", [128, 1], F32)
    biasT = P.sb("biasT", [128, 2048], F32)
    n1g = [P.sb(f"n1g{l}", [128, 8], F32) for l in range(L)]
    n2g = [P.sb(f"n2g{l}", [128, 8], F32) for l in range(L)]
    cw = [P.sb(f"cw{l}", [128, 124], F32) for l in range(L)]
    cb = [P.sb(f"cb{l}", [128, 4], F32) for l in range(L)]
    clg = [P.sb(f"clg{l}", [128, 4], F32) for l in range(L)]
    clb = [P.sb(f"clb{l}", [128, 4], F32) for l in range(L)]
    sqg = [P.sb(f"sqg{l}", [128, 64], F32) for l in range(L)]
    skg = [P.sb(f"skg{l}", [128, 64], F32) for l in range(L)]
    esink = [P.sb(f"esink{l}", [128, 8], F32) for l in range(L)]
    wsT = [P.sb(f"wsT{l}", [128, 512], BF16) for l in range(L)]
    gng = [P.sb("gng", [128, 512], F32)] * L
    glg = [P.sb("glg", [128, 512], F32)] * L
    glb = [P.sb("glb", [128, 512], F32)] * L
    bsb = [P.sb("bsb", [128, 512], F32)] * L
    S32 = [P.sb("S32", [128, 512], F32)] * L
    Sbf = [P.sb("Sbf", [128, 512], BF16)] * L
    Uh = [P.sb("Uh", [128, 4, 30], F32)] * L
    kTr = [P.sb("kTr", [128, 2, 128], BF16)] * L
    vaug = [P.sb("vaug", [128, 2, 2, 66], BF16)] * L
    if pair:
        qT_d = P.dram("qT_s", [NG, 128, 4, GT], BF16, "Internal")
        sg_d = P.dram("sg_s", [TOK, 512], F32, "Internal")
        sq_d = P.dram("sq_s", [TOK, 512], BF16, "Internal")
        kT_d = P.dram("kT_s", [NG, 128, 4, GT], BF16, "Internal")
        kz_d = P.dram("kz_s", [TOK, 512], BF16, "Internal")
        vr_d = P.dram("vr_s", [TOK, 512], BF16, "Internal")
        hT_d = P.dram("hT_s", [NG, 128, 8, GT], BF16, "Internal")
        R_kr = [P.sb(f"R_kr{i}", [128, 512], BF16) for i in range(2)]

        def mk_tok(name):
            t_ = Tile(name, None, "tok")
            P.tiles.append(t_)
            return t_
        tokC = [mk_tok(f"tokC{i}") for i in range(4)]
        tokKs = [mk_tok(f"tokKs{i}") for i in range(4)]
        tokVs = [mk_tok(f"tokVs{i}") for i in range(4)]
        tokKl = [mk_tok(f"tokKl{i}") for i in range(4)]
        tokVl = [mk_tok(f"tokVl{i}") for i in range(4)]
        tokRl = [mk_tok(f"tokRl{i}") for i in range(4)]
        tokHs = [mk_tok(f"tokHs{i}") for i in range(4)]
        tokQs = [mk_tok(f"tokQs{i}") for i in range(4)]
        tokGs = [mk_tok(f"tokGs{i}") for i in range(4)]
        tokGl = [mk_tok(f"tokGl{i}") for i in range(4)]
        tokSs = [mk_tok(f"tokSs{i}") for i in range(2)]
        tokSl = [mk_tok(f"tokSl{i}") for i in range(2)]
        S_qn2 = [P.sb(f"S_qn2_{i}", [128, 512], BF16) for i in range(2)]
        tokKTs = [mk_tok(f"tokKTs{i}") for i in range(4)]
        fl = P.sb("fl", [128, 2], F32)
        msg = P.sb("msg", [128, MSG], F32)
        recv = P.sb("recv", [128, MSG], F32)
    xt = [P.sb(f"xt{i}", [128, D], F32) for i in range(4)]
    hT = P.sb("hT", [128, 8, GT], BF16)
    htok = [P.sb(f"htok{i}", [128, D], BF16) for i in range(2)]
    r1 = P.sb_off
    actT = P.sb("actT", [128, 22, GT], BF16)
    r1e = P.sb_off
    P.sb_off = r1
    qT = P.sb("qT", [128, 4, GT], BF16)
    kT = P.sb("kT", [128, 4, GT], BF16)
    kz = P.sb("kz", [128, 4, 512], BF16)
    vret = P.sb("vret", [128, 4, 512], BF16)
    sgg = P.sb("sgg", [128, 4, 512], F32)
    for t in (qT, kT, kz, vret, sgg):
        P.alias(actT, t)
    P.sb_off = max(P.sb_off, r1e)
    yT = [P.sb(f"yT{b}", [128, 4, GT], BF16) for b in range(4)]
    r2 = P.sb_off
    mix = P.sb("mix", [128, 8, GT], F32)
    r2e = P.sb_off
    P.sb_off = r2
    U = P.sb("U", [128, 4, 30 + GT], BF16)
    acc = P.sb("acc", [128, 4, GT], F32)
    C_mean = P.sb("C_mean", [128, 512], F32)
    P.alias(mix, U)
    P.alias(mix, acc)
    P.alias(mix, C_mean)
    P.sb_off = max(P.sb_off, r2e)
    r3 = P.sb_off
    mixT = P.sb("mixT", [128, 8, GT], BF16)
    P.sb_off = r3
    uT = P.sb("uT", [128, 4, GT], F32)
    P.alias(mixT, uT)
    cs = P.sb("cs", [128, 4, 256], F32)
    WS = [P.sb(f"ws{i}", [128, 4096], BF16) for i in range(NS)]
    SC = [P.sb(f"sc{i}", [128, 512], F32) for i in range(6)]
    SB = [P.sb(f"sbb{i}", [128, 512], BF16) for i in range(4)]
    qTs = [P.sb(f"qTs{i}", [128, 512], BF16) for i in range(2)]
    pT = P.sb("pT", [128, 4, 512], BF16)
    st8 = P.sb("st8", [128, 8, 8], F32)
    mv = P.sb("mv", [128, 8, 2], F32)
    sm = [P.sb(f"sm{i}", [128, 8], F32) for i in range(4)]
    R_qr = [P.sb(f"R_qr{i}", [128, 512], BF16) for i in range(2)]
    R_sT = P.sb("R_sT", [128, 512], BF16)
    R_ya = P.sb("R_ya", [128, 512], BF16)
    C_rstd = P.sb("C_rstd", [128, 512], F32)
    G_vln = P.sb("G_vln", [128, 512], BF16)
    S_qn = P.sb("S_qn", [128, 512], BF16)
    S_kn = P.sb("S_kn", [128, 128], BF16)
    S_yd = P.sb("S_yd", [128, 512], BF16)
    tril = SC[5]
    PSB = [P.ps(f"ps{i}", [128, 512], F32) for i in range(8)]
    PS_CONV = PSB[7]
    print("SBUF used per partition:", P.sb_off - 16512, "of", P.sb_end - 16512)

    cnt = {"ps": 0, "ws": 0, "sc": 0, "sb": 0, "sm": 0}

    def nps():
        cnt["ps"] += 1
        return PSB[cnt["ps"] % 7]

    def nsc():
        cnt["sc"] += 1
        return SC[cnt["sc"] % 6]

    def nsb():
        cnt["sb"] += 1
        return SB[cnt["sb"] % 4]

    def nsm():
        cnt["sm"] += 1
        return sm[cnt["sm"] % 4]

    wfree = list(range(NS))

    def wload(src, shape3):
        assert wfree, "no free weight slot"
        sid = wfree.pop(0)
        w = WS[sid]
        kc, n = shape3
        dst = w[:, 0:kc * n].rearrange("p (c n) -> p c n", c=kc)
        P.dma("pool", dst, src.rearrange("(c p) n -> p c n", p=128))
        dst.sid = sid
        return dst

    def wload_flat(src, n):
        assert wfree, "no free weight slot"
        sid = wfree.pop(0)
        dst = WS[sid][:, 0:n]
        P.dma("pool", dst, src)
        dst.sid = sid
        return dst

    def wrel(*ws):
        for wv in ws:
            wfree.append(wv.sid)

    V, A = "dve", "act"

    def bc(dram_tile, l, n):
        return View(dram_tile, dram_tile.ap[l].partition_broadcast(128))

    P.dma("pool", ident.v, ident_d.v)
    P.dma("sp", dec.v, dec_d.v)
    P.dma("sp", xz.v, xz_d.v)
    if pair:
        P.dma("sp", fl.v, flag_d.v)
    P.dma("sp", tril.v, tril_d.v)
    P.dma("sp", biasT.v, bias_d.v)
    P.dma("sp", SC[0][:, 0:256], mneg_d.v)
    P.add(V, "memset", onesf.v, 1.0)
    P.add(V, "memset", eps.v, 1e-6)
    bt = biasT.v.rearrange("p (a j c q) -> p a j c q", a=2, j=2, c=4)
    mn = SC[0][:, 0:256].rearrange("p (j q) -> p j q", j=2)
    for a in range(2):
        for j in range(2):
            P.add(V, "tensor_tensor", bt[:, a, j], bt[:, a, j],
                  mn[:, j].unsqueeze(1).to_broadcast([128, 4, 128]), ALU.add)
    for l in range(L):
        P.dma("sp", n1g[l].v, n1g_d[l])
        P.dma("sp", n2g[l].v, n2g_d[l])
        P.dma("sp", cw[l].v, cw_d[l])
        P.dma("sp", cb[l].v, cb_d[l])
        P.dma("sp", clg[l].v, clg_d[l])
        P.dma("sp", clb[l].v, clb_d[l])
        P.dma("sp", sqg[l].v, bc(sqg_d, l, 64))
        P.dma("sp", skg[l].v, bc(skg_d, l, 64))
        P.dma("sp", esink[l].v, bc(sink_d, l, 8))
        s0 = nsc()
        P.dma("sp", s0.v, gws_d[l])
        P.add(V, "tensor_tensor", wsT[l].v, s0.v, tril.v, ALU.mult)
        P.add(A, "activation", esink[l].v, esink[l].v, AF.Exp)
        if l == 0:
            P.add(V, "memset", vaug[l].v, 1.0)
        for cc in range(4):
            sid = wfree.pop(0)
            stg = WS[sid]
            for j in range(31):
                if j % 3 == 0:
                    P.add(A, "activation", stg[:, j * 128:(j + 1) * 128], ident.v, AF.Copy,
                          scale=cw[l][:, cc * 31 + j:cc * 31 + j + 1])
                else:
                    P.add(V, "tensor_scalar", stg[:, j * 128:(j + 1) * 128], ident.v,
                          cw[l][:, cc * 31 + j:cc * 31 + j + 1], None, ALU.mult)
            P.dma("sp", dg_d[l * 4 + cc], stg[:, 0:31 * 128], sync_tile=stg)
            wfree.append(sid)

    def rstd_from(ss_view, n, scale, width):
        r = nsm()
        P.add(A, "activation", r[:, 0:width], ss_view, AF.Sqrt, bias=eps.v, scale=scale)
        P.add(V, "reciprocal", r[:, 0:width], r[:, 0:width])
        return r[:, 0:width]

    def rms_a(i):
        ss = nsm()
        hk = htok[i % 2]
        P.add(A, "activation", hk.v, xt[i].v, AF.Square, accum_out=ss[:, 0:1])
        r = rstd_from(ss[:, 0:1], 1, 1.0 / D, 1)
        P.add(A, "activation", hk.v, xt[i].v, AF.Copy, scale=r)

    def rms_b(i, gt):
        hk = htok[i % 2]
        ps = nps()
        pv = ps.v.bitcast(BF16)
        for c in range(8):
            P.add("pe", "transpose", pv[:, c * 128:(c + 1) * 128], hk[:, c * 128:(c + 1) * 128], ident.v)
        P.add(V, "tensor_tensor", hT[:, :, i * 128:(i + 1) * 128],
              pv.rearrange("p (c t) -> p c t", c=8),
              gt.v.unsqueeze(2).to_broadcast([128, 8, 128]), ALU.mult)

    def rmsnorm_to_hT(gt):
        rms_a(0)
        for i in range(4):
            if i + 1 < 4:
                rms_a(i + 1)
            rms_b(i, gt)

    def mm_tok(w, i, ncols=512):
        ps = nps()
        for c in range(8):
            P.add("pe", "matmul", ps[:, 0:ncols], hT[:, c, i * 128:(i + 1) * 128], w[:, c, 0:ncols],
                  start=(c == 0), stop=(c == 7))
        return ps

    def mm_feat(w, col0, src=None, nk=8):
        ps = nps()
        src = hT if src is None else src
        for c in range(nk):
            P.add("pe", "matmul", ps.v, w[:, c, col0:col0 + 128], src[:, c, :],
                  start=(c == 0), stop=(c == nk - 1))
        return ps

    def transpose4(src_tok, dstT, i):
        ps = nps()
        pv = ps.v.bitcast(BF16)
        for c in range(4):
            P.add("pe", "transpose", pv[:, c * 128:(c + 1) * 128], src_tok[:, c * 128:(c + 1) * 128], ident.v)
        P.add(A, "activation", dstT[:, :, i * 128:(i + 1) * 128],
              pv[:, 0:512].rearrange("p (c t) -> p c t", c=4), AF.Copy)

    def gelu_from_ps(ps, out_view, ncols=512):
        P.add(A, "activation", out_view, ps[:, 0:ncols], AF.Gelu_apprx_tanh)

    def w_in_blk(l, col0, ncols=512):
        return wload(View(w_in_d, w_in_d.ap[l, :, col0:col0 + ncols]), (8, ncols))

    def load_group(src_d, g):
        for i in range(4):
            P.dma("sp", xt[i].v, src_d[(g * 4 + i) * 128:(g * 4 + i + 1) * 128, :])
        P.dma("sp", cs.v, View(cs_d, cs_d.ap[g * GT:(g + 1) * GT, :].rearrange("(i p) n -> p i n", p=128)))

    def rotary(ps, i, dst=None):
        q4 = ps.v.rearrange("p (h d) -> p h d", h=4)
        t1, t2 = nsc(), nsc()
        cos2 = cs[:, i, 0:128].unsqueeze(1).to_broadcast([128, 4, 128])
        P.add(V, "tensor_tensor", t1.v.rearrange("p (h d) -> p h d", h=4), q4, cos2, ALU.mult)
        t24 = t2.v.rearrange("p (h d) -> p h d", h=4)
        P.add(V, "tensor_tensor", t24[:, :, 0:64], q4[:, :, 64:128],
              cs[:, i, 128:192].unsqueeze(1).to_broadcast([128, 4, 64]), ALU.mult)
        P.add(V, "tensor_tensor", t24[:, :, 64:128], q4[:, :, 0:64],
              cs[:, i, 192:256].unsqueeze(1).to_broadcast([128, 4, 64]), ALU.mult)
        qr = nsb() if dst is None else dst
        P.add(V, "tensor_tensor", qr.v, t1.v, t2.v, ALU.add)
        return qr

    def swa_kv_a(l, wkv, i, slot):
        pk = mm_tok(wkv, i, 256)
        s2 = nsc()
        P.add(A, "activation", s2[:, 0:128], pk[:, 0:128], AF.Square)
        ss2 = nsm()
        P.add(V, "tensor_reduce", ss2[:, 0:2], s2[:, 0:128].rearrange("p (h d) -> p h d", h=2), AX.X, ALU.add)
        r2 = rstd_from(ss2[:, 0:2], 2, 1.0 / 64, 2)
        P.add(V, "tensor_tensor", s2[:, 0:128].rearrange("p (h d) -> p h d", h=2),
              pk[:, 0:128].rearrange("p (h d) -> p h d", h=2),
              r2.unsqueeze(2).to_broadcast([128, 2, 64]), ALU.mult)
        P.add(V, "tensor_tensor", S_kn.v.rearrange("p (h d) -> p h d", h=2),
              s2[:, 0:128].rearrange("p (h d) -> p h d", h=2),
              skg[l].v.unsqueeze(1).to_broadcast([128, 2, 64]), ALU.mult)
        P.add(A, "activation", vaug[l][:, slot, :, 0:64],
              pk[:, 128:256].rearrange("p (h d) -> p h d", h=2), AF.Copy)

    def swa_kv_b(l, slot):
        pkt = nps()
        pktv = pkt.v.bitcast(BF16)
        P.add("pe", "transpose", pktv[:, 0:128], S_kn.v, ident.v)
        P.add(A, "activation", kTr[l][:, slot, :], pktv[:, 0:128], AF.Copy)

    def state_update(l, ps_k):
        for h in range(4):
            hs = slice(h * 128, (h + 1) * 128)
            gam = 1.0 - 2.0 ** (-5.0 - h)
            P.add(V, "scalar_tensor_tensor", S32[l][:, hs], S32[l][:, hs], float(gam ** 128),
                  ps_k[:, hs], ALU.mult, ALU.add)

    def prepass(l, src_d):
        P.add(V, "memset", S32[l].v, 0.0)
        wq_ = w_in_blk(l, 0)
        wk_ = w_in_blk(l, 512)
        wv_ = w_in_blk(l, 1024)
        wg_ = w_in_blk(l, 1536)
        wsq_ = w_in_blk(l, 4096)
        P.dma("sp", gng[l].v, bc(gng_d, l, 512))

        def pp_sq_a(i, t):
            ps = mm_tok(wsq_, i)
            s1 = nsc()
            P.add(A, "activation", s1.v, ps.v, AF.Square)
            ss = nsm()
            P.add(V, "tensor_reduce", ss.v, s1.v.rearrange("p (h d) -> p h d", h=8), AX.X, ALU.add)
            r = rstd_from(ss.v, 8, 1.0 / 64, 8)
            P.add(V, "tensor_tensor", s1.v.rearrange("p (h d) -> p h d", h=8),
                  ps.v.rearrange("p (h d) -> p h d", h=8),
                  r.unsqueeze(2).to_broadcast([128, 8, 64]), ALU.mult)
            P.add(V, "tensor_tensor", S_qn2[t % 2].v.rearrange("p (h d) -> p h d", h=8),
                  s1.v.rearrange("p (h d) -> p h d", h=8),
                  sqg[l].v.unsqueeze(1).to_broadcast([128, 8, 64]), ALU.mult)

        def pp_sq_b(i, t):
            qts = qTs[t % 2]
            pq = nps()
            pqv = pq.v.bitcast(BF16)
            for c in range(4):
                P.add("pe", "transpose", pqv[:, c * 128:(c + 1) * 128], S_qn2[t % 2][:, c * 128:(c + 1) * 128], ident.v)
            P.add(A, "activation", qts.v, pqv[:, 0:512], AF.Copy)
            P.dma("pool", sq_d[t * 128:(t + 1) * 128, :], qts.v, sync_tile=tokSs[t % 2])

        def pp_proj(i, t):
            ps = mm_tok(wq_, i)
            rotary(ps, i, R_qr[t % 2])
            ps = mm_tok(wk_, i)
            kr = rotary(ps, i, R_kr[t % 2])
            P.add(V, "tensor_tensor", kz[:, i, :].rearrange("p (h d) -> p h d", h=4),
                  kr.v.rearrange("p (h d) -> p h d", h=4),
                  xz[:, 4:8].unsqueeze(2).to_broadcast([128, 4, 128]), ALU.mult)
            ps = mm_tok(wv_, i)
            P.add(A, "activation", vret[:, i, :], ps.v, AF.Copy)
            rows = slice(t * 128, (t + 1) * 128)
            P.dma("pool", kz_d[rows, :], kz[:, i, :], sync_tile=tokKs[i])
            P.dma("pool", vr_d[rows, :], vret[:, i, :], sync_tile=tokVs[i])
            ps = mm_tok(wg_, i)
            s1 = nsc()
            P.add(A, "activation", s1.v, ps.v, AF.Silu)
            P.add(V, "tensor_tensor", sgg[:, i, :], s1.v, gng[l].v, ALU.mult)
            P.dma("pool", sg_d[rows, :], sgg[:, i, :], sync_tile=tokGs[i])

        def pp_T(i, t):
            isl = slice(i * 128, (i + 1) * 128)
            transpose4(R_qr[t % 2], qT, i)
            transpose4(R_kr[t % 2], kT, i)
            P.dma("pool", qT_d[t // 4, :, :, isl], qT[:, :, isl], sync_tile=tokQs[i])
            P.dma("pool", kT_d[t // 4, :, :, isl], kT[:, :, isl], sync_tile=tokKTs[i])

        def pp_kv(i):
            ps_k = nps()
            for h in range(4):
                hs = slice(h * 128, (h + 1) * 128)
                P.add("pe", "matmul", ps_k[:, hs], kz[:, i, hs], vret[:, i, hs], start=True, stop=True)
            state_update(l, ps_k)

        NTT = NG * 4
        for step in range(NTT + 3):
            if step < NTT:
                P.dma("sp", xt[step % 4].v, src_d[step * 128:(step + 1) * 128, :])
                P.dma("sp", cs[:, step % 4, :], cs_d[step * 128:(step + 1) * 128, :], sync_tile=tokC[step % 4])
                rms_a(step % 4)
            if 0 <= step - 1 < NTT:
                t_ = step - 1
                rms_b(t_ % 4, n1g[l])
                isl = slice((t_ % 4) * 128, (t_ % 4 + 1) * 128)
                P.dma("pool", hT_d[t_ // 4, :, :, isl], hT[:, :, isl], sync_tile=tokHs[t_ % 4])
            if 0 <= step - 2 < NTT:
                pp_proj((step - 2) % 4, step - 2)
                pp_sq_a((step - 2) % 4, step - 2)
            if 0 <= step - 3 < NTT:
                pp_T((step - 3) % 4, step - 3)
                pp_sq_b((step - 3) % 4, step - 3)
                pp_kv((step - 3) % 4)
        wrel(wq_, wk_, wv_, wg_, wsq_)
        load_group(src_d, 0)
        for g in [NG - 1]:
            if g == NG - 1:
                wa = w_in_blk(l, 2048)
                wg = w_in_blk(l, 2560)
                for cc in range(4):
                    pa, pg = nps(), nps()
                    for c in range(8):
                        P.add("pe", "matmul", pa[:, 0:128], wa[:, c, cc * 128:(cc + 1) * 128], hT[:, c, 384:512],
                              start=(c == 0), stop=(c == 7))
                    for c in range(8):
                        P.add("pe", "matmul", pg[:, 0:128], wg[:, c, cc * 128:(cc + 1) * 128], hT[:, c, 384:512],
                              start=(c == 0), stop=(c == 7))
                    s1 = nsc()
                    P.add(A, "activation", s1[:, 0:128], pg[:, 0:128], AF.Sigmoid)
                    P.add(V, "tensor_tensor", U[:, cc, 30 + 384:30 + 512], pa[:, 0:128], s1[:, 0:128], ALU.mult)
                P.add(V, "tensor_copy", Uh[l].v, U[:, :, GT:GT + 30])
                wkv = w_in_blk(l, 4608, 256)
                swa_kv_a(l, wkv, 3, 0)
                swa_kv_b(l, 0)
                wrel(wa, wg, wkv)
        P.dma("sp", hT.v, hT_d[0])
        P.add(V, "tensor_copy", msg[:, 0:512], S32[l].v)
        P.add(V, "tensor_copy", msg[:, 512:632].rearrange("p (c t) -> p c t", c=4), Uh[l].v)
        P.add(V, "tensor_copy", msg[:, 632:760], kTr[l][:, 0, :])
        P.add(V, "tensor_copy", msg[:, 760:892].rearrange("p (a e) -> p a e", a=2), vaug[l][:, 0])
        P.dma("sp", msg_in_d.v, msg.v, sync_tile=msg_in_d)
        P.collective("AllGather", pair, msg_in_d, msg_out_d)
        P.dma("sp", recv.v, msg_out_d[0:128, :])
        P.add(V, "tensor_scalar", S32[l].v, recv[:, 0:512], fl[:, 0:1], None, ALU.mult)
        P.add(A, "activation", Sbf[l].v, S32[l].v, AF.Copy)
        P.add(V, "tensor_scalar", Uh[l].v, recv[:, 512:632].rearrange("p (c t) -> p c t", c=4), fl[:, 0:1], None, ALU.mult)
        P.add(V, "tensor_scalar", kTr[l][:, 1, :], recv[:, 632:760], fl[:, 0:1], None, ALU.mult)
        P.add(V, "tensor_scalar", vaug[l][:, 1], recv[:, 760:892].rearrange("p (a e) -> p a e", a=2), fl[:, 0:1], None, ALU.mult)
        P.add(V, "memset", vaug[l][:, 1, :, 64:66], 1.0)

    for l in range(L):
        src_d = x_d if l == 0 else x1_d
        dst_d = out_d if l == L - 1 else x1_d
        if pair:
            prepass(l, src_d)
        else:
            P.add(V, "memset", S32[l].v, 0.0)
            P.add(V, "memset", Sbf[l].v, 0.0)
            P.add(V, "memset", Uh[l].v, 0.0)
        for g in range(NG):
            if g == 0 and not pair:
                load_group(src_d, g)
            if pair:
                P.dma("sp", qT.v, qT_d[g])
                P.dma("sp", kT.v, kT_d[g])
                for i in range(4):
                    rows = slice((g * 4 + i) * 128, (g * 4 + i + 1) * 128)
                    P.dma("sp", kz[:, i, :], kz_d[rows, :], sync_tile=tokKl[i])
                    P.dma("sp", sgg[:, i, :], sg_d[rows, :], sync_tile=tokGl[i])
                    P.dma("sp", vret[:, i, :], vr_d[rows, :], sync_tile=tokVl[i])
            P.dma("sp", gng[l].v, bc(gng_d, l, 512))
            P.dma("sp", glg[l].v, bc(glg_d, l, 512))
            P.dma("sp", glb[l].v, bc(glb_d, l, 512))
            P.dma("sp", bsb[l].v, bc(gbs_d, l, 512))
            if not pair:
                rmsnorm_to_hT(n1g[l])
            def need_slots(n):
                while len(wfree) < n:
                    yield "WAIT"

            def chain_R():
                for (col0, dstT, is_k) in ((0, qT, False), (512, kT, True)):
                    if pair:
                        continue
                    yield from need_slots(1)
                    w = w_in_blk(l, col0)
                    for i in range(4):
                        ps = mm_tok(w, i)
                        qr0 = rotary(ps, i, R_qr[i % 2])
                        if is_k:
                            P.add(V, "tensor_tensor", kz[:, i, :].rearrange("p (h d) -> p h d", h=4),
                                  qr0.v.rearrange("p (h d) -> p h d", h=4),
                                  xz[:, 4:8].unsqueeze(2).to_broadcast([128, 4, 128]), ALU.mult)
                        yield
                        transpose4(qr0, dstT, i)
                        yield
                    wrel(w)
                if not pair:
                    yield from need_slots(1)
                    w = w_in_blk(l, 1024)
                    for i in range(4):
                        ps = mm_tok(w, i)
                        P.add(A, "activation", vret[:, i, :], ps.v, AF.Copy)
                        yield
                    wrel(w)
                if not pair:
                    yield from need_slots(1)
                    w = w_in_blk(l, 1536)
                    for i in range(4):
                        ps = mm_tok(w, i)
                        s1 = nsc()
                        P.add(A, "activation", s1.v, ps.v, AF.Silu)
                        P.add(V, "tensor_tensor", sgg[:, i, :], s1.v, gng[l].v, ALU.mult)
                        yield
                    wrel(w)
                for i in range(4):
                    tsl = slice(i * 128, (i + 1) * 128)
                    ps_s = nps()
                    for h in range(4):
                        P.add("pe", "matmul", ps_s[:, h * 128:(h + 1) * 128], kT[:, h, tsl], qT[:, h, tsl],
                              start=True, stop=True)
                    P.add(V, "tensor_tensor", R_sT.v, ps_s.v, dec.v, ALU.mult)
                    yield
                    ps_i, ps_c, ps_k = nps(), nps(), nps()
                    for h in range(4):
                        hs = slice(h * 128, (h + 1) * 128)
                        P.add("pe", "matmul", ps_i[:, hs], R_sT[:, hs], vret[:, i, hs], start=True, stop=True)
                    for h in range(4):
                        hs = slice(h * 128, (h + 1) * 128)
                        P.add("pe", "matmul", ps_c[:, hs], qT[:, h, tsl], Sbf[l][:, hs], start=True, stop=True)
                    for h in range(4):
                        hs = slice(h * 128, (h + 1) * 128)
                        P.add("pe", "matmul", ps_k[:, hs], kz[:, i, hs], vret[:, i, hs], start=True, stop=True)
                    y = nsc()
                    P.add(V, "tensor_tensor", y.v.rearrange("p (h d) -> p h d", h=4),
                          ps_c.v.rearrange("p (h d) -> p h d", h=4),
                          xz[:, 0:4].unsqueeze(2).to_broadcast([128, 4, 128]), ALU.mult)
                    P.add(V, "tensor_tensor", y.v, y.v, ps_i.v, ALU.add)
                    state_update(l, ps_k)
                    P.add(A, "activation", Sbf[l].v, S32[l].v, AF.Copy)
                    for h in range(4):
                        P.add(V, "bn_stats", st8[:, h, 0:6], y[:, h * 128:(h + 1) * 128])
                    for h in range(4):
                        P.add(V, "bn_aggr", mv[:, h, :], st8[:, h, 0:6])
                    r = rstd_from(mv[:, 0:4, 1], 4, 1.0, 4)
                    y4 = y.v.rearrange("p (h d) -> p h d", h=4)
                    P.add(V, "tensor_tensor", y4, y4, mv[:, 0:4, 0:1].to_broadcast([128, 4, 128]), ALU.subtract)
                    P.add(V, "tensor_tensor", y4, y4, r.unsqueeze(2).to_broadcast([128, 4, 128]), ALU.mult)
                    P.add(V, "tensor_tensor", R_ya.v, y.v, sgg[:, i, :], ALU.mult)
                    yield
                    transpose4(R_ya, yT[0], i)
                    yield

            def chain_C():
                yield from need_slots(2)
                wa = w_in_blk(l, 2048)
                wg = w_in_blk(l, 2560)
                P.add(V, "tensor_copy", U[:, :, 0:30], Uh[l].v)
                for cc in range(4):
                    pa = mm_feat(wa, cc * 128)
                    pg = mm_feat(wg, cc * 128)
                    s1 = nsc()
                    P.add(A, "activation", s1.v, pg.v, AF.Sigmoid)
                    P.add(V, "tensor_tensor", U[:, cc, 30:30 + GT], pa.v, s1.v, ALU.mult)
                    yield
                wrel(wa, wg)
                P.add(V, "tensor_copy", Uh[l].v, U[:, :, GT:GT + 30])
                for cc in range(4):
                    yield from need_slots(1)
                    wd = wload_flat(dg_d[l * 4 + cc], 31 * 128)
                    for j in range(31):
                        P.add("pe", "matmul", PS_CONV.v, wd[:, j * 128:(j + 1) * 128], U[:, cc, j:j + GT],
                              start=(j == 0), stop=(j == 30))
                        if j % 8 == 7:
                            yield
                    P.add(A, "activation", acc[:, cc, :], PS_CONV.v, AF.Identity, bias=cb[l][:, cc:cc + 1])
                    wrel(wd)
                    yield
                p1, p2 = nps(), nps()
                for cc in range(4):
                    P.add("pe", "matmul", p1.v, onesf.v, acc[:, cc, :], start=(cc == 0), stop=(cc == 3))
                for cc in range(4):
                    s1 = nsc()
                    P.add(A, "activation", s1.v, acc[:, cc, :], AF.Square)
                    P.add("pe", "matmul", p2.v, onesf.v, s1.v, start=(cc == 0), stop=(cc == 3))
                P.add(A, "activation", C_mean.v, p1.v, AF.Copy, scale=1.0 / 512)
                P.add(V, "tensor_tensor", C_rstd.v, C_mean.v, C_mean.v, ALU.mult)
                P.add(V, "scalar_tensor_tensor", C_rstd.v, p2.v, 1.0 / 512, C_rstd.v, ALU.mult, ALU.subtract)
                P.add(A, "activation", C_rstd.v, C_rstd.v, AF.Sqrt, bias=eps.v, scale=1.0)
                P.add(V, "reciprocal", C_rstd.v, C_rstd.v)
                yield
                for cc in range(4):
                    P.add(V, "tensor_tensor", acc[:, cc, :], acc[:, cc, :], C_mean.v, ALU.subtract)
                    P.add(V, "tensor_tensor", acc[:, cc, :], acc[:, cc, :], C_rstd.v, ALU.mult)
                    P.add(A, "activation", yT[1][:, cc, :], acc[:, cc, :], AF.Silu,
                          scale=clg[l][:, cc:cc + 1], bias=clb[l][:, cc:cc + 1])
                    yield

            def chain_G():
                yield from need_slots(1)
                w = w_in_blk(l, 3072)
                for cc in range(4):
                    pu = mm_feat(w, cc * 128)
                    gelu_from_ps(pu, uT[:, cc, :])
                    yield
                wrel(w)
                yield from need_slots(1)
                w = w_in_blk(l, 3584)
                for i in range(4):
                    ps = mm_tok(w, i)
                    vv = nsc()
                    gelu_from_ps(ps, vv.v)
                    P.add(V, "bn_stats", st8[:, 4, 0:6], vv.v)
                    P.add(V, "bn_aggr", mv[:, 4, :], st8[:, 4, 0:6])
                    r = rstd_from(mv[:, 4, 1:2], 1, 1.0, 1)
                    P.add(V, "tensor_scalar", vv.v, vv.v, mv[:, 4, 0:1], r, ALU.subtract, ALU.mult)
                    P.add(V, "tensor_tensor", vv.v, vv.v, glg[l].v, ALU.mult)
                    P.add(V, "tensor_tensor", G_vln.v, vv.v, glb[l].v, ALU.add)
                    yield
                    pz = nps()
                    for gg in range(4):
                        gs = slice(gg * 128, (gg + 1) * 128)
                        P.add("pe", "matmul", pz[:, gs], G_vln[:, gs], wsT[l][:, gs], start=True, stop=True)
                    s1 = nsc()
                    P.add(V, "tensor_tensor", s1.v, pz.v, bsb[l].v, ALU.add)
                    P.add(V, "tensor_tensor", yT[2][:, :, i * 128:(i + 1) * 128],
                          s1.v.rearrange("p (g t) -> p g t", g=4), uT[:, :, i * 128:(i + 1) * 128], ALU.mult)
                    yield
                wrel(w)

            def chain_S():
                if pair:
                    yield from need_slots(1)
                    wq = None
                    for i0 in range(2):
                        t0 = g * 4 + i0
                        P.dma("sp", qTs[i0 % 2].v, sq_d[t0 * 128:(t0 + 1) * 128, :], sync_tile=tokSl[i0 % 2])
                else:
                    yield from need_slots(2)
                    wq = w_in_blk(l, 4096)
                wkv = w_in_blk(l, 4608, 256)
                for i in range(4):
                    gi = g * 4 + i
                    cur, prv = gi % 2, (gi + 1) % 2
                    if pair:
                        qts = qTs[i % 2]
                        swa_kv_a(l, wkv, i, cur)
                        yield
                        swa_kv_b(l, cur)
                        yield
                    for _once in ([] if pair else [0]):
                      ps = mm_tok(wq, i)
                      s1 = nsc()
                      P.add(A, "activation", s1.v, ps.v, AF.Square)
                      ss = nsm()
                      P.add(V, "tensor_reduce", ss.v, s1.v.rearrange("p (h d) -> p h d", h=8), AX.X, ALU.add)
                      r = rstd_from(ss.v, 8, 1.0 / 64, 8)
                      P.add(V, "tensor_tensor", s1.v.rearrange("p (h d) -> p h d", h=8),
                            ps.v.rearrange("p (h d) -> p h d", h=8),
                            r.unsqueeze(2).to_broadcast([128, 8, 64]), ALU.mult)
                      P.add(V, "tensor_tensor", S_qn.v.rearrange("p (h d) -> p h d", h=8),
                            s1.v.rearrange("p (h d) -> p h d", h=8),
                            sqg[l].v.unsqueeze(1).to_broadcast([128, 8, 64]), ALU.mult)
                      yield
                      qts = qTs[i % 2]
                      pq = nps()
                      pqv = pq.v.bitcast(BF16)
                      for c in range(4):
                          P.add("pe", "transpose", pqv[:, c * 128:(c + 1) * 128], S_qn[:, c * 128:(c + 1) * 128], ident.v)
                      P.add(A, "activation", qts.v, pqv[:, 0:512], AF.Copy)
                      swa_kv_a(l, wkv, i, cur)
                      yield
                      swa_kv_b(l, cur)
                      yield
                    has_prev = gi > 0 or bool(pair)
                    js = ([0] if has_prev else []) + [1]
                    for a in range(2):
                        for j in js:
                            slot = prv if j == 0 else cur
                            pss = nps()
                            P.add("pe", "matmul", pss.v, kTr[l][a * 64:(a + 1) * 64, slot, :],
                                  qts[a * 64:(a + 1) * 64, :], start=True, stop=True)
                            s3 = nsc()
                            P.add(V, "scalar_tensor_tensor", s3.v, pss.v, 0.125,
                                  biasT[:, (a * 2 + j) * 512:(a * 2 + j + 1) * 512], ALU.mult, ALU.add)
                            if pair and gi == 0 and j == 0:
                                P.add(V, "tensor_scalar", s3.v, s3.v, fl[:, 1:2], None, ALU.add)
                            P.add(A, "activation", pT[:, a * 2 + j, :], s3.v, AF.Exp)
                    if pair and i + 2 < 4:
                        t2 = g * 4 + i + 2
                        P.dma("sp", qTs[i % 2].v, sq_d[t2 * 128:(t2 + 1) * 128, :], sync_tile=tokSl[i % 2])
                    yield
                    po = [nps(), nps()]
                    for a in range(2):
                        for c in range(4):
                            for jn, j in enumerate(js):
                                slot = prv if j == 0 else cur
                                P.add("pe", "matmul", po[a][:, c * 65:(c + 1) * 65],
                                      pT[:, a * 2 + j, c * 128:(c + 1) * 128], vaug[l][:, slot, a, 0:65],
                                      start=(jn == 0), stop=(jn == len(js) - 1))
                    den = nsm()
                    for a in range(2):
                        P.add(V, "tensor_tensor", den[:, a * 4:(a + 1) * 4],
                              po[a][:, 0:260].rearrange("p (c e) -> p c e", c=4)[:, :, 64],
                              esink[l][:, a * 4:(a + 1) * 4], ALU.add)
                    P.add(V, "reciprocal", den.v, den.v)
                    for a in range(2):
                        P.add(V, "tensor_tensor", S_yd[:, a * 256:(a + 1) * 256].rearrange("p (c d) -> p c d", c=4),
                              po[a][:, 0:260].rearrange("p (c e) -> p c e", c=4)[:, :, 0:64],
                              den[:, a * 4:(a + 1) * 4].unsqueeze(2).to_broadcast([128, 4, 64]), ALU.mult)
                    yield
                    transpose4(S_yd, yT[3], i)
                    yield
                wrel(*([wkv] if pair else [wq, wkv]))

            chains = [chain_R, chain_C, chain_G, chain_S]
            saved = (dict(cnt), list(wfree))
            P.dry = True
            costs = []
            for cf in chains:
                wfree[:] = list(range(NS))
                steps = []
                gen = cf()
                while True:
                    P.dry_cost = {"pe": 0.0, "dve": 0.0, "act": 0.0, "pool": 0.0, "sp": 0.0}
                    try:
                        next(gen)
                    except StopIteration:
                        steps.append(dict(P.dry_cost))
                        break
                    steps.append(dict(P.dry_cost))
                costs.append(steps)
            P.dry = False
            cnt.clear()
            cnt.update(saved[0])
            wfree[:] = saved[1]
            clk = {"pe": 0.0, "dve": 0.0, "act": 0.0}
            ready = [0.0] * len(chains)
            pos = [0] * len(chains)
            rem = [sum(c["pe"] + c["dve"] + c["act"] for c in st_) for st_ in costs]
            order = []
            while any(pos[c] < len(costs[c]) for c in range(len(chains))):
                cand = [c for c in range(len(chains)) if pos[c] < len(costs[c])]
                rdy = [c for c in cand if ready[c] <= clk["pe"] + 0.3]
                if rdy:
                    c = max(rdy, key=lambda c: rem[c])
                else:
                    c = min(cand, key=lambda c: ready[c])
                st_ = costs[c][pos[c]]
                tp = max(clk["pe"], ready[c] if st_["pe"] > 0 else 0.0) + st_["pe"]
                if st_["pe"] > 0:
                    clk["pe"] = tp
                fin = tp
                for e in ("dve", "act"):
                    if st_[e] > 0:
                        clk[e] = max(clk[e], tp) + st_[e]
                        fin = max(fin, clk[e])
                ready[c] = fin
                rem[c] -= st_["pe"] + st_["dve"] + st_["act"]
                pos[c] += 1
                order.append(c)
            gens = [cf() for cf in chains]
            alive = [True] * len(chains)
            pend = list(order)
            while pend:
                advanced = False
                for k, c in enumerate(pend):
                    if not alive[c]:
                        pend.pop(k)
                        advanced = True
                        break
                    try:
                        r_ = next(gens[c])
                    except StopIteration:
                        alive[c] = False
                        pend.pop(k)
                        advanced = True
                        break
                    if r_ == "WAIT":
                        continue
                    pend.pop(k)
                    advanced = True
                    break
                assert advanced, "scheduler deadlock on weight slots"
            for c in range(len(chains)):
                while alive[c]:
                    try:
                        next(gens[c])
                    except StopIteration:
                        alive[c] = False
            for b in range(4):
                wb = wload(View(w_br_d, w_br_d.ap[l, b]), (4, 1024))
                wg0 = w_in_blk(l, 4864 + b * 1024)
                wg1 = w_in_blk(l, 4864 + b * 1024 + 512)
                for j in range(8):
                    wgj = wg0 if j < 4 else wg1
                    pg = mm_feat(wgj, (j % 4) * 128)
                    pb = mm_feat(wb, j * 128, src=yT[b], nk=4)
                    s1 = nsc()
                    P.add(A, "activation", s1.v, pg.v, AF.Sigmoid)
                    if b == 0:
                        P.add(V, "tensor_tensor", mix[:, j, :], s1.v, pb.v, ALU.mult)
                    else:
                        P.add(V, "tensor_tensor", s1.v, s1.v, pb.v, ALU.mult)
                        if b < 3:
                            P.add(V, "tensor_tensor", mix[:, j, :], mix[:, j, :], s1.v, ALU.add)
                        else:
                            P.add(V, "tensor_tensor", mixT[:, j, :], mix[:, j, :], s1.v, ALU.add)
                wrel(wb, wg0, wg1)
            wo = [wload(View(w_out_d, w_out_d.ap[l, :, nb * 512:(nb + 1) * 512]), (8, 512)) for nb in range(2)]
            def wout_tile(i):
                for nb in range(2):
                    ps = nps()
                    for c in range(8):
                        P.add("pe", "matmul", ps.v, mixT[:, c, i * 128:(i + 1) * 128], wo[nb][:, c, :],
                              start=(c == 0), stop=(c == 7))
                    P.add(V, "tensor_tensor", xt[i][:, nb * 512:(nb + 1) * 512],
                          xt[i][:, nb * 512:(nb + 1) * 512], ps.v, ALU.add)

            for step in range(6):
                if step < 4:
                    wout_tile(step)
                if 0 <= step - 1 < 4:
                    rms_a(step - 1)
                if 0 <= step - 2 < 4:
                    rms_b(step - 2, n2g[l])
            wrel(*wo)
            for blk in range(11):
                wf = wload(View(w_fi_d, w_fi_d.ap[l, :, blk * 512:(blk + 1) * 512]), (8, 512))
                for q in range(2):
                    pgt = mm_feat(wf, q * 128)
                    pup = mm_feat(wf, 256 + q * 128)
                    s1 = nsc()
                    P.add(A, "activation", s1.v, pgt.v, AF.Silu)
                    P.add(V, "tensor_tensor", actT[:, blk * 2 + q, :], s1.v, pup.v, ALU.mult)
                wrel(wf)
            if pair and g + 1 < NG:
                P.dma("sp", hT.v, hT_d[g + 1])
            for half in range(2):
                k0 = half * 11
                wfs = []
                for (ks, kn_) in ((0, 4), (4, 4), (8, 3)):
                    wfs.append((ks, kn_, wload(View(w_fo_d, w_fo_d.ap[l, (k0 + ks) * 128:(k0 + ks + kn_) * 128, :]),
                                               (kn_, 1024))))
                for i in range(4):
                    for nb in range(2):
                        ps = nps()
                        n = 0
                        for (ks, kn_, wv) in wfs:
                            for c in range(kn_):
                                P.add("pe", "matmul", ps.v, actT[:, k0 + ks + c, i * 128:(i + 1) * 128],
                                      wv[:, c, nb * 512:(nb + 1) * 512], start=(n == 0), stop=(n == 10))
                                n += 1
                        P.add(V, "tensor_tensor", xt[i][:, nb * 512:(nb + 1) * 512],
                              xt[i][:, nb * 512:(nb + 1) * 512], ps.v, ALU.add)
                    if half == 1:
                        P.dma("sp", dst_d[(g * 4 + i) * 128:(g * 4 + i + 1) * 128, :], xt[i].v)
                        if g + 1 < NG:
                            P.dma("sp", xt[i].v, src_d[((g + 1) * 4 + i) * 128:((g + 1) * 4 + i + 1) * 128, :])
                wrel(*[wv for (_, _, wv) in wfs])
            if g + 1 < NG:
                P.dma("sp", cs.v, View(cs_d, cs_d.ap[(g + 1) * GT:(g + 2) * GT, :].rearrange("(i p) n -> p i n", p=128)))
    P.finish("sp")
    P.emit(st)
    st.close()
    return nc, P


def _t5_bucket(dist):
    d = np.maximum(dist, 1).astype(np.float32)
    large = 16 + (np.log(d / np.float32(16)) / np.float32(np.log(128 / 16)) * np.float32(16)).astype(np.int32)
    large = np.minimum(large, 31)
    return np.where(dist < 16, dist, large)


def host_consts(TOK, pos0=0):
    c = {}
    c["ident"] = np.eye(128, dtype=np.float32)
    half = 64
    inv = (np.float32(10000.0) ** (-np.arange(half, dtype=np.float32) / np.float32(half))).astype(np.float32)
    pos = (pos0 + np.arange(TOK)).astype(np.float32)
    ang = (pos[:, None] * inv[None, :]).astype(np.float32).astype(np.float64)
    co, si = np.cos(ang).astype(np.float32), np.sin(ang).astype(np.float32)
    c["cs"] = np.ascontiguousarray(np.concatenate([co, co, -si, si], axis=1))
    gam = (1.0 - 2.0 ** (-5.0 - np.arange(4))).astype(np.float64)
    lg = np.log(gam)
    s = np.arange(128)[:, None]
    t = np.arange(128)[None, :]
    scale = 128.0 ** -0.5
    dec = np.zeros((128, 4, 128), np.float64)
    for h in range(4):
        dec[:, h, :] = np.where(t >= s, np.exp(lg[h] * np.maximum(t - s, 0)), 0.0) * scale
    c["dec"] = dec.reshape(128, 512).astype(np.float32)
    idx = np.arange(128)[:, None].astype(np.float64)
    xi = np.exp(lg[None, :] * (idx + 1.0))
    zeta = np.exp(lg[None, :] * (127.0 - idx)) * scale
    c["xz"] = np.concatenate([xi, zeta], axis=1).astype(np.float32)
    c["tril"] = np.ascontiguousarray(np.tile((t >= s).astype(np.float32), (1, 4)))
    q = np.arange(128)[None, None, :]
    j = np.arange(2)[None, :, None]
    ss = np.arange(128)[:, None, None]
    dist = q + 128 - (j * 128 + ss)
    c["mneg"] = np.where((dist >= 0) & (dist < 128), 0.0, NEG).astype(np.float32).reshape(128, 256)
    c["_bucket"] = _t5_bucket(np.clip(dist, 0, 127))
    return c


def host_params(inp, L):
    p = {}
    f = lambda a: np.ascontiguousarray(a, dtype=np.float32)
    w_in = np.array(inp["w_in"][:L], dtype=np.float32, copy=True)
    qb = w_in[:, :, 4096:4608].reshape(L, D, 2, 4, 64)
    w_in[:, :, 4096:4608] = qb.transpose(0, 1, 3, 2, 4).reshape(L, D, 512)
    p["w_in"] = w_in
    p["w_br"] = f(inp["w_branch"][:L])
    p["w_out"] = f(inp["w_out"][:L])
    wf = np.asarray(inp["w_ffn_in"][:L], dtype=np.float32)
    gt = wf[:, :, :2816].reshape(L, D, 11, 2, 128)
    up = wf[:, :, 2816:].reshape(L, D, 11, 2, 128)
    p["w_fi"] = np.ascontiguousarray(np.concatenate([gt, up], axis=3).reshape(L, D, 5632))
    p["w_fo"] = f(inp["w_ffn_out"][:L])
    pm = lambda v, n: f(np.asarray(v[:L]).reshape(L, n, 128).transpose(0, 2, 1))
    p["n1g"] = pm(inp["norm1_g"], 8)
    p["n2g"] = pm(inp["norm2_g"], 8)
    p["gng"] = f(inp["ret_gn_g"][:L])
    cwv = np.asarray(inp["conv_w"][:L])[:, :, 0, :]
    p["cw"] = f(cwv.reshape(L, 31, 4, 128).transpose(0, 3, 2, 1).reshape(L, 128, 124))
    p["cb"] = pm(inp["conv_b"], 4)
    p["clg"] = pm(inp["conv_ln_g"], 4)
    p["clb"] = pm(inp["conv_ln_b"], 4)
    p["glg"] = f(inp["gmlp_ln_g"][:L])
    p["glb"] = f(inp["gmlp_ln_b"][:L])
    p["gws"] = f(np.asarray(inp["gmlp_ws"][:L]).transpose(0, 3, 1, 2).reshape(L, 128, 512))
    p["gbs"] = f(np.asarray(inp["gmlp_bs"][:L]).reshape(L, 512))
    p["sqg"] = f(inp["swa_q_g"][:L])
    p["skg"] = f(inp["swa_k_g"][:L])
    p["sink"] = f(inp["swa_sinks"][:L])
    return p


def host_bias(rel_bias, bucket):
    rb = np.asarray(rel_bias, dtype=np.float32)
    g = rb[bucket]
    g = g.reshape(128, 2, 128, 2, 4).transpose(0, 3, 1, 4, 2)
    return np.ascontiguousarray(g.reshape(128, 2048))


_CACHE = {}


def run(inp, TOK, L, xs, debug=False, pair=None, pos0s=None):
    key = (TOK, L, debug, str(pair))
    if key not in _CACHE:
        _CACHE[key] = build(TOK, L, debug, pair)[0]
    nc = _CACHE[key]
    shared = host_params(inp, L)
    bucket = None
    cmaps = {}
    in_maps = []
    for ci, x in enumerate(xs):
        pos0 = 0 if pos0s is None else pos0s[ci]
        if pos0 not in cmaps:
            c = host_consts(TOK, pos0)
            bucket = c.pop("_bucket")
            cmaps[pos0] = c
        m = dict(shared)
        m.update(cmaps[pos0])
        if "biasg" not in shared:
            shared["biasg"] = host_bias(inp["rel_bias"], bucket)
        m["biasg"] = shared["biasg"]
        m["x"] = np.ascontiguousarray(x, dtype=np.float32)
        if pair:
            half = 1.0 if pos0 > 0 else 0.0
            m["flag"] = np.tile(np.array([[half, (half - 1.0) * 30000.0]], np.float32), (128, 1))
        in_maps.append(m)
    res = run_bass_kernel_spmd(nc, in_maps, core_ids=list(range(len(xs))))
    return res


def kernel(**inputs):
    x = np.asarray(inputs["x"])
    B, S, _ = x.shape
    H = S // 2
    xs = [x[c // 2, (c % 2) * H:(c % 2 + 1) * H] for c in range(2 * B)]
    pos0s = [(c % 2) * H for c in range(2 * B)]
    pair = [[2 * b, 2 * b + 1] for b in range(B)]
    res = run(inputs, H, 2, xs, pair=pair, pos0s=pos0s)
    out = np.empty((B, S, D), np.float32)
    for c in range(2 * B):
        out[c // 2, (c % 2) * H:(c % 2 + 1) * H] = res.results[c]["out"]
    return out
```

```python
import numpy as np
from contextlib import ExitStack
import concourse.bass as bass
import concourse.mybir as mybir
from concourse.bass_utils import run_bass_kernel_spmd

F32 = mybir.dt.float32
BF16 = mybir.dt.bfloat16
AF = mybir.ActivationFunctionType
ALU = mybir.AluOpType
AX = mybir.AxisListType


class Tile:
    def __init__(self, name, ap, space):
        self.name = name
        self.ap = ap
        self.space = space
        self.lastw = {}
        self.lastr = {}
        self.aliases = []
        self.dma_sem = None
        self.dma_cnt = 0

    def __getitem__(self, k):
        return View(self, self.ap[k])

    @property
    def v(self):
        return View(self, self.ap)


class View:
    def __init__(self, tile, ap):
        self.tile = tile
        self.ap = ap

    def __getitem__(self, k):
        return View(self.tile, self.ap[k])

    def rearrange(self, s, **kw):
        return View(self.tile, self.ap.rearrange(s, **kw))

    def to_broadcast(self, shape):
        return View(self.tile, self.ap.to_broadcast(shape))

    def broadcast_to(self, shape):
        return View(self.tile, self.ap.broadcast_to(shape))

    def unsqueeze(self, ax):
        return View(self.tile, self.ap.unsqueeze(ax))

    def bitcast(self, dt):
        return View(self.tile, self.ap.bitcast(dt))

    @property
    def shape(self):
        return self.ap.shape


class Op:
    __slots__ = ("eng", "meth", "args", "kwargs", "deps", "signal", "cnt", "idx",
                 "is_dma", "dma_tile", "dma_cnt", "extra_waits")

    def __init__(self, eng, meth, args, kwargs):
        self.eng = eng
        self.meth = meth
        self.args = args
        self.kwargs = kwargs
        self.deps = []
        self.signal = False
        self.cnt = 0
        self.idx = 0
        self.is_dma = False
        self.dma_tile = None
        self.dma_cnt = 0


class Prog:
    ENGS = ("pe", "act", "dve", "pool", "sp")

    def __init__(self, nc):
        self.nc = nc
        self.ops = {e: [] for e in self.ENGS}
        self.sb_off = 16512
        self.sb_end = 229344
        self.tiles = []
        self.dram_tiles = {}
        self.near = 3
        self.dry = False
        self.dry_cost = {"pe": 0.0, "dve": 0.0, "act": 0.0, "pool": 0.0, "sp": 0.0}

    def sb(self, name, shape, dtype, at=None):
        esz = 4 if dtype == F32 else 2
        n = 1
        for s in shape[1:]:
            n *= s
        nbytes = n * esz
        if at is None:
            off = (self.sb_off + 63) // 64 * 64
            self.sb_off = off + nbytes
            assert self.sb_off <= self.sb_end, (name, self.sb_off)
        else:
            off = at
        h = self.nc.alloc_sbuf_tensor_at(name, list(shape), dtype, offset=off)
        t = Tile(name, h[:] if len(shape) == 2 else h[(slice(None),) * len(shape)], "sb")
        t.off = off
        t.nbytes = nbytes
        self.tiles.append(t)
        return t

    def alias(self, a, b):
        a.aliases.append(b)
        b.aliases.append(a)

    def ps(self, name, shape, dtype=F32):
        h = self.nc.alloc_psum_tensor(name, list(shape), dtype)
        t = Tile(name, h[(slice(None),) * len(shape)], "ps")
        self.tiles.append(t)
        return t

    def dram(self, name, shape, dtype, kind):
        h = self.nc.dram_tensor(name, list(shape), dtype, kind=kind)
        t = Tile(name, h.ap(), "dram")
        self.dram_tiles[name] = t
        return t

    def _collect(self, op, reads, writes):
        eng = op.eng
        deps = op.deps
        for t in reads:
            for d in t.lastw.values():
                deps.append(d)
        for t in writes:
            for tt in [t] + t.aliases:
                for d in tt.lastw.values():
                    deps.append(d)
                for d in tt.lastr.values():
                    deps.append(d)
        return deps

    def add(self, eng, meth, *args, reads=None, writes=None, **kwargs):
        if self.dry:
            n = 1
            for a in args:
                if isinstance(a, View):
                    for d in a.ap.shape[1:]:
                        n *= d
                    break
            if eng == "pe":
                c = 0.03 + n * 0.00047
            elif eng == "dve":
                c = 0.1 + n * 0.00112
            else:
                c = 0.12 + n * 0.00095
            self.dry_cost[eng] += c
            return None
        op = Op(eng, meth, args, kwargs)
        r, w = [], []
        first = True
        for a in args:
            if isinstance(a, View):
                if first:
                    w.append(a.tile)
                    first = False
                else:
                    r.append(a.tile)
        for k, a in kwargs.items():
            if isinstance(a, View):
                if k in ("accum_out", "out"):
                    w.append(a.tile)
                else:
                    r.append(a.tile)
        if reads:
            r += [x.tile if isinstance(x, View) else x for x in reads]
        if writes:
            w += [x.tile if isinstance(x, View) else x for x in writes]
        self._collect(op, r, w)
        op.idx = len(self.ops[eng])
        self.ops[eng].append(op)
        for t in r:
            if t.space != "dram" or True:
                t.lastr[eng] = ("op", op)
        for t in w:
            t.lastw[eng] = ("op", op)
        return op

    def dma(self, eng, out, in_, sync_tile=None, **kwargs):
        if self.dry:
            return None
        op = Op(eng, "dma_start", (), dict(out=out, in_=in_, **kwargs))
        op.is_dma = True
        if sync_tile is None:
            sync_tile = out.tile if out.tile.space != "dram" else in_.tile
        st = sync_tile
        self._collect(op, [in_.tile], [out.tile])
        st.dma_cnt += 16
        op.dma_tile = st
        op.dma_cnt = st.dma_cnt
        op.idx = len(self.ops[eng])
        self.ops[eng].append(op)
        key = ("dma", st.name)
        dep = ("dma", st, st.dma_cnt)
        in_.tile.lastr[key] = dep
        out.tile.lastw[key] = dep
        return op

    def collective(self, kind, groups, in_tile, out_tile):
        op = Op("pool", "collective_compute", (kind, ALU.bypass),
                dict(replica_groups=groups, ins=[View(in_tile, in_tile.ap.opt())],
                     outs=[View(out_tile, out_tile.ap.opt())]))
        op.is_dma = True
        self._collect(op, [in_tile], [out_tile])
        out_tile.dma_cnt += 1
        op.dma_tile = out_tile
        op.dma_cnt = out_tile.dma_cnt
        op.idx = len(self.ops["pool"])
        self.ops["pool"].append(op)
        dep = ("dma", out_tile, out_tile.dma_cnt)
        in_tile.lastr[("dma", out_tile.name)] = dep
        out_tile.lastw[("dma", out_tile.name)] = dep
        return op

    def finish(self, eng="sp"):
        op = Op(eng, "nop", (), {})
        for t in self.tiles + list(self.dram_tiles.values()):
            if t.dma_cnt:
                op.deps.append(("dma", t, t.dma_cnt))
        op.idx = len(self.ops[eng])
        self.ops[eng].append(op)

    def emit(self, stack):
        nc = self.nc
        for e in self.ENGS:
            for op in self.ops[e]:
                for d in op.deps:
                    if d[0] == "op":
                        y = d[1]
                        if y.eng != e:
                            y.signal = True
                        elif e != "pe" and op.idx - y.idx <= self.near:
                            y.signal = True
        esem = {}
        for e in self.ENGS:
            c = 0
            for op in self.ops[e]:
                if op.signal and not op.is_dma:
                    c += 1
                    op.cnt = c
            if c:
                esem[e] = stack.enter_context(nc.semaphore("s_" + e))
        self.sig_counts = {e: max([o.cnt for o in self.ops[e]] + [0]) for e in self.ENGS}
        for t in self.tiles + list(self.dram_tiles.values()):
            if t.dma_cnt:
                t.dma_sem = stack.enter_context(nc.semaphore("d_" + t.name))
        block = stack.enter_context(nc.Block())
        nwaits = {e: 0 for e in self.ENGS}

        def run(e, eng):
            seen = {}
            for op in self.ops[e]:
                need = {}
                for d in op.deps:
                    if d[0] == "op":
                        y = d[1]
                        if y.is_dma:
                            continue
                        if y.eng == e and not (e != "pe" and op.idx - y.idx <= self.near):
                            continue
                        if y.eng == e and y is op:
                            continue
                        k = ("e", y.eng)
                        v = y.cnt
                        sem = esem[y.eng]
                    else:
                        k = ("d", d[1].name)
                        v = d[2]
                        sem = d[1].dma_sem
                    if seen.get(k, 0) >= v:
                        continue
                    if need.get(k, (None, 0))[1] < v:
                        need[k] = (sem, v)
                for k, (sem, v) in need.items():
                    eng.wait_ge(sem, v)
                    seen[k] = v
                    nwaits[e] += 1
                if op.meth == "nop":
                    continue
                args = [a.ap if isinstance(a, View) else a for a in op.args]
                kwargs = {k: (a.ap if isinstance(a, View) else ([x.ap if isinstance(x, View) else x for x in a] if isinstance(a, list) and a and isinstance(a[0], View) else a)) for k, a in op.kwargs.items()}
                ins = getattr(eng, op.meth)(*args, **kwargs)
                if op.is_dma:
                    if op.meth == "collective_compute":
                        ins.then_inc(op.dma_tile.dma_sem)
                    else:
                        ins.then_inc(op.dma_tile.dma_sem, 16)
                elif op.signal:
                    ins.then_inc(esem[e], 1)

        if self.ops["pe"]:
            @block.tensor
            def _(eng):
                run("pe", eng)
        if self.ops["act"]:
            @block.scalar
            def _(eng):
                run("act", eng)
        if self.ops["dve"]:
            @block.vector
            def _(eng):
                run("dve", eng)
        if self.ops["pool"]:
            @block.gpsimd
            def _(eng):
                run("pool", eng)
        if self.ops["sp"]:
            @block.sync
            def _(eng):
                run("sp", eng)
        self.nwaits = nwaits

D = 1024
GT = 512
NS = 5
SQ044 = 0.044715 ** 0.5
GELU_C = 1.5957691216057308
NEG = -30000.0


def build(TOK, L, debug=False, pair=None):
    nc = bass.Bass("TRN2", target_bir_lowering=False)
    P = Prog(nc)
    NG = TOK // GT
    EI = "ExternalInput"
    x_d = P.dram("x", [TOK, D], F32, EI)
    out_d = P.dram("out", [TOK, D], F32, "ExternalOutput")
    w_in_d = P.dram("w_in", [L, D, 8960], F32, EI)
    w_br_d = P.dram("w_br", [L, 4, 512, D], F32, EI)
    w_out_d = P.dram("w_out", [L, D, D], F32, EI)
    w_fi_d = P.dram("w_fi", [L, D, 5632], F32, EI)
    w_fo_d = P.dram("w_fo", [L, 2816, D], F32, EI)
    n1g_d = P.dram("n1g", [L, 128, 8], F32, EI)
    n2g_d = P.dram("n2g", [L, 128, 8], F32, EI)
    gng_d = P.dram("gng", [L, 512], F32, EI)
    cw_d = P.dram("cw", [L, 128, 4 * 31], F32, EI)
    cb_d = P.dram("cb", [L, 128, 4], F32, EI)
    clg_d = P.dram("clg", [L, 128, 4], F32, EI)
    clb_d = P.dram("clb", [L, 128, 4], F32, EI)
    glg_d = P.dram("glg", [L, 512], F32, EI)
    glb_d = P.dram("glb", [L, 512], F32, EI)
    gws_d = P.dram("gws", [L, 128, 512], F32, EI)
    gbs_d = P.dram("gbs", [L, 512], F32, EI)
    sqg_d = P.dram("sqg", [L, 64], F32, EI)
    skg_d = P.dram("skg", [L, 64], F32, EI)
    sink_d = P.dram("sink", [L, 8], F32, EI)
    bias_d = P.dram("biasg", [128, 2048], F32, EI)
    ident_d = P.dram("ident", [128, 128], F32, EI)
    cs_d = P.dram("cs", [TOK, 256], F32, EI)
    dec_d = P.dram("dec", [128, 512], F32, EI)
    xz_d = P.dram("xz", [128, 8], F32, EI)
    tril_d = P.dram("tril", [128, 512], F32, EI)
    mneg_d = P.dram("mneg", [128, 256], F32, EI)
    MSG = 892
    if pair:
        flag_d = P.dram("flag", [128, 2], F32, EI)
        msg_in_d = P.dram("msg_in", [128, MSG], F32, "Internal")
        msg_out_d = P.dram("msg_out", [256, MSG], F32, "Internal")
    if L > 1:
        x1_d = P.dram("x1", [TOK, D], F32, "Internal")
    dg_d = P.dram("dgm", [L * 4, 128, 31 * 128], BF16, "Internal")
    if debug:
        dbg_d = P.dram("dbg", [4, 128, 2048], F32, "ExternalOutput")
        dbg2_d = P.dram("dbg2", [8, 128, 512], F32, "ExternalOutput")
        def dump(k, view, n=512):
            P.dma("sp", dbg2_d[k, :, 0:n], view)

    st = ExitStack()
    ident = P.sb("ident", [128, 128], BF16)
    dec = P.sb("dec", [128, 512], F32)
    xz = P.sb("xz", [128, 8], F32)
    onesf = P.sb("onesf", [128, 128], F32)
    eps = P.sb("eps", [128, 1], F32)
    biasT = P.sb("biasT", [128, 2048], F32)
    n1g = [P.sb(f"n1g{l}", [128, 8], F32) for l in range(L)]
    n2g = [P.sb(f"n2g{l}", [128, 8], F32) for l in range(L)]
    cw = [P.sb(f"cw{l}", [128, 124], F32) for l in range(L)]
    cb = [P.sb(f"cb{l}", [128, 4], F32) for l in range(L)]
    clg = [P.sb(f"clg{l}", [128, 4], F32) for l in range(L)]
    clb = [P.sb(f"clb{l}", [128, 4], F32) for l in range(L)]
    sqg = [P.sb(f"sqg{l}", [128, 64], F32) for l in range(L)]
    skg = [P.sb(f"skg{l}", [128, 64], F32) for l in range(L)]
    esink = [P.sb(f"esink{l}", [128, 8], F32) for l in range(L)]
    wsT = [P.sb(f"wsT{l}", [128, 512], BF16) for l in range(L)]
    gng = [P.sb("gng", [128, 512], F32)] * L
    glg = [P.sb("glg", [128, 512], F32)] * L
    glb = [P.sb("glb", [128, 512], F32)] * L
    bsb = [P.sb("bsb", [128, 512], F32)] * L
    S32 = [P.sb("S32", [128, 512], F32)] * L
    Sbf = [P.sb("Sbf", [128, 512], BF16)] * L
    Uh = [P.sb("Uh", [128, 4, 30], F32)] * L
    kTr = [P.sb("kTr", [128, 2, 128], BF16)] * L
    vaug = [P.sb("vaug", [128, 2, 2, 66], BF16)] * L
    if pair:
        qT_d = P.dram("qT_s", [NG, 128, 4, GT], BF16, "Internal")
        sg_d = P.dram("sg_s", [TOK, 512], F32, "Internal")
        kT_d = P.dram("kT_s", [NG, 128, 4, GT], BF16, "Internal")
        kz_d = P.dram("kz_s", [TOK, 512], BF16, "Internal")
        vr_d = P.dram("vr_s", [TOK, 512], BF16, "Internal")
        hT_d = P.dram("hT_s", [NG, 128, 8, GT], BF16, "Internal")
        R_kr = [P.sb(f"R_kr{i}", [128, 512], BF16) for i in range(2)]

        def mk_tok(name):
            t_ = Tile(name, None, "tok")
            P.tiles.append(t_)
            return t_
        tokC = [mk_tok(f"tokC{i}") for i in range(4)]
        tokKs = [mk_tok(f"tokKs{i}") for i in range(4)]
        tokVs = [mk_tok(f"tokVs{i}") for i in range(4)]
        tokKl = [mk_tok(f"tokKl{i}") for i in range(4)]
        tokVl = [mk_tok(f"tokVl{i}") for i in range(4)]
        tokRl = [mk_tok(f"tokRl{i}") for i in range(4)]
        tokHs = [mk_tok(f"tokHs{i}") for i in range(4)]
        tokQs = [mk_tok(f"tokQs{i}") for i in range(4)]
        tokGs = [mk_tok(f"tokGs{i}") for i in range(4)]
        tokGl = [mk_tok(f"tokGl{i}") for i in range(4)]
        tokKTs = [mk_tok(f"tokKTs{i}") for i in range(4)]
        fl = P.sb("fl", [128, 2], F32)
        msg = P.sb("msg", [128, MSG], F32)
        recv = P.sb("recv", [128, MSG], F32)
    xt = [P.sb(f"xt{i}", [128, D], F32) for i in range(4)]
    hT = P.sb("hT", [128, 8, GT], BF16)
    htok = [P.sb(f"htok{i}", [128, D], BF16) for i in range(2)]
    r1 = P.sb_off
    actT = P.sb("actT", [128, 22, GT], BF16)
    r1e = P.sb_off
    P.sb_off = r1
    qT = P.sb("qT", [128, 4, GT], BF16)
    kT = P.sb("kT", [128, 4, GT], BF16)
    kz = P.sb("kz", [128, 4, 512], BF16)
    vret = P.sb("vret", [128, 4, 512], BF16)
    sgg = P.sb("sgg", [128, 4, 512], F32)
    for t in (qT, kT, kz, vret, sgg):
        P.alias(actT, t)
    P.sb_off = max(P.sb_off, r1e)
    yT = [P.sb(f"yT{b}", [128, 4, GT], BF16) for b in range(4)]
    r2 = P.sb_off
    mix = P.sb("mix", [128, 8, GT], F32)
    r2e = P.sb_off
    P.sb_off = r2
    U = P.sb("U", [128, 4, 30 + GT], BF16)
    acc = P.sb("acc", [128, 4, GT], F32)
    C_mean = P.sb("C_mean", [128, 512], F32)
    P.alias(mix, U)
    P.alias(mix, acc)
    P.alias(mix, C_mean)
    P.sb_off = max(P.sb_off, r2e)
    r3 = P.sb_off
    mixT = P.sb("mixT", [128, 8, GT], BF16)
    P.sb_off = r3
    uT = P.sb("uT", [128, 4, GT], F32)
    P.alias(mixT, uT)
    cs = P.sb("cs", [128, 4, 256], F32)
    WS = [P.sb(f"ws{i}", [128, 4096], BF16) for i in range(NS)]
    SC = [P.sb(f"sc{i}", [128, 512], F32) for i in range(6)]
    SB = [P.sb(f"sbb{i}", [128, 512], BF16) for i in range(4)]
    qTs = [P.sb(f"qTs{i}", [128, 512], BF16) for i in range(2)]
    pT = P.sb("pT", [128, 4, 512], BF16)
    st8 = P.sb("st8", [128, 8, 8], F32)
    mv = P.sb("mv", [128, 8, 2], F32)
    sm = [P.sb(f"sm{i}", [128, 8], F32) for i in range(4)]
    R_qr = [P.sb(f"R_qr{i}", [128, 512], BF16) for i in range(2)]
    R_sT = P.sb("R_sT", [128, 512], BF16)
    R_ya = P.sb("R_ya", [128, 512], BF16)
    C_rstd = P.sb("C_rstd", [128, 512], F32)
    G_vln = P.sb("G_vln", [128, 512], BF16)
    S_qn = P.sb("S_qn", [128, 512], BF16)
    S_kn = P.sb("S_kn", [128, 128], BF16)
    S_yd = P.sb("S_yd", [128, 512], BF16)
    tril = SC[5]
    PSB = [P.ps(f"ps{i}", [128, 512], F32) for i in range(8)]
    PS_CONV = PSB[7]
    print("SBUF used per partition:", P.sb_off - 16512, "of", P.sb_end - 16512)

    cnt = {"ps": 0, "ws": 0, "sc": 0, "sb": 0, "sm": 0}

    def nps():
        cnt["ps"] += 1
        return PSB[cnt["ps"] % 7]

    def nsc():
        cnt["sc"] += 1
        return SC[cnt["sc"] % 6]

    def nsb():
        cnt["sb"] += 1
        return SB[cnt["sb"] % 4]

    def nsm():
        cnt["sm"] += 1
        return sm[cnt["sm"] % 4]

    wfree = list(range(NS))

    def wload(src, shape3):
        assert wfree, "no free weight slot"
        sid = wfree.pop(0)
        w = WS[sid]
        kc, n = shape3
        dst = w[:, 0:kc * n].rearrange("p (c n) -> p c n", c=kc)
        P.dma("pool", dst, src.rearrange("(c p) n -> p c n", p=128))
        dst.sid = sid
        return dst

    def wload_flat(src, n):
        assert wfree, "no free weight slot"
        sid = wfree.pop(0)
        dst = WS[sid][:, 0:n]
        P.dma("pool", dst, src)
        dst.sid = sid
        return dst

    def wrel(*ws):
        for wv in ws:
            wfree.append(wv.sid)

    V, A = "dve", "act"

    def bc(dram_tile, l, n):
        return View(dram_tile, dram_tile.ap[l].partition_broadcast(128))

    P.dma("pool", ident.v, ident_d.v)
    P.dma("sp", dec.v, dec_d.v)
    P.dma("sp", xz.v, xz_d.v)
    if pair:
        P.dma("sp", fl.v, flag_d.v)
    P.dma("sp", tril.v, tril_d.v)
    P.dma("sp", biasT.v, bias_d.v)
    P.dma("sp", SC[0][:, 0:256], mneg_d.v)
    P.add(V, "memset", onesf.v, 1.0)
    P.add(V, "memset", eps.v, 1e-6)
    bt = biasT.v.rearrange("p (a j c q) -> p a j c q", a=2, j=2, c=4)
    mn = SC[0][:, 0:256].rearrange("p (j q) -> p j q", j=2)
    for a in range(2):
        for j in range(2):
            P.add(V, "tensor_tensor", bt[:, a, j], bt[:, a, j],
                  mn[:, j].unsqueeze(1).to_broadcast([128, 4, 128]), ALU.add)
    for l in range(L):
        P.dma("sp", n1g[l].v, n1g_d[l])
        P.dma("sp", n2g[l].v, n2g_d[l])
        P.dma("sp", cw[l].v, cw_d[l])
        P.dma("sp", cb[l].v, cb_d[l])
        P.dma("sp", clg[l].v, clg_d[l])
        P.dma("sp", clb[l].v, clb_d[l])
        P.dma("sp", sqg[l].v, bc(sqg_d, l, 64))
        P.dma("sp", skg[l].v, bc(skg_d, l, 64))
        P.dma("sp", esink[l].v, bc(sink_d, l, 8))
        s0 = nsc()
        P.dma("sp", s0.v, gws_d[l])
        P.add(V, "tensor_tensor", wsT[l].v, s0.v, tril.v, ALU.mult)
        P.add(A, "activation", esink[l].v, esink[l].v, AF.Exp)
        if l == 0:
            P.add(V, "memset", vaug[l].v, 1.0)
        for cc in range(4):
            sid = wfree.pop(0)
            stg = WS[sid]
            for j in range(31):
                if j % 3 == 0:
                    P.add(A, "activation", stg[:, j * 128:(j + 1) * 128], ident.v, AF.Copy,
                          scale=cw[l][:, cc * 31 + j:cc * 31 + j + 1])
                else:
                    P.add(V, "tensor_scalar", stg[:, j * 128:(j + 1) * 128], ident.v,
                          cw[l][:, cc * 31 + j:cc * 31 + j + 1], None, ALU.mult)
            P.dma("sp", dg_d[l * 4 + cc], stg[:, 0:31 * 128], sync_tile=stg)
            wfree.append(sid)

    def rstd_from(ss_view, n, scale, width):
        r = nsm()
        P.add(A, "activation", r[:, 0:width], ss_view, AF.Sqrt, bias=eps.v, scale=scale)
        P.add(V, "reciprocal", r[:, 0:width], r[:, 0:width])
        return r[:, 0:width]

    def rms_a(i):
        ss = nsm()
        hk = htok[i % 2]
        P.add(A, "activation", hk.v, xt[i].v, AF.Square, accum_out=ss[:, 0:1])
        r = rstd_from(ss[:, 0:1], 1, 1.0 / D, 1)
        P.add(A, "activation", hk.v, xt[i].v, AF.Copy, scale=r)

    def rms_b(i, gt):
        hk = htok[i % 2]
        ps = nps()
        pv = ps.v.bitcast(BF16)
        for c in range(8):
            P.add("pe", "transpose", pv[:, c * 128:(c + 1) * 128], hk[:, c * 128:(c + 1) * 128], ident.v)
        P.add(V, "tensor_tensor", hT[:, :, i * 128:(i + 1) * 128],
              pv.rearrange("p (c t) -> p c t", c=8),
              gt.v.unsqueeze(2).to_broadcast([128, 8, 128]), ALU.mult)

    def rmsnorm_to_hT(gt):
        rms_a(0)
        for i in range(4):
            if i + 1 < 4:
                rms_a(i + 1)
            rms_b(i, gt)

    def mm_tok(w, i, ncols=512):
        ps = nps()
        for c in range(8):
            P.add("pe", "matmul", ps[:, 0:ncols], hT[:, c, i * 128:(i + 1) * 128], w[:, c, 0:ncols],
                  start=(c == 0), stop=(c == 7))
        return ps

    def mm_feat(w, col0, src=None, nk=8):
        ps = nps()
        src = hT if src is None else src
        for c in range(nk):
            P.add("pe", "matmul", ps.v, w[:, c, col0:col0 + 128], src[:, c, :],
                  start=(c == 0), stop=(c == nk - 1))
        return ps

    def transpose4(src_tok, dstT, i):
        ps = nps()
        pv = ps.v.bitcast(BF16)
        for c in range(4):
            P.add("pe", "transpose", pv[:, c * 128:(c + 1) * 128], src_tok[:, c * 128:(c + 1) * 128], ident.v)
        P.add(A, "activation", dstT[:, :, i * 128:(i + 1) * 128],
              pv[:, 0:512].rearrange("p (c t) -> p c t", c=4), AF.Copy)

    def gelu_from_ps(ps, out_view, ncols=512):
        P.add(A, "activation", out_view, ps[:, 0:ncols], AF.Gelu_apprx_tanh)

    def w_in_blk(l, col0, ncols=512):
        return wload(View(w_in_d, w_in_d.ap[l, :, col0:col0 + ncols]), (8, ncols))

    def load_group(src_d, g):
        for i in range(4):
            P.dma("sp", xt[i].v, src_d[(g * 4 + i) * 128:(g * 4 + i + 1) * 128, :])
        P.dma("sp", cs.v, View(cs_d, cs_d.ap[g * GT:(g + 1) * GT, :].rearrange("(i p) n -> p i n", p=128)))

    def rotary(ps, i, dst=None):
        q4 = ps.v.rearrange("p (h d) -> p h d", h=4)
        t1, t2 = nsc(), nsc()
        cos2 = cs[:, i, 0:128].unsqueeze(1).to_broadcast([128, 4, 128])
        P.add(V, "tensor_tensor", t1.v.rearrange("p (h d) -> p h d", h=4), q4, cos2, ALU.mult)
        t24 = t2.v.rearrange("p (h d) -> p h d", h=4)
        P.add(V, "tensor_tensor", t24[:, :, 0:64], q4[:, :, 64:128],
              cs[:, i, 128:192].unsqueeze(1).to_broadcast([128, 4, 64]), ALU.mult)
        P.add(V, "tensor_tensor", t24[:, :, 64:128], q4[:, :, 0:64],
              cs[:, i, 192:256].unsqueeze(1).to_broadcast([128, 4, 64]), ALU.mult)
        qr = nsb() if dst is None else dst
        P.add(V, "tensor_tensor", qr.v, t1.v, t2.v, ALU.add)
        return qr

    def swa_kv_a(l, wkv, i, slot):
        pk = mm_tok(wkv, i, 256)
        s2 = nsc()
        P.add(A, "activation", s2[:, 0:128], pk[:, 0:128], AF.Square)
        ss2 = nsm()
        P.add(V, "tensor_reduce", ss2[:, 0:2], s2[:, 0:128].rearrange("p (h d) -> p h d", h=2), AX.X, ALU.add)
        r2 = rstd_from(ss2[:, 0:2], 2, 1.0 / 64, 2)
        P.add(V, "tensor_tensor", s2[:, 0:128].rearrange("p (h d) -> p h d", h=2),
              pk[:, 0:128].rearrange("p (h d) -> p h d", h=2),
              r2.unsqueeze(2).to_broadcast([128, 2, 64]), ALU.mult)
        P.add(V, "tensor_tensor", S_kn.v.rearrange("p (h d) -> p h d", h=2),
              s2[:, 0:128].rearrange("p (h d) -> p h d", h=2),
              skg[l].v.unsqueeze(1).to_broadcast([128, 2, 64]), ALU.mult)
        P.add(A, "activation", vaug[l][:, slot, :, 0:64],
              pk[:, 128:256].rearrange("p (h d) -> p h d", h=2), AF.Copy)

    def swa_kv_b(l, slot):
        pkt = nps()
        pktv = pkt.v.bitcast(BF16)
        P.add("pe", "transpose", pktv[:, 0:128], S_kn.v, ident.v)
        P.add(A, "activation", kTr[l][:, slot, :], pktv[:, 0:128], AF.Copy)

    def state_update(l, ps_k):
        for h in range(4):
            hs = slice(h * 128, (h + 1) * 128)
            gam = 1.0 - 2.0 ** (-5.0 - h)
            P.add(V, "scalar_tensor_tensor", S32[l][:, hs], S32[l][:, hs], float(gam ** 128),
                  ps_k[:, hs], ALU.mult, ALU.add)

    def prepass(l, src_d):
        P.add(V, "memset", S32[l].v, 0.0)
        wq_ = w_in_blk(l, 0)
        wk_ = w_in_blk(l, 512)
        wv_ = w_in_blk(l, 1024)
        wg_ = w_in_blk(l, 1536)
        P.dma("sp", gng[l].v, bc(gng_d, l, 512))

        def pp_proj(i, t):
            ps = mm_tok(wq_, i)
            rotary(ps, i, R_qr[t % 2])
            ps = mm_tok(wk_, i)
            kr = rotary(ps, i, R_kr[t % 2])
            P.add(V, "tensor_tensor", kz[:, i, :].rearrange("p (h d) -> p h d", h=4),
                  kr.v.rearrange("p (h d) -> p h d", h=4),
                  xz[:, 4:8].unsqueeze(2).to_broadcast([128, 4, 128]), ALU.mult)
            ps = mm_tok(wv_, i)
            P.add(A, "activation", vret[:, i, :], ps.v, AF.Copy)
            rows = slice(t * 128, (t + 1) * 128)
            P.dma("pool", kz_d[rows, :], kz[:, i, :], sync_tile=tokKs[i])
            P.dma("pool", vr_d[rows, :], vret[:, i, :], sync_tile=tokVs[i])
            ps = mm_tok(wg_, i)
            s1 = nsc()
            P.add(A, "activation", s1.v, ps.v, AF.Silu)
            P.add(V, "tensor_tensor", sgg[:, i, :], s1.v, gng[l].v, ALU.mult)
            P.dma("pool", sg_d[rows, :], sgg[:, i, :], sync_tile=tokGs[i])

        def pp_T(i, t):
            isl = slice(i * 128, (i + 1) * 128)
            transpose4(R_qr[t % 2], qT, i)
            transpose4(R_kr[t % 2], kT, i)
            P.dma("pool", qT_d[t // 4, :, :, isl], qT[:, :, isl], sync_tile=tokQs[i])
            P.dma("pool", kT_d[t // 4, :, :, isl], kT[:, :, isl], sync_tile=tokKTs[i])

        def pp_kv(i):
            ps_k = nps()
            for h in range(4):
                hs = slice(h * 128, (h + 1) * 128)
                P.add("pe", "matmul", ps_k[:, hs], kz[:, i, hs], vret[:, i, hs], start=True, stop=True)
            state_update(l, ps_k)

        NTT = NG * 4
        for step in range(NTT + 3):
            if step < NTT:
                P.dma("sp", xt[step % 4].v, src_d[step * 128:(step + 1) * 128, :])
                P.dma("sp", cs[:, step % 4, :], cs_d[step * 128:(step + 1) * 128, :], sync_tile=tokC[step % 4])
                rms_a(step % 4)
            if 0 <= step - 1 < NTT:
                t_ = step - 1
                rms_b(t_ % 4, n1g[l])
                isl = slice((t_ % 4) * 128, (t_ % 4 + 1) * 128)
                P.dma("pool", hT_d[t_ // 4, :, :, isl], hT[:, :, isl], sync_tile=tokHs[t_ % 4])
            if 0 <= step - 2 < NTT:
                pp_proj((step - 2) % 4, step - 2)
            if 0 <= step - 3 < NTT:
                pp_T((step - 3) % 4, step - 3)
                pp_kv((step - 3) % 4)
        wrel(wq_, wk_, wv_, wg_)
        load_group(src_d, 0)
        for g in [NG - 1]:
            if g == NG - 1:
                wa = w_in_blk(l, 2048)
                wg = w_in_blk(l, 2560)
                for cc in range(4):
                    pa, pg = nps(), nps()
                    for c in range(8):
                        P.add("pe", "matmul", pa[:, 0:128], wa[:, c, cc * 128:(cc + 1) * 128], hT[:, c, 384:512],
                              start=(c == 0), stop=(c == 7))
                    for c in range(8):
                        P.add("pe", "matmul", pg[:, 0:128], wg[:, c, cc * 128:(cc + 1) * 128], hT[:, c, 384:512],
                              start=(c == 0), stop=(c == 7))
                    s1 = nsc()
                    P.add(A, "activation", s1[:, 0:128], pg[:, 0:128], AF.Sigmoid)
                    P.add(V, "tensor_tensor", U[:, cc, 30 + 384:30 + 512], pa[:, 0:128], s1[:, 0:128], ALU.mult)
                P.add(V, "tensor_copy", Uh[l].v, U[:, :, GT:GT + 30])
                wkv = w_in_blk(l, 4608, 256)
                swa_kv_a(l, wkv, 3, 0)
                swa_kv_b(l, 0)
                wrel(wa, wg, wkv)
        P.dma("sp", hT.v, hT_d[0])
        P.add(V, "tensor_copy", msg[:, 0:512], S32[l].v)
        P.add(V, "tensor_copy", msg[:, 512:632].rearrange("p (c t) -> p c t", c=4), Uh[l].v)
        P.add(V, "tensor_copy", msg[:, 632:760], kTr[l][:, 0, :])
        P.add(V, "tensor_copy", msg[:, 760:892].rearrange("p (a e) -> p a e", a=2), vaug[l][:, 0])
        P.dma("sp", msg_in_d.v, msg.v, sync_tile=msg_in_d)
        P.collective("AllGather", pair, msg_in_d, msg_out_d)
        P.dma("sp", recv.v, msg_out_d[0:128, :])
        P.add(V, "tensor_scalar", S32[l].v, recv[:, 0:512], fl[:, 0:1], None, ALU.mult)
        P.add(A, "activation", Sbf[l].v, S32[l].v, AF.Copy)
        P.add(V, "tensor_scalar", Uh[l].v, recv[:, 512:632].rearrange("p (c t) -> p c t", c=4), fl[:, 0:1], None, ALU.mult)
        P.add(V, "tensor_scalar", kTr[l][:, 1, :], recv[:, 632:760], fl[:, 0:1], None, ALU.mult)
        P.add(V, "tensor_scalar", vaug[l][:, 1], recv[:, 760:892].rearrange("p (a e) -> p a e", a=2), fl[:, 0:1], None, ALU.mult)
        P.add(V, "memset", vaug[l][:, 1, :, 64:66], 1.0)

    for l in range(L):
        src_d = x_d if l == 0 else x1_d
        dst_d = out_d if l == L - 1 else x1_d
        if pair:
            prepass(l, src_d)
        else:
            P.add(V, "memset", S32[l].v, 0.0)
            P.add(V, "memset", Sbf[l].v, 0.0)
            P.add(V, "memset", Uh[l].v, 0.0)
        for g in range(NG):
            if g == 0 and not pair:
                load_group(src_d, g)
            if pair:
                P.dma("sp", qT.v, qT_d[g])
                P.dma("sp", kT.v, kT_d[g])
                for i in range(4):
                    rows = slice((g * 4 + i) * 128, (g * 4 + i + 1) * 128)
                    P.dma("sp", kz[:, i, :], kz_d[rows, :], sync_tile=tokKl[i])
                    P.dma("sp", sgg[:, i, :], sg_d[rows, :], sync_tile=tokGl[i])
                    P.dma("sp", vret[:, i, :], vr_d[rows, :], sync_tile=tokVl[i])
            P.dma("sp", gng[l].v, bc(gng_d, l, 512))
            P.dma("sp", glg[l].v, bc(glg_d, l, 512))
            P.dma("sp", glb[l].v, bc(glb_d, l, 512))
            P.dma("sp", bsb[l].v, bc(gbs_d, l, 512))
            if not pair:
                rmsnorm_to_hT(n1g[l])
            def need_slots(n):
                while len(wfree) < n:
                    yield "WAIT"

            def chain_R():
                for (col0, dstT, is_k) in ((0, qT, False), (512, kT, True)):
                    if pair:
                        continue
                    yield from need_slots(1)
                    w = w_in_blk(l, col0)
                    for i in range(4):
                        ps = mm_tok(w, i)
                        qr0 = rotary(ps, i, R_qr[i % 2])
                        if is_k:
                            P.add(V, "tensor_tensor", kz[:, i, :].rearrange("p (h d) -> p h d", h=4),
                                  qr0.v.rearrange("p (h d) -> p h d", h=4),
                                  xz[:, 4:8].unsqueeze(2).to_broadcast([128, 4, 128]), ALU.mult)
                        yield
                        transpose4(qr0, dstT, i)
                        yield
                    wrel(w)
                if not pair:
                    yield from need_slots(1)
                    w = w_in_blk(l, 1024)
                    for i in range(4):
                        ps = mm_tok(w, i)
                        P.add(A, "activation", vret[:, i, :], ps.v, AF.Copy)
                        yield
                    wrel(w)
                if not pair:
                    yield from need_slots(1)
                    w = w_in_blk(l, 1536)
                    for i in range(4):
                        ps = mm_tok(w, i)
                        s1 = nsc()
                        P.add(A, "activation", s1.v, ps.v, AF.Silu)
                        P.add(V, "tensor_tensor", sgg[:, i, :], s1.v, gng[l].v, ALU.mult)
                        yield
                    wrel(w)
                for i in range(4):
                    tsl = slice(i * 128, (i + 1) * 128)
                    ps_s = nps()
                    for h in range(4):
                        P.add("pe", "matmul", ps_s[:, h * 128:(h + 1) * 128], kT[:, h, tsl], qT[:, h, tsl],
                              start=True, stop=True)
                    P.add(V, "tensor_tensor", R_sT.v, ps_s.v, dec.v, ALU.mult)
                    yield
                    ps_i, ps_c, ps_k = nps(), nps(), nps()
                    for h in range(4):
                        hs = slice(h * 128, (h + 1) * 128)
                        P.add("pe", "matmul", ps_i[:, hs], R_sT[:, hs], vret[:, i, hs], start=True, stop=True)
                    for h in range(4):
                        hs = slice(h * 128, (h + 1) * 128)
                        P.add("pe", "matmul", ps_c[:, hs], qT[:, h, tsl], Sbf[l][:, hs], start=True, stop=True)
                    for h in range(4):
                        hs = slice(h * 128, (h + 1) * 128)
                        P.add("pe", "matmul", ps_k[:, hs], kz[:, i, hs], vret[:, i, hs], start=True, stop=True)
                    y = nsc()
                    P.add(V, "tensor_tensor", y.v.rearrange("p (h d) -> p h d", h=4),
                          ps_c.v.rearrange("p (h d) -> p h d", h=4),
                          xz[:, 0:4].unsqueeze(2).to_broadcast([128, 4, 128]), ALU.mult)
                    P.add(V, "tensor_tensor", y.v, y.v, ps_i.v, ALU.add)
                    state_update(l, ps_k)
                    P.add(A, "activation", Sbf[l].v, S32[l].v, AF.Copy)
                    for h in range(4):
                        P.add(V, "bn_stats", st8[:, h, 0:6], y[:, h * 128:(h + 1) * 128])
                    for h in range(4):
                        P.add(V, "bn_aggr", mv[:, h, :], st8[:, h, 0:6])
                    r = rstd_from(mv[:, 0:4, 1], 4, 1.0, 4)
                    y4 = y.v.rearrange("p (h d) -> p h d", h=4)
                    P.add(V, "tensor_tensor", y4, y4, mv[:, 0:4, 0:1].to_broadcast([128, 4, 128]), ALU.subtract)
                    P.add(V, "tensor_tensor", y4, y4, r.unsqueeze(2).to_broadcast([128, 4, 128]), ALU.mult)
                    P.add(V, "tensor_tensor", R_ya.v, y.v, sgg[:, i, :], ALU.mult)
                    yield
                    transpose4(R_ya, yT[0], i)
                    yield

            def chain_C():
                yield from need_slots(2)
                wa = w_in_blk(l, 2048)
                wg = w_in_blk(l, 2560)
                P.add(V, "tensor_copy", U[:, :, 0:30], Uh[l].v)
                for cc in range(4):
                    pa = mm_feat(wa, cc * 128)
                    pg = mm_feat(wg, cc * 128)
                    s1 = nsc()
                    P.add(A, "activation", s1.v, pg.v, AF.Sigmoid)
                    P.add(V, "tensor_tensor", U[:, cc, 30:30 + GT], pa.v, s1.v, ALU.mult)
                    yield
                wrel(wa, wg)
                P.add(V, "tensor_copy", Uh[l].v, U[:, :, GT:GT + 30])
                for cc in range(4):
                    yield from need_slots(1)
                    wd = wload_flat(dg_d[l * 4 + cc], 31 * 128)
                    for j in range(31):
                        P.add("pe", "matmul", PS_CONV.v, wd[:, j * 128:(j + 1) * 128], U[:, cc, j:j + GT],
                              start=(j == 0), stop=(j == 30))
                        if j % 8 == 7:
                            yield
                    P.add(A, "activation", acc[:, cc, :], PS_CONV.v, AF.Identity, bias=cb[l][:, cc:cc + 1])
                    wrel(wd)
                    yield
                p1, p2 = nps(), nps()
                for cc in range(4):
                    P.add("pe", "matmul", p1.v, onesf.v, acc[:, cc, :], start=(cc == 0), stop=(cc == 3))
                for cc in range(4):
                    s1 = nsc()
                    P.add(A, "activation", s1.v, acc[:, cc, :], AF.Square)
                    P.add("pe", "matmul", p2.v, onesf.v, s1.v, start=(cc == 0), stop=(cc == 3))
                P.add(A, "activation", C_mean.v, p1.v, AF.Copy, scale=1.0 / 512)
                P.add(V, "tensor_tensor", C_rstd.v, C_mean.v, C_mean.v, ALU.mult)
                P.add(V, "scalar_tensor_tensor", C_rstd.v, p2.v, 1.0 / 512, C_rstd.v, ALU.mult, ALU.subtract)
                P.add(A, "activation", C_rstd.v, C_rstd.v, AF.Sqrt, bias=eps.v, scale=1.0)
                P.add(V, "reciprocal", C_rstd.v, C_rstd.v)
                yield
                for cc in range(4):
                    P.add(V, "tensor_tensor", acc[:, cc, :], acc[:, cc, :], C_mean.v, ALU.subtract)
                    P.add(V, "tensor_tensor", acc[:, cc, :], acc[:, cc, :], C_rstd.v, ALU.mult)
                    P.add(A, "activation", yT[1][:, cc, :], acc[:, cc, :], AF.Silu,
                          scale=clg[l][:, cc:cc + 1], bias=clb[l][:, cc:cc + 1])
                    yield

            def chain_G():
                yield from need_slots(1)
                w = w_in_blk(l, 3072)
                for cc in range(4):
                    pu = mm_feat(w, cc * 128)
                    gelu_from_ps(pu, uT[:, cc, :])
                    yield
                wrel(w)
                yield from need_slots(1)
                w = w_in_blk(l, 3584)
                for i in range(4):
                    ps = mm_tok(w, i)
                    vv = nsc()
                    gelu_from_ps(ps, vv.v)
                    P.add(V, "bn_stats", st8[:, 4, 0:6], vv.v)
                    P.add(V, "bn_aggr", mv[:, 4, :], st8[:, 4, 0:6])
                    r = rstd_from(mv[:, 4, 1:2], 1, 1.0, 1)
                    P.add(V, "tensor_scalar", vv.v, vv.v, mv[:, 4, 0:1], r, ALU.subtract, ALU.mult)
                    P.add(V, "tensor_tensor", vv.v, vv.v, glg[l].v, ALU.mult)
                    P.add(V, "tensor_tensor", G_vln.v, vv.v, glb[l].v, ALU.add)
                    yield
                    pz = nps()
                    for gg in range(4):
                        gs = slice(gg * 128, (gg + 1) * 128)
                        P.add("pe", "matmul", pz[:, gs], G_vln[:, gs], wsT[l][:, gs], start=True, stop=True)
                    s1 = nsc()
                    P.add(V, "tensor_tensor", s1.v, pz.v, bsb[l].v, ALU.add)
                    P.add(V, "tensor_tensor", yT[2][:, :, i * 128:(i + 1) * 128],
                          s1.v.rearrange("p (g t) -> p g t", g=4), uT[:, :, i * 128:(i + 1) * 128], ALU.mult)
                    yield
                wrel(w)

            def chain_S():
                yield from need_slots(2)
                wq = w_in_blk(l, 4096)
                wkv = w_in_blk(l, 4608, 256)
                for i in range(4):
                    gi = g * 4 + i
                    cur, prv = gi % 2, (gi + 1) % 2
                    ps = mm_tok(wq, i)
                    s1 = nsc()
                    P.add(A, "activation", s1.v, ps.v, AF.Square)
                    ss = nsm()
                    P.add(V, "tensor_reduce", ss.v, s1.v.rearrange("p (h d) -> p h d", h=8), AX.X, ALU.add)
                    r = rstd_from(ss.v, 8, 1.0 / 64, 8)
                    P.add(V, "tensor_tensor", s1.v.rearrange("p (h d) -> p h d", h=8),
                          ps.v.rearrange("p (h d) -> p h d", h=8),
                          r.unsqueeze(2).to_broadcast([128, 8, 64]), ALU.mult)
                    P.add(V, "tensor_tensor", S_qn.v.rearrange("p (h d) -> p h d", h=8),
                          s1.v.rearrange("p (h d) -> p h d", h=8),
                          sqg[l].v.unsqueeze(1).to_broadcast([128, 8, 64]), ALU.mult)
                    yield
                    qts = qTs[i % 2]
                    pq = nps()
                    pqv = pq.v.bitcast(BF16)
                    for c in range(4):
                        P.add("pe", "transpose", pqv[:, c * 128:(c + 1) * 128], S_qn[:, c * 128:(c + 1) * 128], ident.v)
                    P.add(A, "activation", qts.v, pqv[:, 0:512], AF.Copy)
                    swa_kv_a(l, wkv, i, cur)
                    yield
                    swa_kv_b(l, cur)
                    yield
                    has_prev = gi > 0 or bool(pair)
                    js = ([0] if has_prev else []) + [1]
                    for a in range(2):
                        for j in js:
                            slot = prv if j == 0 else cur
                            pss = nps()
                            P.add("pe", "matmul", pss.v, kTr[l][a * 64:(a + 1) * 64, slot, :],
                                  qts[a * 64:(a + 1) * 64, :], start=True, stop=True)
                            s3 = nsc()
                            P.add(V, "scalar_tensor_tensor", s3.v, pss.v, 0.125,
                                  biasT[:, (a * 2 + j) * 512:(a * 2 + j + 1) * 512], ALU.mult, ALU.add)
                            if pair and gi == 0 and j == 0:
                                P.add(V, "tensor_scalar", s3.v, s3.v, fl[:, 1:2], None, ALU.add)
                            P.add(A, "activation", pT[:, a * 2 + j, :], s3.v, AF.Exp)
                    yield
                    po = [nps(), nps()]
                    for a in range(2):
                        for c in range(4):
                            for jn, j in enumerate(js):
                                slot = prv if j == 0 else cur
                                P.add("pe", "matmul", po[a][:, c * 65:(c + 1) * 65],
                                      pT[:, a * 2 + j, c * 128:(c + 1) * 128], vaug[l][:, slot, a, 0:65],
                                      start=(jn == 0), stop=(jn == len(js) - 1))
                    den = nsm()
                    for a in range(2):
                        P.add(V, "tensor_tensor", den[:, a * 4:(a + 1) * 4],
                              po[a][:, 0:260].rearrange("p (c e) -> p c e", c=4)[:, :, 64],
                              esink[l][:, a * 4:(a + 1) * 4], ALU.add)
                    P.add(V, "reciprocal", den.v, den.v)
                    for a in range(2):
                        P.add(V, "tensor_tensor", S_yd[:, a * 256:(a + 1) * 256].rearrange("p (c d) -> p c d", c=4),
                              po[a][:, 0:260].rearrange("p (c e) -> p c e", c=4)[:, :, 0:64],
                              den[:, a * 4:(a + 1) * 4].unsqueeze(2).to_broadcast([128, 4, 64]), ALU.mult)
                    yield
                    transpose4(S_yd, yT[3], i)
                    yield
                wrel(wq, wkv)

            chains = [chain_R, chain_C, chain_G, chain_S]
            saved = (dict(cnt), list(wfree))
            P.dry = True
            costs = []
            for cf in chains:
                wfree[:] = list(range(NS))
                steps = []
                gen = cf()
                while True:
                    P.dry_cost = {"pe": 0.0, "dve": 0.0, "act": 0.0, "pool": 0.0, "sp": 0.0}
                    try:
                        next(gen)
                    except StopIteration:
                        steps.append(dict(P.dry_cost))
                        break
                    steps.append(dict(P.dry_cost))
                costs.append(steps)
            P.dry = False
            cnt.clear()
            cnt.update(saved[0])
            wfree[:] = saved[1]
            clk = {"pe": 0.0, "dve": 0.0, "act": 0.0}
            ready = [0.0] * len(chains)
            pos = [0] * len(chains)
            rem = [sum(c["pe"] + c["dve"] + c["act"] for c in st_) for st_ in costs]
            order = []
            while any(pos[c] < len(costs[c]) for c in range(len(chains))):
                cand = [c for c in range(len(chains)) if pos[c] < len(costs[c])]
                rdy = [c for c in cand if ready[c] <= clk["pe"] + 0.3]
                if rdy:
                    c = max(rdy, key=lambda c: rem[c])
                else:
                    c = min(cand, key=lambda c: ready[c])
                st_ = costs[c][pos[c]]
                tp = max(clk["pe"], ready[c] if st_["pe"] > 0 else 0.0) + st_["pe"]
                if st_["pe"] > 0:
                    clk["pe"] = tp
                fin = tp
                for e in ("dve", "act"):
                    if st_[e] > 0:
                        clk[e] = max(clk[e], tp) + st_[e]
                        fin = max(fin, clk[e])
                ready[c] = fin
                rem[c] -= st_["pe"] + st_["dve"] + st_["act"]
                pos[c] += 1
                order.append(c)
            gens = [cf() for cf in chains]
            alive = [True] * len(chains)
            pend = list(order)
            while pend:
                advanced = False
                for k, c in enumerate(pend):
                    if not alive[c]:
                        pend.pop(k)
                        advanced = True
                        break
                    try:
                        r_ = next(gens[c])
                    except StopIteration:
                        alive[c] = False
                        pend.pop(k)
                        advanced = True
                        break
                    if r_ == "WAIT":
                        continue
                    pend.pop(k)
                    advanced = True
                    break
                assert advanced, "scheduler deadlock on weight slots"
            for c in range(len(chains)):
                while alive[c]:
                    try:
                        next(gens[c])
                    except StopIteration:
                        alive[c] = False
            for b in range(4):
                wb = wload(View(w_br_d, w_br_d.ap[l, b]), (4, 1024))
                for hh in range(2):
                    wgh = w_in_blk(l, 4864 + b * 1024 + hh * 512)
                    for jj in range(4):
                        j = hh * 4 + jj
                        pg = mm_feat(wgh, jj * 128)
                        pb = mm_feat(wb, j * 128, src=yT[b], nk=4)
                        s1 = nsc()
                        P.add(A, "activation", s1.v, pg.v, AF.Sigmoid)
                        if b == 0:
                            P.add(V, "tensor_tensor", mix[:, j, :], s1.v, pb.v, ALU.mult)
                        else:
                            P.add(V, "tensor_tensor", s1.v, s1.v, pb.v, ALU.mult)
                            if b < 3:
                                P.add(V, "tensor_tensor", mix[:, j, :], mix[:, j, :], s1.v, ALU.add)
                            else:
                                P.add(V, "tensor_tensor", mixT[:, j, :], mix[:, j, :], s1.v, ALU.add)
                    wrel(wgh)
                wrel(wb)
            wo = [wload(View(w_out_d, w_out_d.ap[l, :, nb * 512:(nb + 1) * 512]), (8, 512)) for nb in range(2)]
            def wout_tile(i):
                for nb in range(2):
                    ps = nps()
                    for c in range(8):
                        P.add("pe", "matmul", ps.v, mixT[:, c, i * 128:(i + 1) * 128], wo[nb][:, c, :],
                              start=(c == 0), stop=(c == 7))
                    P.add(V, "tensor_tensor", xt[i][:, nb * 512:(nb + 1) * 512],
                          xt[i][:, nb * 512:(nb + 1) * 512], ps.v, ALU.add)

            for step in range(6):
                if step < 4:
                    wout_tile(step)
                if 0 <= step - 1 < 4:
                    rms_a(step - 1)
                if 0 <= step - 2 < 4:
                    rms_b(step - 2, n2g[l])
            wrel(*wo)
            for blk in range(11):
                wf = wload(View(w_fi_d, w_fi_d.ap[l, :, blk * 512:(blk + 1) * 512]), (8, 512))
                for q in range(2):
                    pgt = mm_feat(wf, q * 128)
                    pup = mm_feat(wf, 256 + q * 128)
                    s1 = nsc()
                    P.add(A, "activation", s1.v, pgt.v, AF.Silu)
                    P.add(V, "tensor_tensor", actT[:, blk * 2 + q, :], s1.v, pup.v, ALU.mult)
                wrel(wf)
            if pair and g + 1 < NG:
                P.dma("sp", hT.v, hT_d[g + 1])
            for half in range(2):
                k0 = half * 11
                wfs = []
                for (ks, kn_) in ((0, 4), (4, 4), (8, 3)):
                    wfs.append((ks, kn_, wload(View(w_fo_d, w_fo_d.ap[l, (k0 + ks) * 128:(k0 + ks + kn_) * 128, :]),
                                               (kn_, 1024))))
                for i in range(4):
                    for nb in range(2):
                        ps = nps()
                        n = 0
                        for (ks, kn_, wv) in wfs:
                            for c in range(kn_):
                                P.add("pe", "matmul", ps.v, actT[:, k0 + ks + c, i * 128:(i + 1) * 128],
                                      wv[:, c, nb * 512:(nb + 1) * 512], start=(n == 0), stop=(n == 10))
                                n += 1
                        P.add(V, "tensor_tensor", xt[i][:, nb * 512:(nb + 1) * 512],
                              xt[i][:, nb * 512:(nb + 1) * 512], ps.v, ALU.add)
                    if half == 1:
                        P.dma("sp", dst_d[(g * 4 + i) * 128:(g * 4 + i + 1) * 128, :], xt[i].v)
                        if g + 1 < NG:
                            P.dma("sp", xt[i].v, src_d[((g + 1) * 4 + i) * 128:((g + 1) * 4 + i + 1) * 128, :])
                wrel(*[wv for (_, _, wv) in wfs])
            if g + 1 < NG:
                P.dma("sp", cs.v, View(cs_d, cs_d.ap[(g + 1) * GT:(g + 2) * GT, :].rearrange("(i p) n -> p i n", p=128)))
    P.finish("sp")
    P.emit(st)
    st.close()
    return nc, P


def _t5_bucket(dist):
    d = np.maximum(dist, 1).astype(np.float32)
    large = 16 + (np.log(d / np.float32(16)) / np.float32(np.log(128 / 16)) * np.float32(16)).astype(np.int32)
    large = np.minimum(large, 31)
    return np.where(dist < 16, dist, large)


def host_consts(TOK, pos0=0):
    c = {}
    c["ident"] = np.eye(128, dtype=np.float32)
    half = 64
    inv = (np.float32(10000.0) ** (-np.arange(half, dtype=np.float32) / np.float32(half))).astype(np.float32)
    pos = (pos0 + np.arange(TOK)).astype(np.float32)
    ang = (pos[:, None] * inv[None, :]).astype(np.float32).astype(np.float64)
    co, si = np.cos(ang).astype(np.float32), np.sin(ang).astype(np.float32)
    c["cs"] = np.ascontiguousarray(np.concatenate([co, co, -si, si], axis=1))
    gam = (1.0 - 2.0 ** (-5.0 - np.arange(4))).astype(np.float64)
    lg = np.log(gam)
    s = np.arange(128)[:, None]
    t = np.arange(128)[None, :]
    scale = 128.0 ** -0.5
    dec = np.zeros((128, 4, 128), np.float64)
    for h in range(4):
        dec[:, h, :] = np.where(t >= s, np.exp(lg[h] * np.maximum(t - s, 0)), 0.0) * scale
    c["dec"] = dec.reshape(128, 512).astype(np.float32)
    idx = np.arange(128)[:, None].astype(np.float64)
    xi = np.exp(lg[None, :] * (idx + 1.0))
    zeta = np.exp(lg[None, :] * (127.0 - idx)) * scale
    c["xz"] = np.concatenate([xi, zeta], axis=1).astype(np.float32)
    c["tril"] = np.ascontiguousarray(np.tile((t >= s).astype(np.float32), (1, 4)))
    q = np.arange(128)[None, None, :]
    j = np.arange(2)[None, :, None]
    ss = np.arange(128)[:, None, None]
    dist = q + 128 - (j * 128 + ss)
    c["mneg"] = np.where((dist >= 0) & (dist < 128), 0.0, NEG).astype(np.float32).reshape(128, 256)
    c["_bucket"] = _t5_bucket(np.clip(dist, 0, 127))
    return c


def host_params(inp, L):
    p = {}
    f = lambda a: np.ascontiguousarray(a, dtype=np.float32)
    w_in = np.array(inp["w_in"][:L], dtype=np.float32, copy=True)
    qb = w_in[:, :, 4096:4608].reshape(L, D, 2, 4, 64)
    w_in[:, :, 4096:4608] = qb.transpose(0, 1, 3, 2, 4).reshape(L, D, 512)
    p["w_in"] = w_in
    p["w_br"] = f(inp["w_branch"][:L])
    p["w_out"] = f(inp["w_out"][:L])
    wf = np.asarray(inp["w_ffn_in"][:L], dtype=np.float32)
    gt = wf[:, :, :2816].reshape(L, D, 11, 2, 128)
    up = wf[:, :, 2816:].reshape(L, D, 11, 2, 128)
    p["w_fi"] = np.ascontiguousarray(np.concatenate([gt, up], axis=3).reshape(L, D, 5632))
    p["w_fo"] = f(inp["w_ffn_out"][:L])
    pm = lambda v, n: f(np.asarray(v[:L]).reshape(L, n, 128).transpose(0, 2, 1))
    p["n1g"] = pm(inp["norm1_g"], 8)
    p["n2g"] = pm(inp["norm2_g"], 8)
    p["gng"] = f(inp["ret_gn_g"][:L])
    cwv = np.asarray(inp["conv_w"][:L])[:, :, 0, :]
    p["cw"] = f(cwv.reshape(L, 31, 4, 128).transpose(0, 3, 2, 1).reshape(L, 128, 124))
    p["cb"] = pm(inp["conv_b"], 4)
    p["clg"] = pm(inp["conv_ln_g"], 4)
    p["clb"] = pm(inp["conv_ln_b"], 4)
    p["glg"] = f(inp["gmlp_ln_g"][:L])
    p["glb"] = f(inp["gmlp_ln_b"][:L])
    p["gws"] = f(np.asarray(inp["gmlp_ws"][:L]).transpose(0, 3, 1, 2).reshape(L, 128, 512))
    p["gbs"] = f(np.asarray(inp["gmlp_bs"][:L]).reshape(L, 512))
    p["sqg"] = f(inp["swa_q_g"][:L])
    p["skg"] = f(inp["swa_k_g"][:L])
    p["sink"] = f(inp["swa_sinks"][:L])
    return p


def host_bias(rel_bias, bucket):
    rb = np.asarray(rel_bias, dtype=np.float32)
    g = rb[bucket]
    g = g.reshape(128, 2, 128, 2, 4).transpose(0, 3, 1, 4, 2)
    return np.ascontiguousarray(g.reshape(128, 2048))


_CACHE = {}


def run(inp, TOK, L, xs, debug=False, pair=None, pos0s=None):
    key = (TOK, L, debug, str(pair))
    if key not in _CACHE:
        _CACHE[key] = build(TOK, L, debug, pair)[0]
    nc = _CACHE[key]
    shared = host_params(inp, L)
    bucket = None
    cmaps = {}
    in_maps = []
    for ci, x in enumerate(xs):
        pos0 = 0 if pos0s is None else pos0s[ci]
        if pos0 not in cmaps:
            c = host_consts(TOK, pos0)
            bucket = c.pop("_bucket")
            cmaps[pos0] = c
        m = dict(shared)
        m.update(cmaps[pos0])
        if "biasg" not in shared:
            shared["biasg"] = host_bias(inp["rel_bias"], bucket)
        m["biasg"] = shared["biasg"]
        m["x"] = np.ascontiguousarray(x, dtype=np.float32)
        if pair:
            half = 1.0 if pos0 > 0 else 0.0
            m["flag"] = np.tile(np.array([[half, (half - 1.0) * 30000.0]], np.float32), (128, 1))
        in_maps.append(m)
    res = run_bass_kernel_spmd(nc, in_maps, core_ids=list(range(len(xs))))
    return res


def kernel(**inputs):
    x = np.asarray(inputs["x"])
    B, S, _ = x.shape
    H = S // 2
    xs = [x[c // 2, (c % 2) * H:(c % 2 + 1) * H] for c in range(2 * B)]
    pos0s = [(c % 2) * H for c in range(2 * B)]
    pair = [[2 * b, 2 * b + 1] for b in range(B)]
    res = run(inputs, H, 2, xs, pair=pair, pos0s=pos0s)
    out = np.empty((B, S, D), np.float32)
    for c in range(2 * B):
        out[c // 2, (c % 2) * H:(c % 2 + 1) * H] = res.results[c]["out"]
    return out
```

```python
import numpy as np
from contextlib import ExitStack
import concourse.bass as bass
import concourse.mybir as mybir
from concourse.bass_utils import run_bass_kernel_spmd

F32 = mybir.dt.float32
BF16 = mybir.dt.bfloat16
AF = mybir.ActivationFunctionType
ALU = mybir.AluOpType
AX = mybir.AxisListType


class Tile:
    def __init__(self, name, ap, space):
        self.name = name
        self.ap = ap
        self.space = space
        self.lastw = {}
        self.lastr = {}
        self.aliases = []
        self.dma_sem = None
        self.dma_cnt = 0

    def __getitem__(self, k):
        return View(self, self.ap[k])

    @property
    def v(self):
        return View(self, self.ap)


class View:
    def __init__(self, tile, ap):
        self.tile = tile
        self.ap = ap

    def __getitem__(self, k):
        return View(self.tile, self.ap[k])

    def rearrange(self, s, **kw):
        return View(self.tile, self.ap.rearrange(s, **kw))

    def to_broadcast(self, shape):
        return View(self.tile, self.ap.to_broadcast(shape))

    def broadcast_to(self, shape):
        return View(self.tile, self.ap.broadcast_to(shape))

    def unsqueeze(self, ax):
        return View(self.tile, self.ap.unsqueeze(ax))

    def bitcast(self, dt):
        return View(self.tile, self.ap.bitcast(dt))

    @property
    def shape(self):
        return self.ap.shape


class Op:
    __slots__ = ("eng", "meth", "args", "kwargs", "deps", "signal", "cnt", "idx",
                 "is_dma", "dma_tile", "dma_cnt", "extra_waits")

    def __init__(self, eng, meth, args, kwargs):
        self.eng = eng
        self.meth = meth
        self.args = args
        self.kwargs = kwargs
        self.deps = []
        self.signal = False
        self.cnt = 0
        self.idx = 0
        self.is_dma = False
        self.dma_tile = None
        self.dma_cnt = 0


class Prog:
    ENGS = ("pe", "act", "dve", "pool", "sp")

    def __init__(self, nc):
        self.nc = nc
        self.ops = {e: [] for e in self.ENGS}
        self.sb_off = 16512
        self.sb_end = 229344
        self.tiles = []
        self.dram_tiles = {}
        self.near = 3
        self.dry = False
        self.dry_cost = {"pe": 0.0, "dve": 0.0, "act": 0.0, "pool": 0.0, "sp": 0.0}

    def sb(self, name, shape, dtype, at=None):
        esz = 4 if dtype == F32 else 2
        n = 1
        for s in shape[1:]:
            n *= s
        nbytes = n * esz
        if at is None:
            off = (self.sb_off + 63) // 64 * 64
            self.sb_off = off + nbytes
            assert self.sb_off <= self.sb_end, (name, self.sb_off)
        else:
            off = at
        h = self.nc.alloc_sbuf_tensor_at(name, list(shape), dtype, offset=off)
        t = Tile(name, h[:] if len(shape) == 2 else h[(slice(None),) * len(shape)], "sb")
        t.off = off
        t.nbytes = nbytes
        self.tiles.append(t)
        return t

    def alias(self, a, b):
        a.aliases.append(b)
        b.aliases.append(a)

    def ps(self, name, shape, dtype=F32):
        h = self.nc.alloc_psum_tensor(name, list(shape), dtype)
        t = Tile(name, h[(slice(None),) * len(shape)], "ps")
        self.tiles.append(t)
        return t

    def dram(self, name, shape, dtype, kind):
        h = self.nc.dram_tensor(name, list(shape), dtype, kind=kind)
        t = Tile(name, h.ap(), "dram")
        self.dram_tiles[name] = t
        return t

    def _collect(self, op, reads, writes):
        eng = op.eng
        deps = op.deps
        for t in reads:
            for d in t.lastw.values():
                deps.append(d)
        for t in writes:
            for tt in [t] + t.aliases:
                for d in tt.lastw.values():
                    deps.append(d)
                for d in tt.lastr.values():
                    deps.append(d)
        return deps

    def add(self, eng, meth, *args, reads=None, writes=None, **kwargs):
        if self.dry:
            n = 1
            for a in args:
                if isinstance(a, View):
                    for d in a.ap.shape[1:]:
                        n *= d
                    break
            if eng == "pe":
                c = 0.03 + n * 0.00047
            elif eng == "dve":
                c = 0.1 + n * 0.00112
            else:
                c = 0.12 + n * 0.00095
            self.dry_cost[eng] += c
            return None
        op = Op(eng, meth, args, kwargs)
        r, w = [], []
        first = True
        for a in args:
            if isinstance(a, View):
                if first:
                    w.append(a.tile)
                    first = False
                else:
                    r.append(a.tile)
        for k, a in kwargs.items():
            if isinstance(a, View):
                if k in ("accum_out", "out"):
                    w.append(a.tile)
                else:
                    r.append(a.tile)
        if reads:
            r += [x.tile if isinstance(x, View) else x for x in reads]
        if writes:
            w += [x.tile if isinstance(x, View) else x for x in writes]
        self._collect(op, r, w)
        op.idx = len(self.ops[eng])
        self.ops[eng].append(op)
        for t in r:
            if t.space != "dram" or True:
                t.lastr[eng] = ("op", op)
        for t in w:
            t.lastw[eng] = ("op", op)
        return op

    def dma(self, eng, out, in_, sync_tile=None, **kwargs):
        if self.dry:
            return None
        op = Op(eng, "dma_start", (), dict(out=out, in_=in_, **kwargs))
        op.is_dma = True
        if sync_tile is None:
            sync_tile = out.tile if out.tile.space != "dram" else in_.tile
        st = sync_tile
        self._collect(op, [in_.tile], [out.tile])
        st.dma_cnt += 16
        op.dma_tile = st
        op.dma_cnt = st.dma_cnt
        op.idx = len(self.ops[eng])
        self.ops[eng].append(op)
        key = ("dma", st.name)
        dep = ("dma", st, st.dma_cnt)
        in_.tile.lastr[key] = dep
        out.tile.lastw[key] = dep
        return op

    def collective(self, kind, groups, in_tile, out_tile):
        op = Op("pool", "collective_compute", (kind, ALU.bypass),
                dict(replica_groups=groups, ins=[View(in_tile, in_tile.ap.opt())],
                     outs=[View(out_tile, out_tile.ap.opt())]))
        op.is_dma = True
        self._collect(op, [in_tile], [out_tile])
        out_tile.dma_cnt += 1
        op.dma_tile = out_tile
        op.dma_cnt = out_tile.dma_cnt
        op.idx = len(self.ops["pool"])
        self.ops["pool"].append(op)
        dep = ("dma", out_tile, out_tile.dma_cnt)
        in_tile.lastr[("dma", out_tile.name)] = dep
        out_tile.lastw[("dma", out_tile.name)] = dep
        return op

    def finish(self, eng="sp"):
        op = Op(eng, "nop", (), {})
        for t in self.tiles + list(self.dram_tiles.values()):
            if t.dma_cnt:
                op.deps.append(("dma", t, t.dma_cnt))
        op.idx = len(self.ops[eng])
        self.ops[eng].append(op)

    def emit(self, stack):
        nc = self.nc
        for e in self.ENGS:
            for op in self.ops[e]:
                for d in op.deps:
                    if d[0] == "op":
                        y = d[1]
                        if y.eng != e:
                            y.signal = True
                        elif e != "pe" and op.idx - y.idx <= self.near:
                            y.signal = True
        esem = {}
        for e in self.ENGS:
            c = 0
            for op in self.ops[e]:
                if op.signal and not op.is_dma:
                    c += 1
                    op.cnt = c
            if c:
                esem[e] = stack.enter_context(nc.semaphore("s_" + e))
        self.sig_counts = {e: max([o.cnt for o in self.ops[e]] + [0]) for e in self.ENGS}
        for t in self.tiles + list(self.dram_tiles.values()):
            if t.dma_cnt:
                t.dma_sem = stack.enter_context(nc.semaphore("d_" + t.name))
        block = stack.enter_context(nc.Block())
        nwaits = {e: 0 for e in self.ENGS}

        def run(e, eng):
            seen = {}
            for op in self.ops[e]:
                need = {}
                for d in op.deps:
                    if d[0] == "op":
                        y = d[1]
                        if y.is_dma:
                            continue
                        if y.eng == e and not (e != "pe" and op.idx - y.idx <= self.near):
                            continue
                        if y.eng == e and y is op:
                            continue
                        k = ("e", y.eng)
                        v = y.cnt
                        sem = esem[y.eng]
                    else:
                        k = ("d", d[1].name)
                        v = d[2]
                        sem = d[1].dma_sem
                    if seen.get(k, 0) >= v:
                        continue
                    if need.get(k, (None, 0))[1] < v:
                        need[k] = (sem, v)
                for k, (sem, v) in need.items():
                    eng.wait_ge(sem, v)
                    seen[k] = v
                    nwaits[e] += 1
                if op.meth == "nop":
                    continue
                args = [a.ap if isinstance(a, View) else a for a in op.args]
                kwargs = {k: (a.ap if isinstance(a, View) else ([x.ap if isinstance(x, View) else x for x in a] if isinstance(a, list) and a and isinstance(a[0], View) else a)) for k, a in op.kwargs.items()}
                ins = getattr(eng, op.meth)(*args, **kwargs)
                if op.is_dma:
                    if op.meth == "collective_compute":
                        ins.then_inc(op.dma_tile.dma_sem)
                    else:
                        ins.then_inc(op.dma_tile.dma_sem, 16)
                elif op.signal:
                    ins.then_inc(esem[e], 1)

        if self.ops["pe"]:
            @block.tensor
            def _(eng):
                run("pe", eng)
        if self.ops["act"]:
            @block.scalar
            def _(eng):
                run("act", eng)
        if self.ops["dve"]:
            @block.vector
            def _(eng):
                run("dve", eng)
        if self.ops["pool"]:
            @block.gpsimd
            def _(eng):
                run("pool", eng)
        if self.ops["sp"]:
            @block.sync
            def _(eng):
                run("sp", eng)
        self.nwaits = nwaits

D = 1024
GT = 512
NS = 5
SQ044 = 0.044715 ** 0.5
GELU_C = 1.5957691216057308
NEG = -30000.0


def build(TOK, L, debug=False, pair=None):
    nc = bass.Bass("TRN2", target_bir_lowering=False)
    P = Prog(nc)
    NG = TOK // GT
    EI = "ExternalInput"
    x_d = P.dram("x", [TOK, D], F32, EI)
    out_d = P.dram("out", [TOK, D], F32, "ExternalOutput")
    w_in_d = P.dram("w_in", [L, D, 8960], F32, EI)
    w_br_d = P.dram("w_br", [L, 4, 512, D], F32, EI)
    w_out_d = P.dram("w_out", [L, D, D], F32, EI)
    w_fi_d = P.dram("w_fi", [L, D, 5632], F32, EI)
    w_fo_d = P.dram("w_fo", [L, 2816, D], F32, EI)
    n1g_d = P.dram("n1g", [L, 128, 8], F32, EI)
    n2g_d = P.dram("n2g", [L, 128, 8], F32, EI)
    gng_d = P.dram("gng", [L, 512], F32, EI)
    cw_d = P.dram("cw", [L, 128, 4 * 31], F32, EI)
    cb_d = P.dram("cb", [L, 128, 4], F32, EI)
    clg_d = P.dram("clg", [L, 128, 4], F32, EI)
    clb_d = P.dram("clb", [L, 128, 4], F32, EI)
    glg_d = P.dram("glg", [L, 512], F32, EI)
    glb_d = P.dram("glb", [L, 512], F32, EI)
    gws_d = P.dram("gws", [L, 128, 512], F32, EI)
    gbs_d = P.dram("gbs", [L, 512], F32, EI)
    sqg_d = P.dram("sqg", [L, 64], F32, EI)
    skg_d = P.dram("skg", [L, 64], F32, EI)
    sink_d = P.dram("sink", [L, 8], F32, EI)
    bias_d = P.dram("biasg", [128, 2048], F32, EI)
    ident_d = P.dram("ident", [128, 128], F32, EI)
    cs_d = P.dram("cs", [TOK, 256], F32, EI)
    dec_d = P.dram("dec", [128, 512], F32, EI)
    xz_d = P.dram("xz", [128, 8], F32, EI)
    tril_d = P.dram("tril", [128, 512], F32, EI)
    mneg_d = P.dram("mneg", [128, 256], F32, EI)
    MSG = 892
    if pair:
        flag_d = P.dram("flag", [128, 2], F32, EI)
        msg_in_d = P.dram("msg_in", [128, MSG], F32, "Internal")
        msg_out_d = P.dram("msg_out", [256, MSG], F32, "Internal")
    if L > 1:
        x1_d = P.dram("x1", [TOK, D], F32, "Internal")
    dg_d = P.dram("dgm", [L * 4, 128, 31 * 128], BF16, "Internal")
    if debug:
        dbg_d = P.dram("dbg", [4, 128, 2048], F32, "ExternalOutput")
        dbg2_d = P.dram("dbg2", [8, 128, 512], F32, "ExternalOutput")
        def dump(k, view, n=512):
            P.dma("sp", dbg2_d[k, :, 0:n], view)

    st = ExitStack()
    ident = P.sb("ident", [128, 128], BF16)
    dec = P.sb("dec", [128, 512], F32)
    xz = P.sb("xz", [128, 8], F32)
    onesf = P.sb("onesf", [128, 128], F32)
    eps = P.sb("eps", [128, 1], F32)
    biasT = P.sb("biasT", [128, 2048], F32)
    n1g = [P.sb(f"n1g{l}", [128, 8], F32) for l in range(L)]
    n2g = [P.sb(f"n2g{l}", [128, 8], F32) for l in range(L)]
    cw = [P.sb(f"cw{l}", [128, 124], F32) for l in range(L)]
    cb = [P.sb(f"cb{l}", [128, 4], F32) for l in range(L)]
    clg = [P.sb(f"clg{l}", [128, 4], F32) for l in range(L)]
    clb = [P.sb(f"clb{l}", [128, 4], F32) for l in range(L)]
    sqg = [P.sb(f"sqg{l}", [128, 64], F32) for l in range(L)]
    skg = [P.sb(f"skg{l}", [128, 64], F32) for l in range(L)]
    esink = [P.sb(f"esink{l}", [128, 8], F32) for l in range(L)]
    wsT = [P.sb(f"wsT{l}", [128, 512], BF16) for l in range(L)]
    gng = [P.sb("gng", [128, 512], F32)] * L
    glg = [P.sb("glg", [128, 512], F32)] * L
    glb = [P.sb("glb", [128, 512], F32)] * L
    bsb = [P.sb("bsb", [128, 512], F32)] * L
    S32 = [P.sb("S32", [128, 512], F32)] * L
    Sbf = [P.sb("Sbf", [128, 512], BF16)] * L
    Uh = [P.sb("Uh", [128, 4, 30], F32)] * L
    kTr = [P.sb("kTr", [128, 2, 128], BF16)] * L
    vaug = [P.sb("vaug", [128, 2, 2, 66], BF16)] * L
    if pair:
        qT_d = P.dram("qT_s", [NG, 128, 4, GT], BF16, "Internal")
        sg_d = P.dram("sg_s", [TOK, 512], F32, "Internal")
        kT_d = P.dram("kT_s", [NG, 128, 4, GT], BF16, "Internal")
        kz_d = P.dram("kz_s", [TOK, 512], BF16, "Internal")
        vr_d = P.dram("vr_s", [TOK, 512], BF16, "Internal")
        hT_d = P.dram("hT_s", [NG, 128, 8, GT], BF16, "Internal")
        R_kr = [P.sb(f"R_kr{i}", [128, 512], BF16) for i in range(2)]

        def mk_tok(name):
            t_ = Tile(name, None, "tok")
            P.tiles.append(t_)
            return t_
        tokC = [mk_tok(f"tokC{i}") for i in range(4)]
        tokKs = [mk_tok(f"tokKs{i}") for i in range(4)]
        tokVs = [mk_tok(f"tokVs{i}") for i in range(4)]
        tokKl = [mk_tok(f"tokKl{i}") for i in range(4)]
        tokVl = [mk_tok(f"tokVl{i}") for i in range(4)]
        tokRl = [mk_tok(f"tokRl{i}") for i in range(4)]
        tokHs = [mk_tok(f"tokHs{i}") for i in range(4)]
        tokQs = [mk_tok(f"tokQs{i}") for i in range(4)]
        tokGs = [mk_tok(f"tokGs{i}") for i in range(4)]
        tokGl = [mk_tok(f"tokGl{i}") for i in range(4)]
        tokKTs = [mk_tok(f"tokKTs{i}") for i in range(4)]
        fl = P.sb("fl", [128, 2], F32)
        msg = P.sb("msg", [128, MSG], F32)
        recv = P.sb("recv", [128, MSG], F32)
    xt = [P.sb(f"xt{i}", [128, D], F32) for i in range(4)]
    hT = P.sb("hT", [128, 8, GT], BF16)
    htok = [P.sb(f"htok{i}", [128, D], BF16) for i in range(2)]
    r1 = P.sb_off
    actT = P.sb("actT", [128, 22, GT], BF16)
    r1e = P.sb_off
    P.sb_off = r1
    qT = P.sb("qT", [128, 4, GT], BF16)
    kT = P.sb("kT", [128, 4, GT], BF16)
    kz = P.sb("kz", [128, 4, 512], BF16)
    vret = P.sb("vret", [128, 4, 512], BF16)
    sgg = P.sb("sgg", [128, 4, 512], F32)
    for t in (qT, kT, kz, vret, sgg):
        P.alias(actT, t)
    P.sb_off = max(P.sb_off, r1e)
    yT = [P.sb(f"yT{b}", [128, 4, GT], BF16) for b in range(4)]
    r2 = P.sb_off
    mix = P.sb("mix", [128, 8, GT], F32)
    r2e = P.sb_off
    P.sb_off = r2
    U = P.sb("U", [128, 4, 30 + GT], BF16)
    acc = P.sb("acc", [128, 4, GT], F32)
    C_mean = P.sb("C_mean", [128, 512], F32)
    P.alias(mix, U)
    P.alias(mix, acc)
    P.alias(mix, C_mean)
    P.sb_off = max(P.sb_off, r2e)
    r3 = P.sb_off
    mixT = P.sb("mixT", [128, 8, GT], BF16)
    P.sb_off = r3
    uT = P.sb("uT", [128, 4, GT], F32)
    P.alias(mixT, uT)
    cs = P.sb("cs", [128, 4, 256], F32)
    WS = [P.sb(f"ws{i}", [128, 4096], BF16) for i in range(NS)]
    SC = [P.sb(f"sc{i}", [128, 512], F32) for i in range(6)]
    SB = [P.sb(f"sbb{i}", [128, 512], BF16) for i in range(4)]
    qTs = [P.sb(f"qTs{i}", [128, 512], BF16) for i in range(2)]
    pT = P.sb("pT", [128, 4, 512], BF16)
    st8 = P.sb("st8", [128, 8, 8], F32)
    mv = P.sb("mv", [128, 8, 2], F32)
    sm = [P.sb(f"sm{i}", [128, 8], F32) for i in range(4)]
    R_qr = [P.sb(f"R_qr{i}", [128, 512], BF16) for i in range(2)]
    R_sT = P.sb("R_sT", [128, 512], BF16)
    R_ya = P.sb("R_ya", [128, 512], BF16)
    C_rstd = P.sb("C_rstd", [128, 512], F32)
    G_vln = P.sb("G_vln", [128, 512], BF16)
    S_qn = P.sb("S_qn", [128, 512], BF16)
    S_kn = P.sb("S_kn", [128, 128], BF16)
    S_yd = P.sb("S_yd", [128, 512], BF16)
    tril = SC[5]
    PSB = [P.ps(f"ps{i}", [128, 512], F32) for i in range(8)]
    PS_CONV = PSB[7]
    print("SBUF used per partition:", P.sb_off - 16512, "of", P.sb_end - 16512)

    cnt = {"ps": 0, "ws": 0, "sc": 0, "sb": 0, "sm": 0}

    def nps():
        cnt["ps"] += 1
        return PSB[cnt["ps"] % 7]

    def nsc():
        cnt["sc"] += 1
        return SC[cnt["sc"] % 6]

    def nsb():
        cnt["sb"] += 1
        return SB[cnt["sb"] % 4]

    def nsm():
        cnt["sm"] += 1
        return sm[cnt["sm"] % 4]

    wfree = list(range(NS))

    def wload(src, shape3):
        assert wfree, "no free weight slot"
        sid = wfree.pop(0)
        w = WS[sid]
        kc, n = shape3
        dst = w[:, 0:kc * n].rearrange("p (c n) -> p c n", c=kc)
        P.dma("pool", dst, src.rearrange("(c p) n -> p c n", p=128))
        dst.sid = sid
        return dst

    def wload_flat(src, n):
        assert wfree, "no free weight slot"
        sid = wfree.pop(0)
        dst = WS[sid][:, 0:n]
        P.dma("pool", dst, src)
        dst.sid = sid
        return dst

    def wrel(*ws):
        for wv in ws:
            wfree.append(wv.sid)

    V, A = "dve", "act"

    def bc(dram_tile, l, n):
        return View(dram_tile, dram_tile.ap[l].partition_broadcast(128))

    P.dma("pool", ident.v, ident_d.v)
    P.dma("sp", dec.v, dec_d.v)
    P.dma("sp", xz.v, xz_d.v)
    if pair:
        P.dma("sp", fl.v, flag_d.v)
    P.dma("sp", tril.v, tril_d.v)
    P.dma("sp", biasT.v, bias_d.v)
    P.dma("sp", SC[0][:, 0:256], mneg_d.v)
    P.add(V, "memset", onesf.v, 1.0)
    P.add(V, "memset", eps.v, 1e-6)
    bt = biasT.v.rearrange("p (a j c q) -> p a j c q", a=2, j=2, c=4)
    mn = SC[0][:, 0:256].rearrange("p (j q) -> p j q", j=2)
    for a in range(2):
        for j in range(2):
            P.add(V, "tensor_tensor", bt[:, a, j], bt[:, a, j],
                  mn[:, j].unsqueeze(1).to_broadcast([128, 4, 128]), ALU.add)
    for l in range(L):
        P.dma("sp", n1g[l].v, n1g_d[l])
        P.dma("sp", n2g[l].v, n2g_d[l])
        P.dma("sp", cw[l].v, cw_d[l])
        P.dma("sp", cb[l].v, cb_d[l])
        P.dma("sp", clg[l].v, clg_d[l])
        P.dma("sp", clb[l].v, clb_d[l])
        P.dma("sp", sqg[l].v, bc(sqg_d, l, 64))
        P.dma("sp", skg[l].v, bc(skg_d, l, 64))
        P.dma("sp", esink[l].v, bc(sink_d, l, 8))
        s0 = nsc()
        P.dma("sp", s0.v, gws_d[l])
        P.add(V, "tensor_tensor", wsT[l].v, s0.v, tril.v, ALU.mult)
        P.add(A, "activation", esink[l].v, esink[l].v, AF.Exp)
        if l == 0:
            P.add(V, "memset", vaug[l].v, 1.0)
        for cc in range(4):
            sid = wfree.pop(0)
            stg = WS[sid]
            for j in range(31):
                if j % 3 == 0:
                    P.add(A, "activation", stg[:, j * 128:(j + 1) * 128], ident.v, AF.Copy,
                          scale=cw[l][:, cc * 31 + j:cc * 31 + j + 1])
                else:
                    P.add(V, "tensor_scalar", stg[:, j * 128:(j + 1) * 128], ident.v,
                          cw[l][:, cc * 31 + j:cc * 31 + j + 1], None, ALU.mult)
            P.dma("sp", dg_d[l * 4 + cc], stg[:, 0:31 * 128], sync_tile=stg)
            wfree.append(sid)

    def rstd_from(ss_view, n, scale, width):
        r = nsm()
        P.add(A, "activation", r[:, 0:width], ss_view, AF.Sqrt, bias=eps.v, scale=scale)
        P.add(V, "reciprocal", r[:, 0:width], r[:, 0:width])
        return r[:, 0:width]

    def rms_a(i):
        ss = nsm()
        hk = htok[i % 2]
        P.add(A, "activation", hk.v, xt[i].v, AF.Square, accum_out=ss[:, 0:1])
        r = rstd_from(ss[:, 0:1], 1, 1.0 / D, 1)
        P.add(A, "activation", hk.v, xt[i].v, AF.Copy, scale=r)

    def rms_b(i, gt):
        hk = htok[i % 2]
        ps = nps()
        pv = ps.v.bitcast(BF16)
        for c in range(8):
            P.add("pe", "transpose", pv[:, c * 128:(c + 1) * 128], hk[:, c * 128:(c + 1) * 128], ident.v)
        P.add(V, "tensor_tensor", hT[:, :, i * 128:(i + 1) * 128],
              pv.rearrange("p (c t) -> p c t", c=8),
              gt.v.unsqueeze(2).to_broadcast([128, 8, 128]), ALU.mult)

    def rmsnorm_to_hT(gt):
        rms_a(0)
        for i in range(4):
            if i + 1 < 4:
                rms_a(i + 1)
            rms_b(i, gt)

    def mm_tok(w, i, ncols=512):
        ps = nps()
        for c in range(8):
            P.add("pe", "matmul", ps[:, 0:ncols], hT[:, c, i * 128:(i + 1) * 128], w[:, c, 0:ncols],
                  start=(c == 0), stop=(c == 7))
        return ps

    def mm_feat(w, col0, src=None, nk=8):
        ps = nps()
        src = hT if src is None else src
        for c in range(nk):
            P.add("pe", "matmul", ps.v, w[:, c, col0:col0 + 128], src[:, c, :],
                  start=(c == 0), stop=(c == nk - 1))
        return ps

    def transpose4(src_tok, dstT, i):
        ps = nps()
        pv = ps.v.bitcast(BF16)
        for c in range(4):
            P.add("pe", "transpose", pv[:, c * 128:(c + 1) * 128], src_tok[:, c * 128:(c + 1) * 128], ident.v)
        P.add(A, "activation", dstT[:, :, i * 128:(i + 1) * 128],
              pv[:, 0:512].rearrange("p (c t) -> p c t", c=4), AF.Copy)

    def gelu_from_ps(ps, out_view, ncols=512):
        P.add(A, "activation", out_view, ps[:, 0:ncols], AF.Gelu_apprx_tanh)

    def w_in_blk(l, col0, ncols=512):
        return wload(View(w_in_d, w_in_d.ap[l, :, col0:col0 + ncols]), (8, ncols))

    def load_group(src_d, g):
        for i in range(4):
            P.dma("sp", xt[i].v, src_d[(g * 4 + i) * 128:(g * 4 + i + 1) * 128, :])
        P.dma("sp", cs.v, View(cs_d, cs_d.ap[g * GT:(g + 1) * GT, :].rearrange("(i p) n -> p i n", p=128)))

    def rotary(ps, i, dst=None):
        q4 = ps.v.rearrange("p (h d) -> p h d", h=4)
        t1, t2 = nsc(), nsc()
        cos2 = cs[:, i, 0:128].unsqueeze(1).to_broadcast([128, 4, 128])
        P.add(V, "tensor_tensor", t1.v.rearrange("p (h d) -> p h d", h=4), q4, cos2, ALU.mult)
        t24 = t2.v.rearrange("p (h d) -> p h d", h=4)
        P.add(V, "tensor_tensor", t24[:, :, 0:64], q4[:, :, 64:128],
              cs[:, i, 128:192].unsqueeze(1).to_broadcast([128, 4, 64]), ALU.mult)
        P.add(V, "tensor_tensor", t24[:, :, 64:128], q4[:, :, 0:64],
              cs[:, i, 192:256].unsqueeze(1).to_broadcast([128, 4, 64]), ALU.mult)
        qr = nsb() if dst is None else dst
        P.add(V, "tensor_tensor", qr.v, t1.v, t2.v, ALU.add)
        return qr

    def swa_kv_a(l, wkv, i, slot):
        pk = mm_tok(wkv, i, 256)
        s2 = nsc()
        P.add(A, "activation", s2[:, 0:128], pk[:, 0:128], AF.Square)
        ss2 = nsm()
        P.add(V, "tensor_reduce", ss2[:, 0:2], s2[:, 0:128].rearrange("p (h d) -> p h d", h=2), AX.X, ALU.add)
        r2 = rstd_from(ss2[:, 0:2], 2, 1.0 / 64, 2)
        P.add(V, "tensor_tensor", s2[:, 0:128].rearrange("p (h d) -> p h d", h=2),
              pk[:, 0:128].rearrange("p (h d) -> p h d", h=2),
              r2.unsqueeze(2).to_broadcast([128, 2, 64]), ALU.mult)
        P.add(V, "tensor_tensor", S_kn.v.rearrange("p (h d) -> p h d", h=2),
              s2[:, 0:128].rearrange("p (h d) -> p h d", h=2),
              skg[l].v.unsqueeze(1).to_broadcast([128, 2, 64]), ALU.mult)
        P.add(A, "activation", vaug[l][:, slot, :, 0:64],
              pk[:, 128:256].rearrange("p (h d) -> p h d", h=2), AF.Copy)

    def swa_kv_b(l, slot):
        pkt = nps()
        pktv = pkt.v.bitcast(BF16)
        P.add("pe", "transpose", pktv[:, 0:128], S_kn.v, ident.v)
        P.add(A, "activation", kTr[l][:, slot, :], pktv[:, 0:128], AF.Copy)

    def state_update(l, ps_k):
        for h in range(4):
            hs = slice(h * 128, (h + 1) * 128)
            gam = 1.0 - 2.0 ** (-5.0 - h)
            P.add(V, "scalar_tensor_tensor", S32[l][:, hs], S32[l][:, hs], float(gam ** 128),
                  ps_k[:, hs], ALU.mult, ALU.add)

    def prepass(l, src_d):
        P.add(V, "memset", S32[l].v, 0.0)
        wq_ = w_in_blk(l, 0)
        wk_ = w_in_blk(l, 512)
        wv_ = w_in_blk(l, 1024)
        wg_ = w_in_blk(l, 1536)
        P.dma("sp", gng[l].v, bc(gng_d, l, 512))

        def pp_proj(i, t):
            ps = mm_tok(wq_, i)
            rotary(ps, i, R_qr[t % 2])
            ps = mm_tok(wk_, i)
            kr = rotary(ps, i, R_kr[t % 2])
            P.add(V, "tensor_tensor", kz[:, i, :].rearrange("p (h d) -> p h d", h=4),
                  kr.v.rearrange("p (h d) -> p h d", h=4),
                  xz[:, 4:8].unsqueeze(2).to_broadcast([128, 4, 128]), ALU.mult)
            ps = mm_tok(wv_, i)
            P.add(A, "activation", vret[:, i, :], ps.v, AF.Copy)
            rows = slice(t * 128, (t + 1) * 128)
            P.dma("pool", kz_d[rows, :], kz[:, i, :], sync_tile=tokKs[i])
            P.dma("pool", vr_d[rows, :], vret[:, i, :], sync_tile=tokVs[i])
            ps = mm_tok(wg_, i)
            s1 = nsc()
            P.add(A, "activation", s1.v, ps.v, AF.Silu)
            P.add(V, "tensor_tensor", sgg[:, i, :], s1.v, gng[l].v, ALU.mult)
            P.dma("pool", sg_d[rows, :], sgg[:, i, :], sync_tile=tokGs[i])

        def pp_T(i, t):
            isl = slice(i * 128, (i + 1) * 128)
            transpose4(R_qr[t % 2], qT, i)
            transpose4(R_kr[t % 2], kT, i)
            P.dma("pool", qT_d[t // 4, :, :, isl], qT[:, :, isl], sync_tile=tokQs[i])
            P.dma("pool", kT_d[t // 4, :, :, isl], kT[:, :, isl], sync_tile=tokKTs[i])

        def pp_kv(i):
            ps_k = nps()
            for h in range(4):
                hs = slice(h * 128, (h + 1) * 128)
                P.add("pe", "matmul", ps_k[:, hs], kz[:, i, hs], vret[:, i, hs], start=True, stop=True)
            state_update(l, ps_k)

        NTT = NG * 4
        for step in range(NTT + 3):
            if step < NTT:
                P.dma("sp", xt[step % 4].v, src_d[step * 128:(step + 1) * 128, :])
                P.dma("sp", cs[:, step % 4, :], cs_d[step * 128:(step + 1) * 128, :], sync_tile=tokC[step % 4])
                rms_a(step % 4)
            if 0 <= step - 1 < NTT:
                t_ = step - 1
                rms_b(t_ % 4, n1g[l])
                isl = slice((t_ % 4) * 128, (t_ % 4 + 1) * 128)
                P.dma("pool", hT_d[t_ // 4, :, :, isl], hT[:, :, isl], sync_tile=tokHs[t_ % 4])
            if 0 <= step - 2 < NTT:
                pp_proj((step - 2) % 4, step - 2)
            if 0 <= step - 3 < NTT:
                pp_T((step - 3) % 4, step - 3)
                pp_kv((step - 3) % 4)
        wrel(wq_, wk_, wv_, wg_)
        load_group(src_d, 0)
        for g in [NG - 1]:
            if g == NG - 1:
                wa = w_in_blk(l, 2048)
                wg = w_in_blk(l, 2560)
                for cc in range(4):
                    pa, pg = nps(), nps()
                    for c in range(8):
                        P.add("pe", "matmul", pa[:, 0:128], wa[:, c, cc * 128:(cc + 1) * 128], hT[:, c, 384:512],
                              start=(c == 0), stop=(c == 7))
                    for c in range(8):
                        P.add("pe", "matmul", pg[:, 0:128], wg[:, c, cc * 128:(cc + 1) * 128], hT[:, c, 384:512],
                              start=(c == 0), stop=(c == 7))
                    s1 = nsc()
                    P.add(A, "activation", s1[:, 0:128], pg[:, 0:128], AF.Sigmoid)
                    P.add(V, "tensor_tensor", U[:, cc, 30 + 384:30 + 512], pa[:, 0:128], s1[:, 0:128], ALU.mult)
                P.add(V, "tensor_copy", Uh[l].v, U[:, :, GT:GT + 30])
                wkv = w_in_blk(l, 4608, 256)
                swa_kv_a(l, wkv, 3, 0)
                swa_kv_b(l, 0)
                wrel(wa, wg, wkv)
        P.dma("sp", hT.v, hT_d[0])
        P.add(V, "tensor_copy", msg[:, 0:512], S32[l].v)
        P.add(V, "tensor_copy", msg[:, 512:632].rearrange("p (c t) -> p c t", c=4), Uh[l].v)
        P.add(V, "tensor_copy", msg[:, 632:760], kTr[l][:, 0, :])
        P.add(V, "tensor_copy", msg[:, 760:892].rearrange("p (a e) -> p a e", a=2), vaug[l][:, 0])
        P.dma("sp", msg_in_d.v, msg.v, sync_tile=msg_in_d)
        P.collective("AllGather", pair, msg_in_d, msg_out_d)
        P.dma("sp", recv.v, msg_out_d[0:128, :])
        P.add(V, "tensor_scalar", S32[l].v, recv[:, 0:512], fl[:, 0:1], None, ALU.mult)
        P.add(A, "activation", Sbf[l].v, S32[l].v, AF.Copy)
        P.add(V, "tensor_scalar", Uh[l].v, recv[:, 512:632].rearrange("p (c t) -> p c t", c=4), fl[:, 0:1], None, ALU.mult)
        P.add(V, "tensor_scalar", kTr[l][:, 1, :], recv[:, 632:760], fl[:, 0:1], None, ALU.mult)
        P.add(V, "tensor_scalar", vaug[l][:, 1], recv[:, 760:892].rearrange("p (a e) -> p a e", a=2), fl[:, 0:1], None, ALU.mult)
        P.add(V, "memset", vaug[l][:, 1, :, 64:66], 1.0)

    for l in range(L):
        src_d = x_d if l == 0 else x1_d
        dst_d = out_d if l == L - 1 else x1_d
        if pair:
            prepass(l, src_d)
        else:
            P.add(V, "memset", S32[l].v, 0.0)
            P.add(V, "memset", Sbf[l].v, 0.0)
            P.add(V, "memset", Uh[l].v, 0.0)
        for g in range(NG):
            if g == 0 and not pair:
                load_group(src_d, g)
            if pair:
                P.dma("sp", qT.v, qT_d[g])
                P.dma("sp", kT.v, kT_d[g])
                for i in range(4):
                    rows = slice((g * 4 + i) * 128, (g * 4 + i + 1) * 128)
                    P.dma("sp", kz[:, i, :], kz_d[rows, :], sync_tile=tokKl[i])
                    P.dma("sp", sgg[:, i, :], sg_d[rows, :], sync_tile=tokGl[i])
                    P.dma("sp", vret[:, i, :], vr_d[rows, :], sync_tile=tokVl[i])
            P.dma("sp", gng[l].v, bc(gng_d, l, 512))
            P.dma("sp", glg[l].v, bc(glg_d, l, 512))
            P.dma("sp", glb[l].v, bc(glb_d, l, 512))
            P.dma("sp", bsb[l].v, bc(gbs_d, l, 512))
            if not pair:
                rmsnorm_to_hT(n1g[l])
            def need_slots(n):
                while len(wfree) < n:
                    yield "WAIT"

            def chain_R():
                for (col0, dstT, is_k) in ((0, qT, False), (512, kT, True)):
                    if pair:
                        continue
                    yield from need_slots(1)
                    w = w_in_blk(l, col0)
                    for i in range(4):
                        ps = mm_tok(w, i)
                        qr0 = rotary(ps, i, R_qr[i % 2])
                        if is_k:
                            P.add(V, "tensor_tensor", kz[:, i, :].rearrange("p (h d) -> p h d", h=4),
                                  qr0.v.rearrange("p (h d) -> p h d", h=4),
                                  xz[:, 4:8].unsqueeze(2).to_broadcast([128, 4, 128]), ALU.mult)
                        yield
                        transpose4(qr0, dstT, i)
                        yield
                    wrel(w)
                if not pair:
                    yield from need_slots(1)
                    w = w_in_blk(l, 1024)
                    for i in range(4):
                        ps = mm_tok(w, i)
                        P.add(A, "activation", vret[:, i, :], ps.v, AF.Copy)
                        yield
                    wrel(w)
                if not pair:
                    yield from need_slots(1)
                    w = w_in_blk(l, 1536)
                    for i in range(4):
                        ps = mm_tok(w, i)
                        s1 = nsc()
                        P.add(A, "activation", s1.v, ps.v, AF.Silu)
                        P.add(V, "tensor_tensor", sgg[:, i, :], s1.v, gng[l].v, ALU.mult)
                        yield
                    wrel(w)
                for i in range(4):
                    tsl = slice(i * 128, (i + 1) * 128)
                    ps_s = nps()
                    for h in range(4):
                        P.add("pe", "matmul", ps_s[:, h * 128:(h + 1) * 128], kT[:, h, tsl], qT[:, h, tsl],
                              start=True, stop=True)
                    P.add(V, "tensor_tensor", R_sT.v, ps_s.v, dec.v, ALU.mult)
                    yield
                    ps_i, ps_c, ps_k = nps(), nps(), nps()
                    for h in range(4):
                        hs = slice(h * 128, (h + 1) * 128)
                        P.add("pe", "matmul", ps_i[:, hs], R_sT[:, hs], vret[:, i, hs], start=True, stop=True)
                    for h in range(4):
                        hs = slice(h * 128, (h + 1) * 128)
                        P.add("pe", "matmul", ps_c[:, hs], qT[:, h, tsl], Sbf[l][:, hs], start=True, stop=True)
                    for h in range(4):
                        hs = slice(h * 128, (h + 1) * 128)
                        P.add("pe", "matmul", ps_k[:, hs], kz[:, i, hs], vret[:, i, hs], start=True, stop=True)
                    y = nsc()
                    P.add(V, "tensor_tensor", y.v.rearrange("p (h d) -> p h d", h=4),
                          ps_c.v.rearrange("p (h d) -> p h d", h=4),
                          xz[:, 0:4].unsqueeze(2).to_broadcast([128, 4, 128]), ALU.mult)
                    P.add(V, "tensor_tensor", y.v, y.v, ps_i.v, ALU.add)
                    state_update(l, ps_k)
                    P.add(A, "activation", Sbf[l].v, S32[l].v, AF.Copy)
                    for h in range(4):
                        P.add(V, "bn_stats", st8[:, h, 0:6], y[:, h * 128:(h + 1) * 128])
                    for h in range(4):
                        P.add(V, "bn_aggr", mv[:, h, :], st8[:, h, 0:6])
                    r = rstd_from(mv[:, 0:4, 1], 4, 1.0, 4)
                    y4 = y.v.rearrange("p (h d) -> p h d", h=4)
                    P.add(V, "tensor_tensor", y4, y4, mv[:, 0:4, 0:1].to_broadcast([128, 4, 128]), ALU.subtract)
                    P.add(V, "tensor_tensor", y4, y4, r.unsqueeze(2).to_broadcast([128, 4, 128]), ALU.mult)
                    P.add(V, "tensor_tensor", R_ya.v, y.v, sgg[:, i, :], ALU.mult)
                    yield
                    transpose4(R_ya, yT[0], i)
                    yield

            def chain_C():
                yield from need_slots(2)
                wa = w_in_blk(l, 2048)
                wg = w_in_blk(l, 2560)
                P.add(V, "tensor_copy", U[:, :, 0:30], Uh[l].v)
                for cc in range(4):
                    pa = mm_feat(wa, cc * 128)
                    pg = mm_feat(wg, cc * 128)
                    s1 = nsc()
                    P.add(A, "activation", s1.v, pg.v, AF.Sigmoid)
                    P.add(V, "tensor_tensor", U[:, cc, 30:30 + GT], pa.v, s1.v, ALU.mult)
                    yield
                wrel(wa, wg)
                P.add(V, "tensor_copy", Uh[l].v, U[:, :, GT:GT + 30])
                for cc in range(4):
                    yield from need_slots(1)
                    wd = wload_flat(dg_d[l * 4 + cc], 31 * 128)
                    for j in range(31):
                        P.add("pe", "matmul", PS_CONV.v, wd[:, j * 128:(j + 1) * 128], U[:, cc, j:j + GT],
                              start=(j == 0), stop=(j == 30))
                        if j % 8 == 7:
                            yield
                    P.add(A, "activation", acc[:, cc, :], PS_CONV.v, AF.Identity, bias=cb[l][:, cc:cc + 1])
                    wrel(wd)
                    yield
                p1, p2 = nps(), nps()
                for cc in range(4):
                    P.add("pe", "matmul", p1.v, onesf.v, acc[:, cc, :], start=(cc == 0), stop=(cc == 3))
                for cc in range(4):
                    s1 = nsc()
                    P.add(A, "activation", s1.v, acc[:, cc, :], AF.Square)
                    P.add("pe", "matmul", p2.v, onesf.v, s1.v, start=(cc == 0), stop=(cc == 3))
                P.add(A, "activation", C_mean.v, p1.v, AF.Copy, scale=1.0 / 512)
                P.add(V, "tensor_tensor", C_rstd.v, C_mean.v, C_mean.v, ALU.mult)
                P.add(V, "scalar_tensor_tensor", C_rstd.v, p2.v, 1.0 / 512, C_rstd.v, ALU.mult, ALU.subtract)
                P.add(A, "activation", C_rstd.v, C_rstd.v, AF.Sqrt, bias=eps.v, scale=1.0)
                P.add(V, "reciprocal", C_rstd.v, C_rstd.v)
                yield
                for cc in range(4):
                    P.add(V, "tensor_tensor", acc[:, cc, :], acc[:, cc, :], C_mean.v, ALU.subtract)
                    P.add(V, "tensor_tensor", acc[:, cc, :], acc[:, cc, :], C_rstd.v, ALU.mult)
                    P.add(A, "activation", yT[1][:, cc, :], acc[:, cc, :], AF.Silu,
                          scale=clg[l][:, cc:cc + 1], bias=clb[l][:, cc:cc + 1])
                    yield

            def chain_G():
                yield from need_slots(1)
                w = w_in_blk(l, 3072)
                for cc in range(4):
                    pu = mm_feat(w, cc * 128)
                    gelu_from_ps(pu, uT[:, cc, :])
                    yield
                wrel(w)
                yield from need_slots(1)
                w = w_in_blk(l, 3584)
                for i in range(4):
                    ps = mm_tok(w, i)
                    vv = nsc()
                    gelu_from_ps(ps, vv.v)
                    P.add(V, "bn_stats", st8[:, 4, 0:6], vv.v)
                    P.add(V, "bn_aggr", mv[:, 4, :], st8[:, 4, 0:6])
                    r = rstd_from(mv[:, 4, 1:2], 1, 1.0, 1)
                    P.add(V, "tensor_scalar", vv.v, vv.v, mv[:, 4, 0:1], r, ALU.subtract, ALU.mult)
                    P.add(V, "tensor_tensor", vv.v, vv.v, glg[l].v, ALU.mult)
                    P.add(V, "tensor_tensor", G_vln.v, vv.v, glb[l].v, ALU.add)
                    yield
                    pz = nps()
                    for gg in range(4):
                        gs = slice(gg * 128, (gg + 1) * 128)
                        P.add("pe", "matmul", pz[:, gs], G_vln[:, gs], wsT[l][:, gs], start=True, stop=True)
                    s1 = nsc()
                    P.add(V, "tensor_tensor", s1.v, pz.v, bsb[l].v, ALU.add)
                    P.add(V, "tensor_tensor", yT[2][:, :, i * 128:(i + 1) * 128],
                          s1.v.rearrange("p (g t) -> p g t", g=4), uT[:, :, i * 128:(i + 1) * 128], ALU.mult)
                    yield
                wrel(w)

            def chain_S():
                yield from need_slots(2)
                wq = w_in_blk(l, 4096)
                wkv = w_in_blk(l, 4608, 256)
                for i in range(4):
                    gi = g * 4 + i
                    cur, prv = gi % 2, (gi + 1) % 2
                    ps = mm_tok(wq, i)
                    s1 = nsc()
                    P.add(A, "activation", s1.v, ps.v, AF.Square)
                    ss = nsm()
                    P.add(V, "tensor_reduce", ss.v, s1.v.rearrange("p (h d) -> p h d", h=8), AX.X, ALU.add)
                    r = rstd_from(ss.v, 8, 1.0 / 64, 8)
                    P.add(V, "tensor_tensor", s1.v.rearrange("p (h d) -> p h d", h=8),
                          ps.v.rearrange("p (h d) -> p h d", h=8),
                          r.unsqueeze(2).to_broadcast([128, 8, 64]), ALU.mult)
                    P.add(V, "tensor_tensor", S_qn.v.rearrange("p (h d) -> p h d", h=8),
                          s1.v.rearrange("p (h d) -> p h d", h=8),
                          sqg[l].v.unsqueeze(1).to_broadcast([128, 8, 64]), ALU.mult)
                    yield
                    qts = qTs[i % 2]
                    pq = nps()
                    pqv = pq.v.bitcast(BF16)
                    for c in range(4):
                        P.add("pe", "transpose", pqv[:, c * 128:(c + 1) * 128], S_qn[:, c * 128:(c + 1) * 128], ident.v)
                    P.add(A, "activation", qts.v, pqv[:, 0:512], AF.Copy)
                    swa_kv_a(l, wkv, i, cur)
                    yield
                    swa_kv_b(l, cur)
                    yield
                    has_prev = gi > 0 or bool(pair)
                    js = ([0] if has_prev else []) + [1]
                    for a in range(2):
                        for j in js:
                            slot = prv if j == 0 else cur
                            pss = nps()
                            P.add("pe", "matmul", pss.v, kTr[l][a * 64:(a + 1) * 64, slot, :],
                                  qts[a * 64:(a + 1) * 64, :], start=True, stop=True)
                            s3 = nsc()
                            P.add(V, "scalar_tensor_tensor", s3.v, pss.v, 0.125,
                                  biasT[:, (a * 2 + j) * 512:(a * 2 + j + 1) * 512], ALU.mult, ALU.add)
                            if pair and gi == 0 and j == 0:
                                P.add(V, "tensor_scalar", s3.v, s3.v, fl[:, 1:2], None, ALU.add)
                            P.add(A, "activation", pT[:, a * 2 + j, :], s3.v, AF.Exp)
                    yield
                    po = [nps(), nps()]
                    for a in range(2):
                        for c in range(4):
                            for jn, j in enumerate(js):
                                slot = prv if j == 0 else cur
                                P.add("pe", "matmul", po[a][:, c * 65:(c + 1) * 65],
                                      pT[:, a * 2 + j, c * 128:(c + 1) * 128], vaug[l][:, slot, a, 0:65],
                                      start=(jn == 0), stop=(jn == len(js) - 1))
                    den = nsm()
                    for a in range(2):
                        P.add(V, "tensor_tensor", den[:, a * 4:(a + 1) * 4],
                              po[a][:, 0:260].rearrange("p (c e) -> p c e", c=4)[:, :, 64],
                              esink[l][:, a * 4:(a + 1) * 4], ALU.add)
                    P.add(V, "reciprocal", den.v, den.v)
                    for a in range(2):
                        P.add(V, "tensor_tensor", S_yd[:, a * 256:(a + 1) * 256].rearrange("p (c d) -> p c d", c=4),
                              po[a][:, 0:260].rearrange("p (c e) -> p c e", c=4)[:, :, 0:64],
                              den[:, a * 4:(a + 1) * 4].unsqueeze(2).to_broadcast([128, 4, 64]), ALU.mult)
                    yield
                    transpose4(S_yd, yT[3], i)
                    yield
                wrel(wq, wkv)

            chains = [chain_R, chain_C, chain_G, chain_S]
            saved = (dict(cnt), list(wfree))
            P.dry = True
            costs = []
            for cf in chains:
                wfree[:] = list(range(NS))
                steps = []
                gen = cf()
                while True:
                    P.dry_cost = {"pe": 0.0, "dve": 0.0, "act": 0.0, "pool": 0.0, "sp": 0.0}
                    try:
                        next(gen)
                    except StopIteration:
                        steps.append(dict(P.dry_cost))
                        break
                    steps.append(dict(P.dry_cost))
                costs.append(steps)
            P.dry = False
            cnt.clear()
            cnt.update(saved[0])
            wfree[:] = saved[1]
            clk = {"pe": 0.0, "dve": 0.0, "act": 0.0}
            ready = [0.0] * len(chains)
            pos = [0] * len(chains)
            rem = [sum(c["pe"] + c["dve"] + c["act"] for c in st_) for st_ in costs]
            order = []
            while any(pos[c] < len(costs[c]) for c in range(len(chains))):
                cand = [c for c in range(len(chains)) if pos[c] < len(costs[c])]
                rdy = [c for c in cand if ready[c] <= clk["pe"] + 0.3]
                if rdy:
                    c = max(rdy, key=lambda c: rem[c])
                else:
                    c = min(cand, key=lambda c: ready[c])
                st_ = costs[c][pos[c]]
                tp = max(clk["pe"], ready[c] if st_["pe"] > 0 else 0.0) + st_["pe"]
                if st_["pe"] > 0:
                    clk["pe"] = tp
                fin = tp
                for e in ("dve", "act"):
                    if st_[e] > 0:
                        clk[e] = max(clk[e], tp) + st_[e]
                        fin = max(fin, clk[e])
                ready[c] = fin
                rem[c] -= st_["pe"] + st_["dve"] + st_["act"]
                pos[c] += 1
                order.append(c)
            gens = [cf() for cf in chains]
            alive = [True] * len(chains)
            pend = list(order)
            while pend:
                advanced = False
                for k, c in enumerate(pend):
                    if not alive[c]:
                        pend.pop(k)
                        advanced = True
                        break
                    try:
                        r_ = next(gens[c])
                    except StopIteration:
                        alive[c] = False
                        pend.pop(k)
                        advanced = True
                        break
                    if r_ == "WAIT":
                        continue
                    pend.pop(k)
                    advanced = True
                    break
                assert advanced, "scheduler deadlock on weight slots"
            for c in range(len(chains)):
                while alive[c]:
                    try:
                        next(gens[c])
                    except StopIteration:
                        alive[c] = False
            for b in range(4):
                wb = wload(View(w_br_d, w_br_d.ap[l, b]), (4, 1024))
                for hh in range(2):
                    wgh = w_in_blk(l, 4864 + b * 1024 + hh * 512)
                    for jj in range(4):
                        j = hh * 4 + jj
                        pg = mm_feat(wgh, jj * 128)
                        pb = mm_feat(wb, j * 128, src=yT[b], nk=4)
                        s1 = nsc()
                        P.add(A, "activation", s1.v, pg.v, AF.Sigmoid)
                        if b == 0:
                            P.add(V, "tensor_tensor", mix[:, j, :], s1.v, pb.v, ALU.mult)
                        else:
                            P.add(V, "tensor_tensor", s1.v, s1.v, pb.v, ALU.mult)
                            if b < 3:
                                P.add(V, "tensor_tensor", mix[:, j, :], mix[:, j, :], s1.v, ALU.add)
                            else:
                                P.add(V, "tensor_tensor", mixT[:, j, :], mix[:, j, :], s1.v, ALU.add)
                    wrel(wgh)
                wrel(wb)
            wo = [wload(View(w_out_d, w_out_d.ap[l, :, nb * 512:(nb + 1) * 512]), (8, 512)) for nb in range(2)]
            def wout_tile(i):
                for nb in range(2):
                    ps = nps()
                    for c in range(8):
                        P.add("pe", "matmul", ps.v, mixT[:, c, i * 128:(i + 1) * 128], wo[nb][:, c, :],
                              start=(c == 0), stop=(c == 7))
                    P.add(V, "tensor_tensor", xt[i][:, nb * 512:(nb + 1) * 512],
                          xt[i][:, nb * 512:(nb + 1) * 512], ps.v, ALU.add)

            for step in range(6):
                if step < 4:
                    wout_tile(step)
                if 0 <= step - 1 < 4:
                    rms_a(step - 1)
                if 0 <= step - 2 < 4:
                    rms_b(step - 2, n2g[l])
            wrel(*wo)
            for blk in range(11):
                wf = wload(View(w_fi_d, w_fi_d.ap[l, :, blk * 512:(blk + 1) * 512]), (8, 512))
                for q in range(2):
                    pgt = mm_feat(wf, q * 128)
                    pup = mm_feat(wf, 256 + q * 128)
                    s1 = nsc()
                    P.add(A, "activation", s1.v, pgt.v, AF.Silu)
                    P.add(V, "tensor_tensor", actT[:, blk * 2 + q, :], s1.v, pup.v, ALU.mult)
                wrel(wf)
            if pair and g + 1 < NG:
                P.dma("sp", hT.v, hT_d[g + 1])
            passes = ((0, ((0, 4), (4, 4))), (8, ((0, 4), (4, 4))), (16, ((0, 4), (4, 2))))
            for pi, (k0, blks) in enumerate(passes):
                wfs = []
                for (ks, kn_) in blks:
                    wfs.append((ks, kn_, wload(View(w_fo_d, w_fo_d.ap[l, (k0 + ks) * 128:(k0 + ks + kn_) * 128, :]),
                                               (kn_, 1024))))
                nmm = sum(kn_ for (_, kn_) in blks)
                for i in range(4):
                    for nb in range(2):
                        ps = nps()
                        n = 0
                        for (ks, kn_, wv) in wfs:
                            for c in range(kn_):
                                P.add("pe", "matmul", ps.v, actT[:, k0 + ks + c, i * 128:(i + 1) * 128],
                                      wv[:, c, nb * 512:(nb + 1) * 512], start=(n == 0), stop=(n == nmm - 1))
                                n += 1
                        P.add(V, "tensor_tensor", xt[i][:, nb * 512:(nb + 1) * 512],
                              xt[i][:, nb * 512:(nb + 1) * 512], ps.v, ALU.add)
                    if pi == 2:
                        P.dma("sp", dst_d[(g * 4 + i) * 128:(g * 4 + i + 1) * 128, :], xt[i].v)
                        if g + 1 < NG:
                            P.dma("sp", xt[i].v, src_d[((g + 1) * 4 + i) * 128:((g + 1) * 4 + i + 1) * 128, :])
                wrel(*[wv for (_, _, wv) in wfs])
            if g + 1 < NG:
                P.dma("sp", cs.v, View(cs_d, cs_d.ap[(g + 1) * GT:(g + 2) * GT, :].rearrange("(i p) n -> p i n", p=128)))
    P.finish("sp")
    P.emit(st)
    st.close()
    return nc, P


def _t5_bucket(dist):
    d = np.maximum(dist, 1).astype(np.float32)
    large = 16 + (np.log(d / np.float32(16)) / np.float32(np.log(128 / 16)) * np.float32(16)).astype(np.int32)
    large = np.minimum(large, 31)
    return np.where(dist < 16, dist, large)


def host_consts(TOK, pos0=0):
    c = {}
    c["ident"] = np.eye(128, dtype=np.float32)
    half = 64
    inv = (np.float32(10000.0) ** (-np.arange(half, dtype=np.float32) / np.float32(half))).astype(np.float32)
    pos = (pos0 + np.arange(TOK)).astype(np.float32)
    ang = (pos[:, None] * inv[None, :]).astype(np.float32).astype(np.float64)
    co, si = np.cos(ang).astype(np.float32), np.sin(ang).astype(np.float32)
    c["cs"] = np.ascontiguousarray(np.concatenate([co, co, -si, si], axis=1))
    gam = (1.0 - 2.0 ** (-5.0 - np.arange(4))).astype(np.float64)
    lg = np.log(gam)
    s = np.arange(128)[:, None]
    t = np.arange(128)[None, :]
    scale = 128.0 ** -0.5
    dec = np.zeros((128, 4, 128), np.float64)
    for h in range(4):
        dec[:, h, :] = np.where(t >= s, np.exp(lg[h] * np.maximum(t - s, 0)), 0.0) * scale
    c["dec"] = dec.reshape(128, 512).astype(np.float32)
    idx = np.arange(128)[:, None].astype(np.float64)
    xi = np.exp(lg[None, :] * (idx + 1.0))
    zeta = np.exp(lg[None, :] * (127.0 - idx)) * scale
    c["xz"] = np.concatenate([xi, zeta], axis=1).astype(np.float32)
    c["tril"] = np.ascontiguousarray(np.tile((t >= s).astype(np.float32), (1, 4)))
    q = np.arange(128)[None, None, :]
    j = np.arange(2)[None, :, None]
    ss = np.arange(128)[:, None, None]
    dist = q + 128 - (j * 128 + ss)
    c["mneg"] = np.where((dist >= 0) & (dist < 128), 0.0, NEG).astype(np.float32).reshape(128, 256)
    c["_bucket"] = _t5_bucket(np.clip(dist, 0, 127))
    return c


def host_params(inp, L):
    p = {}
    f = lambda a: np.ascontiguousarray(a, dtype=np.float32)
    w_in = np.array(inp["w_in"][:L], dtype=np.float32, copy=True)
    qb = w_in[:, :, 4096:4608].reshape(L, D, 2, 4, 64)
    w_in[:, :, 4096:4608] = qb.transpose(0, 1, 3, 2, 4).reshape(L, D, 512)
    p["w_in"] = w_in
    p["w_br"] = f(inp["w_branch"][:L])
    p["w_out"] = f(inp["w_out"][:L])
    wf = np.asarray(inp["w_ffn_in"][:L], dtype=np.float32)
    gt = wf[:, :, :2816].reshape(L, D, 11, 2, 128)
    up = wf[:, :, 2816:].reshape(L, D, 11, 2, 128)
    p["w_fi"] = np.ascontiguousarray(np.concatenate([gt, up], axis=3).reshape(L, D, 5632))
    p["w_fo"] = f(inp["w_ffn_out"][:L])
    pm = lambda v, n: f(np.asarray(v[:L]).reshape(L, n, 128).transpose(0, 2, 1))
    p["n1g"] = pm(inp["norm1_g"], 8)
    p["n2g"] = pm(inp["norm2_g"], 8)
    p["gng"] = f(inp["ret_gn_g"][:L])
    cwv = np.asarray(inp["conv_w"][:L])[:, :, 0, :]
    p["cw"] = f(cwv.reshape(L, 31, 4, 128).transpose(0, 3, 2, 1).reshape(L, 128, 124))
    p["cb"] = pm(inp["conv_b"], 4)
    p["clg"] = pm(inp["conv_ln_g"], 4)
    p["clb"] = pm(inp["conv_ln_b"], 4)
    p["glg"] = f(inp["gmlp_ln_g"][:L])
    p["glb"] = f(inp["gmlp_ln_b"][:L])
    p["gws"] = f(np.asarray(inp["gmlp_ws"][:L]).transpose(0, 3, 1, 2).reshape(L, 128, 512))
    p["gbs"] = f(np.asarray(inp["gmlp_bs"][:L]).reshape(L, 512))
    p["sqg"] = f(inp["swa_q_g"][:L])
    p["skg"] = f(inp["swa_k_g"][:L])
    p["sink"] = f(inp["swa_sinks"][:L])
    return p


def host_bias(rel_bias, bucket):
    rb = np.asarray(rel_bias, dtype=np.float32)
    g = rb[bucket]
    g = g.reshape(128, 2, 128, 2, 4).transpose(0, 3, 1, 4, 2)
    return np.ascontiguousarray(g.reshape(128, 2048))


_CACHE = {}


def run(inp, TOK, L, xs, debug=False, pair=None, pos0s=None):
    key = (TOK, L, debug, str(pair))
    if key not in _CACHE:
        _CACHE[key] = build(TOK, L, debug, pair)[0]
    nc = _CACHE[key]
    shared = host_params(inp, L)
    bucket = None
    cmaps = {}
    in_maps = []
    for ci, x in enumerate(xs):
        pos0 = 0 if pos0s is None else pos0s[ci]
        if pos0 not in cmaps:
            c = host_consts(TOK, pos0)
            bucket = c.pop("_bucket")
            cmaps[pos0] = c
        m = dict(shared)
        m.update(cmaps[pos0])
        if "biasg" not in shared:
            shared["biasg"] = host_bias(inp["rel_bias"], bucket)
        m["biasg"] = shared["biasg"]
        m["x"] = np.ascontiguousarray(x, dtype=np.float32)
        if pair:
            half = 1.0 if pos0 > 0 else 0.0
            m["flag"] = np.tile(np.array([[half, (half - 1.0) * 30000.0]], np.float32), (128, 1))
        in_maps.append(m)
    res = run_bass_kernel_spmd(nc, in_maps, core_ids=list(range(len(xs))))
    return res


def kernel(**inputs):
    x = np.asarray(inputs["x"])
    B, S, _ = x.shape
    H = S // 2
    xs = [x[c // 2, (c % 2) * H:(c % 2 + 1) * H] for c in range(2 * B)]
    pos0s = [(c % 2) * H for c in range(2 * B)]
    pair = [[2 * b, 2 * b + 1] for b in range(B)]
    res = run(inputs, H, 2, xs, pair=pair, pos0s=pos0s)
    out = np.empty((B, S, D), np.float32)
    for c in range(2 * B):
        out[c // 2, (c % 2) * H:(c % 2 + 1) * H] = res.results[c]["out"]
    return out
```
